# Optimizing a Trainium2 kernel written in Bass

```python
import jax, jax.numpy as jnp
from jax import lax
import numpy as np

D_MODEL = 1024
BATCH = 4
SEQ = 4096
DEPTH = 2
DEC_BATCH = 16
DEC_SEQ = 16
PAST_LEN = 4096

CHUNK = 64
Q_BLOCK = 128
EPS = 1e-6
A_CHUNK = 128
A_GROUPS = 4
A_GROUP_DIM = 128
A_HALF = A_GROUPS * A_GROUP_DIM
B_HEADS = 8
B_KV_HEADS = 2
B_HEAD_DIM = 64
B_TOPK_MAX = 256
IDX_HEADS = 8
IDX_DIM = 32
C_HEADS = 8
C_HEAD_DIM = 64
N_BRANCH = 3
D_FF = ((8 * D_MODEL // 3 + 255) // 256) * 256
PLE_DIM = 256
IN_COLS = (2 * A_HALF + (B_HEADS + 2 * B_KV_HEADS) * B_HEAD_DIM + IDX_HEADS * IDX_DIM + IDX_DIM
           + IDX_HEADS + 3 * C_HEADS * C_HEAD_DIM + N_BRANCH * D_MODEL)

kernel_name = 'hybrid_gmlp_dsa_stickbreak_stream_step'


def rms_norm(x, g):
    xf = x.astype(jnp.float32)
    y = xf * lax.rsqrt(jnp.mean(xf * xf, axis=-1, keepdims=True) + EPS)
    return (y * g.astype(jnp.float32)).astype(x.dtype)


def split_columns(z):
    sizes = (A_HALF, A_HALF, B_HEADS * B_HEAD_DIM, B_KV_HEADS * B_HEAD_DIM, B_KV_HEADS * B_HEAD_DIM,
             IDX_HEADS * IDX_DIM, IDX_DIM, IDX_HEADS, C_HEADS * C_HEAD_DIM, C_HEADS * C_HEAD_DIM,
             C_HEADS * C_HEAD_DIM, N_BRANCH * D_MODEL)
    offs, acc = [], 0
    for s in sizes[:-1]:
        acc += s
        offs.append(acc)
    return jnp.split(z, offs, axis=-1)


def to_blocks(a):
    b, s = a.shape[:2]
    return a.reshape(b, s // Q_BLOCK, Q_BLOCK, *a.shape[2:]).swapaxes(0, 1)


def from_blocks(o):
    nb, b, qb = o.shape[:3]
    return o.swapaxes(0, 1).reshape(b, nb * qb, *o.shape[3:])


def gmlp_spatial(u, v, ws, bias):
    b, t, _ = v.shape
    nc = -(-t // A_CHUNK)
    vp = jnp.pad(v, ((0, 0), (0, nc * A_CHUNK - t), (0, 0)))
    vp = vp.reshape(b, nc, A_CHUNK, A_GROUPS, A_GROUP_DIM)
    i = jnp.arange(A_CHUNK)
    mask = (i[None, :] // CHUNK) <= (i[:, None] // CHUNK)
    wm = jnp.where(mask[None], ws, 0)
    mixed = jnp.einsum('gij,bcjgd->bcigd', wm, vp) + bias.T[None, None, :, :, None]
    mixed = mixed.reshape(b, nc * A_CHUNK, A_HALF)[:, :t]
    return u * mixed


def dsa_attend(q, qi, wi, q_pos, k, v, ki, k_pos, topk):
    f32 = jnp.float32
    b, nq = q.shape[:2]
    dots = jnp.einsum('bqhe,ble->bqhl', qi.astype(f32), ki.astype(f32)) * (IDX_DIM ** -0.5)
    score = jnp.einsum('bqhl,bqh->bql', jax.nn.relu(dots), wi.astype(f32))
    admissible = (k_pos[None, :] // CHUNK) <= (q_pos[:, None] // CHUNK)
    score = jnp.where(admissible[None], score, -jnp.inf)
    _, idx = lax.top_k(score, topk)
    valid = (k_pos[idx] // CHUNK) <= (q_pos // CHUNK)[None, :, None]
    gather = jax.vmap(lambda a, ix: a[ix])
    k_sel = gather(k, idx).astype(f32)
    v_sel = gather(v, idx).astype(f32)
    qg = q.reshape(b, nq, B_KV_HEADS, B_HEADS // B_KV_HEADS, B_HEAD_DIM).astype(f32)
    logits = jnp.einsum('bqhgd,bqnhd->bqhgn', qg, k_sel) * (B_HEAD_DIM ** -0.5)
    logits = jnp.where(valid[:, :, None, None, :], logits, -jnp.inf)
    probs = jax.nn.softmax(logits, axis=-1)
    o = jnp.einsum('bqhgn,bqnhd->bqhgd', probs, v_sel)
    return o.reshape(b, nq, B_HEADS * B_HEAD_DIM).astype(q.dtype)


def dsa_prompt(q, qi, wi, k, v, ki):
    s = q.shape[1]
    topk = min(B_TOPK_MAX, s // 4)
    pos = jnp.arange(s)

    def blk(args):
        qb, qib, wib, pb = args
        return dsa_attend(qb, qib, wib, pb, k, v, ki, pos, topk)

    out = lax.map(blk, (to_blocks(q), to_blocks(qi), to_blocks(wi), pos.reshape(s // Q_BLOCK, Q_BLOCK)))
    return from_blocks(out)


def stick_breaking(q, q_pos, k, v, k_pos):
    f32 = jnp.float32
    b, nq = q.shape[:2]
    z = jnp.einsum('bqhd,blhd->bhql', q.astype(f32), k.astype(f32)) * (C_HEAD_DIM ** -0.5)
    mask = k_pos[None, :] < q_pos[:, None]
    log_stay = jnp.where(mask, jax.nn.log_sigmoid(-z), 0.0)
    after = lax.cumsum(log_stay, axis=3, reverse=True) - log_stay
    w = jnp.where(mask, jnp.exp(jax.nn.log_sigmoid(z) + after), 0.0)
    o = jnp.einsum('bhql,blhd->bqhd', w, v.astype(f32))
    return o.reshape(b, nq, C_HEADS * C_HEAD_DIM).astype(q.dtype)


def stick_prompt(q, k, v):
    s = q.shape[1]
    pos = jnp.arange(s)

    def blk(args):
        qb, pb = args
        return stick_breaking(qb, pb, k, v, pos)

    out = lax.map(blk, (to_blocks(q), pos.reshape(s // Q_BLOCK, Q_BLOCK)))
    return from_blocks(out)


def mix_projections(x, prm):
    bsz, t, _ = x.shape
    h = rms_norm(x, prm['norm_mix'])
    au, av, bq, bk, bv, iq, ik, iw, cq, ck, cv, gl = split_columns(h @ prm['w_in'])
    au = jax.nn.gelu(au)
    av = rms_norm(jax.nn.gelu(av), prm['a_vnorm'])
    bq = rms_norm(bq.reshape(bsz, t, B_HEADS, B_HEAD_DIM), prm['b_qnorm'])
    bk = rms_norm(bk.reshape(bsz, t, B_KV_HEADS, B_HEAD_DIM), prm['b_knorm'])
    bv = bv.reshape(bsz, t, B_KV_HEADS, B_HEAD_DIM)
    iq = iq.reshape(bsz, t, IDX_HEADS, IDX_DIM)
    iw = iw * (IDX_HEADS ** -0.5)
    cq = cq.reshape(bsz, t, C_HEADS, C_HEAD_DIM)
    ck = ck.reshape(bsz, t, C_HEADS, C_HEAD_DIM)
    cv = cv.reshape(bsz, t, C_HEADS, C_HEAD_DIM)
    gates = jax.nn.sigmoid(gl + prm['gate_bias']).reshape(bsz, t, N_BRANCH, D_MODEL)
    return au, av, bq, bk, bv, iq, ik, iw, cq, ck, cv, gates


def run_layer(x, p, prm, past):
    au, av, bq, bk, bv, iq, ik, iw, cq, ck, cv, gates = mix_projections(x, prm)
    oa = gmlp_spatial(au, av, prm['a_ws'], prm['a_bias'])
    if past is None:
        ob = dsa_prompt(bq, iq, iw, bk, bv, ik)
        oc = stick_prompt(cq, ck, cv)
    else:
        pbk, pbv, pik, pck, pcv = past
        t = x.shape[1]
        n_past = pbk.shape[1]
        n_keys = n_past + t
        k_pos = jnp.arange(n_keys)
        q_pos = n_past + jnp.arange(t)
        ob = dsa_attend(bq, iq, iw, q_pos,
                        jnp.concatenate([pbk, bk], axis=1), jnp.concatenate([pbv, bv], axis=1),
                        jnp.concatenate([pik, ik], axis=1), k_pos, min(B_TOPK_MAX, n_keys // 4))
        oc = stick_breaking(cq, q_pos, jnp.concatenate([pck, ck], axis=1),
                            jnp.concatenate([pcv, cv], axis=1), k_pos)
    merged = (gates[:, :, 0] * (oa @ prm['w_br_a']) + gates[:, :, 1] * (ob @ prm['w_br_b'])
              + gates[:, :, 2] * (oc @ prm['w_br_c']))
    x = x + merged @ prm['w_out']
    hf = rms_norm(x, prm['norm_ffn'])
    g, up = jnp.split(hf @ prm['w_ffn_in'], 2, axis=-1)
    x = x + (jax.nn.silu(g) * up) @ prm['w_ffn_out']
    ple_gate = jax.nn.sigmoid(rms_norm(x, prm['norm_ple']) @ prm['w_ple_gate'])
    x = x + ple_gate * (p @ prm['w_ple_proj'])
    return x, (bk, bv, ik, ck, cv, av)


def stack_layers(states, i):
    return jnp.stack([s[i] for s in states])


def setup_inputs(seed: int = 0) -> dict:
    key = jax.random.key(seed)
    ks = jax.random.split(key, 32)

    def nrm(k, shape, scale=1.0):
        return jax.random.normal(k, shape, jnp.float32) * scale

    def gain(k, shape):
        return 1.0 + nrm(k, shape, 0.01)

    return {
        'x_prompt': nrm(ks[0], (BATCH, SEQ, D_MODEL)),
        'x_sample': nrm(ks[1], (DEC_BATCH, DEC_SEQ, D_MODEL)),
        'cache_b_k': nrm(ks[2], (DEPTH, DEC_BATCH, PAST_LEN, B_KV_HEADS, B_HEAD_DIM)),
        'cache_b_v': nrm(ks[3], (DEPTH, DEC_BATCH, PAST_LEN, B_KV_HEADS, B_HEAD_DIM)),
        'cache_b_kidx': nrm(ks[4], (DEPTH, DEC_BATCH, PAST_LEN, IDX_DIM)),
        'cache_c_k': nrm(ks[5], (DEPTH, DEC_BATCH, PAST_LEN, C_HEADS, C_HEAD_DIM)),
        'cache_c_v': nrm(ks[6], (DEPTH, DEC_BATCH, PAST_LEN, C_HEADS, C_HEAD_DIM)),
        'p_prompt': nrm(ks[7], (DEPTH, BATCH, SEQ, PLE_DIM)),
        'p_sample': nrm(ks[8], (DEPTH, DEC_BATCH, DEC_SEQ, PLE_DIM)),
        'norm_mix': gain(ks[9], (DEPTH, D_MODEL)),
        'w_in': nrm(ks[10], (DEPTH, D_MODEL, IN_COLS), D_MODEL ** -0.5),
        'gate_bias': nrm(ks[11], (DEPTH, N_BRANCH * D_MODEL), 0.01),
        'a_vnorm': gain(ks[12], (DEPTH, A_HALF)),
        'a_ws': nrm(ks[13], (DEPTH, A_GROUPS, A_CHUNK, A_CHUNK), A_CHUNK ** -0.5),
        'a_bias': 1.0 + nrm(ks[14], (DEPTH, A_GROUPS, A_CHUNK), 0.1),
        'b_qnorm': gain(ks[15], (DEPTH, B_HEAD_DIM)),
        'b_knorm': gain(ks[16], (DEPTH, B_HEAD_DIM)),
        'w_br_a': nrm(ks[17], (DEPTH, A_HALF, D_MODEL), A_HALF ** -0.5),
        'w_br_b': nrm(ks[18], (DEPTH, B_HEADS * B_HEAD_DIM, D_MODEL), (B_HEADS * B_HEAD_DIM) ** -0.5),
        'w_br_c': nrm(ks[19], (DEPTH, C_HEADS * C_HEAD_DIM, D_MODEL), (C_HEADS * C_HEAD_DIM) ** -0.5),
        'w_out': nrm(ks[20], (DEPTH, D_MODEL, D_MODEL), D_MODEL ** -0.5),
        'norm_ffn': gain(ks[21], (DEPTH, D_MODEL)),
        'w_ffn_in': nrm(ks[22], (DEPTH, D_MODEL, 2 * D_FF), D_MODEL ** -0.5),
        'w_ffn_out': nrm(ks[23], (DEPTH, D_FF, D_MODEL), D_FF ** -0.5),
        'norm_ple': gain(ks[24], (DEPTH, D_MODEL)),
        'w_ple_gate': nrm(ks[25], (DEPTH, D_MODEL, D_MODEL), D_MODEL ** -0.5),
        'w_ple_proj': nrm(ks[26], (DEPTH, PLE_DIM, D_MODEL), PLE_DIM ** -0.5),
    }


def reference(x_prompt, x_sample, cache_b_k, cache_b_v, cache_b_kidx, cache_c_k, cache_c_v,
              p_prompt, p_sample, norm_mix, w_in, gate_bias, a_vnorm, a_ws, a_bias, b_qnorm, b_knorm,
              w_br_a, w_br_b, w_br_c, w_out, norm_ffn, w_ffn_in, w_ffn_out, norm_ple, w_ple_gate,
              w_ple_proj):
    yp, ys = x_prompt, x_sample
    new_p, new_s = [], []
    for l in range(DEPTH):
        prm = {
            'norm_mix': norm_mix[l], 'w_in': w_in[l], 'gate_bias': gate_bias[l],
            'a_vnorm': a_vnorm[l], 'a_ws': a_ws[l], 'a_bias': a_bias[l],
            'b_qnorm': b_qnorm[l], 'b_knorm': b_knorm[l],
            'w_br_a': w_br_a[l], 'w_br_b': w_br_b[l], 'w_br_c': w_br_c[l], 'w_out': w_out[l],
            'norm_ffn': norm_ffn[l], 'w_ffn_in': w_ffn_in[l], 'w_ffn_out': w_ffn_out[l],
            'norm_ple': norm_ple[l], 'w_ple_gate': w_ple_gate[l], 'w_ple_proj': w_ple_proj[l],
        }
        yp, st_p = run_layer(yp, p_prompt[l], prm, None)
        ys, st_s = run_layer(ys, p_sample[l], prm,
                             (cache_b_k[l], cache_b_v[l], cache_b_kidx[l], cache_c_k[l], cache_c_v[l]))
        new_p.append(st_p)
        new_s.append(st_s)
    return (yp, ys,
            stack_layers(new_p, 0), stack_layers(new_p, 1), stack_layers(new_p, 2),
            stack_layers(new_p, 3), stack_layers(new_p, 4),
            stack_layers(new_s, 0), stack_layers(new_s, 1), stack_layers(new_s, 2),
            stack_layers(new_s, 3), stack_layers(new_s, 4), stack_layers(new_s, 5))
```

```python
import numpy as np
from contextlib import ExitStack
import concourse.bass as bass
import concourse.mybir as mybir
from concourse.bass_utils import run_bass_kernel_spmd

F32 = mybir.dt.float32
BF16 = mybir.dt.bfloat16
AF = mybir.ActivationFunctionType
ALU = mybir.AluOpType
AX = mybir.AxisListType

D = 1024
SEQ = 4096
NBP = 32
NBT = 34
NTOK = NBT * 128
DFF = 2816
EPS = 1e-6
NEG = -30000.0
BIG = 1.0e30
NIT = 24
C_AU, C_AV, C_BQ, C_BK, C_BV, C_IQ, C_IK, C_IW, C_CQ, C_CK, C_CV, C_GL = (
    0, 512, 1024, 1536, 1664, 1792, 2048, 2080, 2088, 2600, 3112, 3624)
K_ID, K_NU, K_NL, K_ONE, K_DM, K_GM, K_W2 = 0, 128, 256, 384, 512, 512 + 2048, 512 + 2048 + 128
K_TOT = K_W2 + 32
K2_ONE, K2_GM, K2_W2, K2_TOT = 0, 128, 256, 288


def _consts():
    c = np.zeros((128, K_TOT), np.float32)
    j = np.arange(128)[:, None]
    l = np.arange(128)[None, :]
    c[:, K_ID:K_ID + 128] = np.eye(128)
    c[:, K_NU:K_NU + 128] = -1.0 * (j >= l)
    c[:, K_NL:K_NL + 128] = -1.0 * (j < l)
    c[:, K_ONE:K_ONE + 128] = 1.0
    q = np.arange(512)[None, :]
    for o in range(4):
        c[:, K_DM + 512 * o:K_DM + 512 * (o + 1)] = np.where((128 * o + j) >= q, NEG, 0.0)
    c[:, K_GM:K_GM + 128] = ((j // 64) <= (l // 64))
    c[:, K_W2:K_W2 + NIT] = 2.0 ** (-(np.arange(NIT)[None, :] + 0.0))
    c2 = np.concatenate([c[:, K_ONE:K_ONE + 128], c[:, K_GM:K_GM + 128], c[:, K_W2:K_W2 + 32]], axis=1)
    return c, np.ascontiguousarray(c2)


class Trk:
    EP = 16000

    def __init__(self, nc, es):
        self.nc = nc
        self.q = {'pe': nc.tensor, 'act': nc.scalar, 'dve': nc.vector, 'pool': nc.gpsimd, 'sp': nc.sync}
        self.sems = []
        self.es = es
        self.cnt = {e: 0 for e in ('pe', 'act', 'dve', 'pool')}
        self.esem = {e: [] for e in self.cnt}
        self.seen = {e: {} for e in self.q}
        self.lastw = {}
        self.readers = {}
        self.ndma = 40
        self.dsem = [self._new(f"d{i}") for i in range(self.ndma)]
        self.dval = [0] * self.ndma
        self.drr = 0
        self.drr_sw = 0
        self.n_inst = 0
        self.dead = False

    def _new(self, name):
        s = self.es.enter_context(self.nc.semaphore(name))
        self.sems.append(s)
        return len(self.sems) - 1

    def _wait(self, e, ev):
        src, si, val = ev
        if src == 'pe' and e == 'pe':
            return
        if self.seen[e].get(si, 0) >= val:
            return
        self.q[e].wait_ge(self.sems[si], val)
        self.seen[e][si] = val

    def _deps(self, e, r, w):
        evs = []
        for k in r:
            if k in self.lastw:
                evs.append(self.lastw[k])
            if isinstance(k, tuple) and k[0] in ('ps', 'pst'):
                rd = self.readers.get(k)
                if rd:
                    evs.extend(ev for ev in rd.values() if ev[0] != e)
        for k in w:
            if k in self.lastw:
                evs.append(self.lastw[k])
            rd = self.readers.get(k)
            if rd:
                evs.extend(rd.values())
        for ev in evs:
            self._wait(e, ev)

    def _record(self, ev, r, w):
        for k in w:
            self.lastw[k] = ev
            self.readers[k] = {}
        for k in r:
            d = self.readers.setdefault(k, {})
            o = d.get(ev[1])
            if o is None or o[2] < ev[2]:
                d[ev[1]] = ev

    def op(self, e, fn, r=(), w=()):
        if self.dead:
            return None
        self._deps(e, r, w)
        inst = fn(self.q[e])
        n = self.cnt[e]
        ep, off = divmod(n, self.EP)
        if ep >= len(self.esem[e]):
            self.esem[e].append(self._new(f"{e}{ep}"))
        si = self.esem[e][ep]
        inst.then_inc(self.sems[si], 1)
        self.cnt[e] = n + 1
        self.n_inst += 1
        ev = (e, si, off + 1)
        if e != 'pe':
            self.seen[e][si] = max(self.seen[e].get(si, 0), 0)
        self._record(ev, r, w)
        return ev

    def dma(self, e, out, in_, r=(), w=()):
        if self.dead:
            return None
        half = self.ndma // 2
        if e == 'pool':
            i = self.drr_sw
            self.drr_sw = (self.drr_sw + 1) % half
        else:
            i = half + self.drr
            self.drr = (self.drr + 1) % half
        si = self.dsem[i]
        if self.dval[i] > 0:
            self._wait(e, ('dma', si, self.dval[i]))
        self._deps(e, r, w)
        inst = self.q[e].dma_start(out=out, in_=in_)
        self.dval[i] += 16
        inst.then_inc(self.sems[si], 16)
        self.n_inst += 1
        ev = ('dma', si, self.dval[i])
        self._record(ev, r, w)
        return ev

    def barrier(self):
        for e in self.q:
            for o in self.cnt:
                n = self.cnt[o]
                if n == 0 or o == e:
                    continue
                ep, off = divmod(n - 1, self.EP)
                self._wait(e, (o, self.esem[o][ep], off + 1))
            for i in range(self.ndma):
                if self.dval[i]:
                    self._wait(e, ('dma', self.dsem[i], self.dval[i]))
        for e in ('act', 'dve', 'pool'):
            n = self.cnt[e]
            if n:
                ep, off = divmod(n - 1, self.EP)
                self._wait(e, (e, self.esem[e][ep], off + 1))


class _Stop(Exception):
    pass


def build_program(stop=None):
    nc = bass.Bass("TRN2", target_bir_lowering=False)

    def din(name, shape):
        return nc.dram_tensor(name, list(shape), F32, kind="ExternalInput").ap()

    def dout(name, shape):
        return nc.dram_tensor(name, list(shape), F32, kind="ExternalOutput").ap()

    x_p = din("x_p", [SEQ, D]); x_s = din("x_s", [2, 16, D])
    cbk = din("cbk", [2, 2, SEQ, 128]); cbv = din("cbv", [2, 2, SEQ, 128]); cbi = din("cbi", [2, 2, SEQ, 32])
    cck = din("cck", [2, 2, SEQ, 512]); ccv = din("ccv", [2, 2, SEQ, 512])
    p_p = din("p_p", [2, SEQ, 256]); p_s = din("p_s", [2, 2, 16, 256])
    norm_mix = din("norm_mix", [2, D]); w_in = din("w_in", [2, D, 6696]); gbT = din("gbT", [2, 128, 24])
    a_vnorm = din("a_vnorm", [2, 512]); a_wsT = din("a_wsT", [2, 128, 4, 128]); a_bias = din("a_bias", [2, 512])
    b_qnorm = din("b_qnorm", [2, 64]); b_knorm = din("b_knorm", [2, 64])
    w_br = [din("w_br_a", [2, 512, D]), din("w_br_b", [2, 512, D]), din("w_br_c", [2, 512, D])]
    w_out = din("w_out", [2, D, D]); norm_ffn = din("norm_ffn", [2, D]); w_ffn_in = din("w_ffn_in", [2, D, 2 * DFF])
    w_ffn_out = din("w_ffn_out", [2, DFF, D]); norm_ple = din("norm_ple", [2, D]); w_ple_gate = din("w_ple_gate", [2, D, D])
    w_ple_proj = din("w_ple_proj", [2, 256, D]); cst = din("cst", [128, K_TOT]); cst2 = din("cst2", [128, K2_TOT])

    y_p = dout("y_p", [SEQ, D]); y_s = dout("y_s", [2, 16, D])
    o_bk_p = dout("o_bk_p", [2, SEQ, 128]); o_bv_p = dout("o_bv_p", [2, SEQ, 128]); o_bi_p = dout("o_bi_p", [2, SEQ, 32])
    o_ck_p = dout("o_ck_p", [2, SEQ, 512]); o_cv_p = dout("o_cv_p", [2, SEQ, 512])
    o_bk_s = dout("o_bk_s", [2, 2, 16, 128]); o_bv_s = dout("o_bv_s", [2, 2, 16, 128]); o_bi_s = dout("o_bi_s", [2, 2, 16, 32])
    o_ck_s = dout("o_ck_s", [2, 2, 16, 512]); o_cv_s = dout("o_cv_s", [2, 2, 16, 512]); o_av_s = dout("o_av_s", [2, 2, 16, 512])

    dbgk = "ExternalOutput" if stop is not None else "Internal"
    xs = nc.dram_tensor("xs", [NTOK, D], F32, kind=dbgk).ap()
    oT_d = nc.dram_tensor("oT_d", [3, 4, 128, NTOK], BF16, kind=dbgk).ap()

    es = ExitStack()
    with es:
        T = Trk(nc, es)
        uid = [0]

        def U(name):
            uid[0] += 1
            return f"{name}_{uid[0]}"

        def sb(name, shape, dt=F32):
            return es.enter_context(nc.sbuf_tensor(name, list(shape), dt))

        ps = [es.enter_context(nc.psum_tensor(f"ps{i}", [128, 512], F32)) for i in range(7)]
        pst = es.enter_context(nc.psum_tensor("pst", [128, 1024], BF16))
        PK = [('ps', i) for i in range(7)]
        PT = ('pst',)

        cf = sb("cf", [128, K2_TOT]); cb = sb("cb", [128, K_GM + 128], BF16)
        T.dma('sp', cf[:], cst2[:, :], w=['cf'])
        T.dma('pool', cb[:], cst[:, 0:K_GM + 128], w=['cb'])
        ident = cb[:, K_ID:K_ID + 128]; negU = cb[:, K_NU:K_NU + 128]; negL = cb[:, K_NL:K_NL + 128]
        onesb = cb[:, K_ONE:K_ONE + 128]
        ones1 = cf[0:1, K2_ONE:K2_ONE + 128]
        CB = ['cb']

        gmix = sb("gmix", [128, D]); gq8 = sb("gq8", [128, 64]); gk = sb("gk", [128, 64])
        gb = sb("gb", [128, 24])
        rowt = sb("rowt", [1, 512]); shiftc = sb("shiftc", [128, 4])
        xt = [None, None]
        xb = [sb(f"xbh{i}", [128, D], BF16) for i in range(2)]
        col = sb("col", [128, 64])
        stg = [sb(f"stg{i}", [128, 512]) for i in range(2)]
        wk = [sb(f"wk{i}", [128, 512]) for i in range(4)]
        wkb = [sb(f"wkb{i}", [128, 512], BF16) for i in range(4)]
        mmrr = [0]

        def mmbank():
            mmrr[0] ^= 1
            return mmrr[0]

        def bcast(dst, key, row_ap, n, scale=None):
            for c0 in range(0, n, 512):
                c1 = min(n, c0 + 512)
                T.dma('sp', rowt[0:1, 0:c1 - c0], row_ap[:, c0:c1], w=['rowt'])
                b = mmbank()
                T.op('pe', lambda q: q.matmul(ps[b][:, 0:c1 - c0], lhsT=ones1, rhs=rowt[0:1, 0:c1 - c0], start=True, stop=True),
                     r=['rowt', 'cf'], w=[PK[b]])
                if scale is None:
                    T.op('dve', lambda q: q.tensor_copy(out=dst[:, c0:c1], in_=ps[b][:, 0:c1 - c0]), r=[PK[b]], w=[key])
                else:
                    T.op('dve', lambda q: q.tensor_scalar(out=dst[:, c0:c1], in0=ps[b][:, 0:c1 - c0], scalar1=scale, scalar2=None, op0=ALU.mult),
                         r=[PK[b]], w=[key])

        wslot = [0]

        def load_w(src3, ncols, key=None):
            i = wslot[0]; wslot[0] ^= 1
            kc = src3.shape[1]
            T.dma('pool', wbuf[i][:, 0:kc, 0:ncols], src3, w=[('wbuf', i)])
            return wbuf[i], ('wbuf', i)

        def wview(wap, l, c0, c1):
            return wap[l].rearrange("(kc p) n -> p kc n", p=128)[:, :, c0:c1]

        def rmsnorm_rows(src_ap, src_keys, n, npart, gtile, gkey, dst_bf, dst_key, cidx):
            T.op('act', lambda q: q.activation(out=dst_bf, in_=src_ap, func=AF.Square,
                                               accum_out=col[0:npart, cidx:cidx + 1]), r=src_keys, w=[dst_key, ('col', cidx)])
            T.op('act', lambda q: q.activation(out=col[0:npart, cidx:cidx + 1], in_=col[0:npart, cidx:cidx + 1], func=AF.Sqrt, bias=EPS, scale=1.0 / n), r=[('col', cidx)], w=[('col', cidx)])
            T.op('dve', lambda q: q.reciprocal(out=col[0:npart, cidx:cidx + 1], in_=col[0:npart, cidx:cidx + 1]), r=[('col', cidx)], w=[('col', cidx)])
            T.op('dve', lambda q: q.scalar_tensor_tensor(out=dst_bf, in0=src_ap, scalar=col[0:npart, cidx:cidx + 1], in1=gtile[0:npart, 0:n],
                                                         op0=ALU.mult, op1=ALU.mult), r=list(src_keys) + [('col', cidx), gkey], w=[dst_key])

        def transpose_to(dst3, dst_key, src_bf, src_key, nchunks, npart=128):
            for c in range(nchunks):
                T.op('pe', lambda q: q.transpose(pst[:, c * 128:c * 128 + npart], src_bf[0:npart, c * 128:(c + 1) * 128], ident[0:npart, 0:npart]),
                     r=[src_key] + CB, w=[PT])
            T.op('act', lambda q: q.copy(out=dst3, in_=pst[:, 0:nchunks * 128].rearrange("p (c t) -> p c t", c=nchunks)[:, :, 0:npart]),
                 r=[PT], w=[dst_key])

        def gelu_to(dst, dst_key, src_ap, src_keys, npart, n, tmp_i):
            a = wk[tmp_i][0:npart, 0:n]; b = wk[tmp_i + 1][0:npart, 0:n]
            ka, kb = ('wk', tmp_i), ('wk', tmp_i + 1)
            T.op('act', lambda q: q.activation(out=a, in_=src_ap, func=AF.Square), r=src_keys, w=[ka])
            T.op('dve', lambda q: q.tensor_scalar(out=a, in0=a, scalar1=0.044715, scalar2=1.0, op0=ALU.mult, op1=ALU.add), r=[ka], w=[ka])
            T.op('dve', lambda q: q.tensor_tensor(out=a, in0=a, in1=src_ap, op=ALU.mult), r=[ka] + list(src_keys), w=[ka])
            T.op('act', lambda q: q.activation(out=b, in_=a, func=AF.Sigmoid, scale=1.5957691216057308), r=[ka], w=[kb])
            T.op('dve', lambda q: q.tensor_tensor(out=dst, in0=b, in1=src_ap, op=ALU.mult), r=[kb] + list(src_keys), w=[dst_key])

        def x_block_load(l, gblk, slot):
            key = ('xt', slot)
            if gblk < NBP:
                src = (x_p if l == 0 else xs)[gblk * 128:(gblk + 1) * 128, :]
                T.dma('sp', xt[slot][:], src, w=[key])
            else:
                s = gblk - NBP
                T.op('pool', lambda q: q.memset(xt[slot][:], 0.0), w=[key])
                src = x_s[s] if l == 0 else xs[gblk * 128:gblk * 128 + 16, :]
                T.dma('sp', xt[slot][0:16, :], src, w=[key])
            return key

        def norm_block_to_hT(l, gblk, slot, gt, gkey, hdst, hkey):
            key = ('xt', slot)
            rmsnorm_rows(xt[slot][:], [key], D, 128, gt, gkey, xb[slot][:], ('xb', slot), 0)
            transpose_to(hdst, hkey, xb[slot], ('xb', slot), 8)

        open_scopes = []

        def CK(name):
            if stop == name:
                T.dead = True

        try:
          for l in range(2):
              bcast(gmix, 'gmix', norm_mix[l:l + 1, :], D)
              bcast(gq8, 'gq8', b_qnorm[l:l + 1, :], 64, scale=0.125)
              bcast(gk, 'gk', b_knorm[l:l + 1, :], 64)
              T.dma('sp', gb[:], gbT[l], w=['gb'])
              T.op('dve', lambda q: q.tensor_reduce(out=shiftc[:, 0:1], in_=gq8[:], axis=AX.X, op=ALU.max, apply_absolute_value=True), r=['gq8'], w=['shiftc'])
              T.op('dve', lambda q: q.tensor_reduce(out=shiftc[:, 1:2], in_=gk[:], axis=AX.X, op=ALU.max, apply_absolute_value=True), r=['gk', 'shiftc'], w=['shiftc'])
              T.op('dve', lambda q: q.tensor_scalar(out=shiftc[:, 2:3], in0=shiftc[:, 0:1], scalar1=shiftc[:, 1:2], scalar2=-64.0, op0=ALU.mult, op1=ALU.mult),
                   r=['shiftc'], w=['shiftc2'])
              nshift = shiftc[:, 2:3]

              s12 = ExitStack()
              s12.__enter__(); open_scopes.append(s12)
              hT = s12.enter_context(nc.sbuf_tensor(U("hT"), [128, 8, NTOK], BF16))
              for i_x in range(2):
                  xt[i_x] = s12.enter_context(nc.sbuf_tensor(U("xt"), [128, D], F32))
              for gblk in range(NBT):
                  slot = gblk & 1
                  x_block_load(l, gblk, slot)
                  norm_block_to_hT(l, gblk, slot, gmix, 'gmix', hT[:, :, gblk * 128:(gblk + 1) * 128], ('hT', gblk))

              CK('p1')
              with ExitStack() as sa:
                  def sba(name, shape, dt=F32):
                      return sa.enter_context(nc.sbuf_tensor(U(name), list(shape), dt))
                  w_au = sba("w_au", [128, 8, 512], BF16); w_av = sba("w_av", [128, 8, 512], BF16)
                  avn = sba("avn", [128, 512]); abias = sba("abias", [128, 512])
                  wsT = sba("wsT", [128, 4, 128], BF16); wsTf = sba("wsTf", [128, 4, 128])
                  bcast(avn, 'avn', a_vnorm[l:l + 1, :], 512)
                  bcast(abias, 'abias', a_bias[l:l + 1, :], 512)
                  T.dma('sp', wsTf[:], a_wsT[l], w=['wsTf'])
                  for g in range(4):
                      T.op('dve', lambda q: q.tensor_tensor(out=wsT[:, g, :], in0=wsTf[:, g, :], in1=cf[:, K2_GM:K2_GM + 128], op=ALU.mult),
                           r=['wsTf', 'cf'], w=['wsT'])
                  oaT = [sba(f"oaT{i}", [128, 4, 128], BF16) for i in range(2)]
                  vtm = [sba(f"vtm{i}", [128, 512], BF16) for i in range(2)]
                  uT = [sba(f"uT{i}", [128, 4, 128]) for i in range(2)]
                  T.dma('pool', w_au[:], wview(w_in, l, C_AU, C_AU + 512), w=['w_au'])
                  T.dma('pool', w_av[:], wview(w_in, l, C_AV, C_AV + 512), w=['w_av'])
                  for gblk in range(NBT):
                      sl = gblk & 1
                      hk = ('hT', gblk)
                      hcols = slice(gblk * 128, (gblk + 1) * 128)
                      b = mmbank()
                      for kc in range(8):
                          T.op('pe', lambda q: q.matmul(ps[b][:, :], lhsT=hT[:, kc, hcols], rhs=w_av[:, kc, :], start=(kc == 0), stop=(kc == 7)),
                               r=[hk, 'w_av'], w=[PK[b]])
                      gelu_to(wk[2][:, :], ('wk', 2), ps[b][:, :], [PK[b]], 128, 512, 0)
                      rmsnorm_rows(wk[2][:, :], [('wk', 2)], 512, 128, avn, 'avn', vtm[sl][:], ('vtm', sl), 1)
                      if gblk >= NBP:
                          s = gblk - NBP
                          T.op('dve', lambda q: q.scalar_tensor_tensor(out=stg[0][0:16, 0:512], in0=wk[2][0:16, :], scalar=col[0:16, 1:2], in1=avn[0:16, :],
                                                                       op0=ALU.mult, op1=ALU.mult), r=[('wk', 2), ('col', 1), 'avn'], w=[('stg', 0)])
                          T.dma('sp', o_av_s[l, s], stg[0][0:16, 0:512], r=[('stg', 0)])
                      b2 = mmbank()
                      for g in range(4):
                          for kc in range(8):
                              T.op('pe', lambda q: q.matmul(ps[b2][:, g * 128:(g + 1) * 128], lhsT=w_au[:, kc, g * 128:(g + 1) * 128], rhs=hT[:, kc, hcols],
                                                            start=(kc == 0), stop=(kc == 7)), r=[hk, 'w_au'], w=[PK[b2]])
                      gelu_to(uT[sl][:].rearrange("p g t -> p (g t)"), ('uT', sl), ps[b2][:, :], [PK[b2]], 128, 512, 0)
                      b3 = 2
                      for g in range(4):
                          T.op('pe', lambda q: q.matmul(ps[b3][:, g * 128:(g + 1) * 128], lhsT=vtm[sl][:, g * 128:(g + 1) * 128], rhs=wsT[:, g, :],
                                                        start=True, stop=True), r=[('vtm', sl), 'wsT'], w=[PK[b3]])
                      T.op('dve', lambda q: q.tensor_tensor(out=wk[2][:, :], in0=ps[b3][:, :], in1=abias[:, :], op=ALU.add), r=[PK[b3], 'abias'], w=[('wk', 2)])
                      T.op('dve', lambda q: q.tensor_tensor(out=oaT[sl][:].rearrange("p g t -> p (g t)"), in0=wk[2][:, :],
                                                            in1=uT[sl][:].rearrange("p g t -> p (g t)"), op=ALU.mult), r=[('wk', 2), ('uT', sl)], w=[('oaT', sl)])
                      T.dma('sp', oT_d[0, :, :, gblk * 128:(gblk + 1) * 128].rearrange("c p t -> p c t"), oaT[sl][:], r=[('oaT', sl)], w=[('oTd', 0, gblk)])
                  T.barrier()

              CK('pa')
              jobs = [dict(kind='p', nb=NBP, qblocks=list(range(NBP)), gbase=0)]
              for s in range(2):
                  jobs.append(dict(kind='s', s=s, nb=NBP + 1, qblocks=[NBP], gbase=None))

              def gcol(job, blk):
                  if job['kind'] == 'p':
                      return blk
                  assert blk == NBP
                  return NBP + job['s']

              for job in jobs:
                  nb = job['nb']
                  L = nb * 128
                  issamp = job['kind'] == 's'
                  comp_blocks = list(range(NBP)) if not issamp else [NBP]
                  with ExitStack() as sB:
                      def sbb(name, shape, dt=F32):
                          return sB.enter_context(nc.sbuf_tensor(U(name), list(shape), dt))
                      bkT = sbb("bkT", [128, L], BF16); bv2 = sbb("bv2", [128, nb, 128], BF16); ikT = sbb("ikT", [32, L], BF16)
                      w_k = sbb("w_k", [128, 8, 288], BF16); w_q = sbb("w_q", [128, 8, 512], BF16); w_i = sbb("w_i", [128, 8, 264], BF16)
                      scores = sbb("scores", [128, L]); maskb = sbb("maskb", [128, L], BF16); mneg2 = [sbb(f"mnegT{i}", [128, nb, 128], BF16) for i in range(2)] if not issamp else [sbb("mnegT0", [128, nb, 128], BF16)] * 2
                      kb = [sbb(f"kb{i}", [128, 288], BF16) for i in range(2)]
                      bqn = sbb("bqn", [128, 512], BF16); bqT2 = [sbb(f"bqT{i}", [128, 4, 128], BF16) for i in range(2)]
                      iqb = sbb("iqb", [128, 256], BF16); iqT = sbb("iqT", [32, 8, 128], BF16); wq = sbb("wq", [128, 8])
                      bis = sbb("bis", [128, 32]); obT = sbb("obT", [128, 4, 128], BF16); oraw = [sbb("oraw0", [128, 512])] * 2; draw = [sbb("draw0", [128, 512])] * 2
                      pB = [sbb(f"pB{i}", [128, 512], BF16) for i in range(2)]
                      T.dma('pool', w_k[:, :, 0:256], wview(w_in, l, C_BK, C_BK + 256), w=['w_k'])
                      T.dma('pool', w_k[:, :, 256:288], wview(w_in, l, C_IK, C_IK + 32), w=['w_k'])
                      T.dma('pool', w_q[:], wview(w_in, l, C_BQ, C_BQ + 512), w=['w_q'])
                      T.dma('pool', w_i[:, :, 0:256], wview(w_in, l, C_IQ, C_IQ + 256), w=['w_i'])
                      T.dma('pool', w_i[:, :, 256:264], wview(w_in, l, C_IW, C_IW + 8), w=['w_i'])

                      if issamp:
                          s = job['s']
                          ks8 = [sbb(f"ks8{i}", [128, 8, 160], BF16) for i in range(2)]
                          for gi in range(NBP // 8):
                              b0 = gi * 8
                              st = ks8[gi & 1]
                              rws = slice(b0 * 128, (b0 + 8) * 128)
                              T.dma('pool', st[:, :, 0:128], cbk[l, s, rws, :].rearrange("(b p) c -> p b c", p=128), w=[('ks8k', gi & 1)])
                              T.dma('pool', st[:, :, 128:160], cbi[l, s, rws, :].rearrange("(b p) c -> p b c", p=128), w=[('ks8i', gi & 1)])
                              T.dma('pool', bv2[:, b0:b0 + 8, :], cbv[l, s, rws, :].rearrange("(b p) c -> p b c", p=128), w=[('bv2', b_) for b_ in range(b0, b0 + 8)])
                              for j in range(8):
                                  blk = b0 + j
                                  T.op('pe', lambda q: q.transpose(pst[:, 0:128], st[:, j, 0:128], ident), r=[('ks8k', gi & 1)] + CB, w=[PT])
                                  T.op('pe', lambda q: q.transpose(pst[0:32, 128:256], st[:, j, 128:160], ident), r=[('ks8i', gi & 1)] + CB, w=[PT])
                                  T.op('act', lambda q: q.copy(out=bkT[:, blk * 128:(blk + 1) * 128], in_=pst[:, 0:128]), r=[PT], w=[('bkT', blk)])
                                  T.op('act', lambda q: q.copy(out=ikT[0:32, blk * 128:(blk + 1) * 128], in_=pst[0:32, 128:256]), r=[PT], w=[('ikT', blk)])
                      for blk in range(nb):
                          if issamp and blk < NBP:
                              continue
                          sl = blk & 1
                          kkey = ('kb', sl)
                          if blk in comp_blocks:
                              gc = gcol(job, blk)
                              hk = ('hT', gc)
                              hcols = slice(gc * 128, (gc + 1) * 128)
                              b = mmbank()
                              for kc in range(8):
                                  T.op('pe', lambda q: q.matmul(ps[b][:, 0:288], lhsT=hT[:, kc, hcols], rhs=w_k[:, kc, :], start=(kc == 0), stop=(kc == 7)),
                                       r=[hk, 'w_k'], w=[PK[b]])
                              T.op('act', lambda q: q.activation(out=wk[0][:, 0:128], in_=ps[b][:, 0:128], func=AF.Square), r=[PK[b]], w=[('wk', 0)])
                              T.op('dve', lambda q: q.tensor_reduce(out=col[:, 8:10], in_=wk[0][:, 0:128].rearrange("p (h d) -> p h d", h=2), axis=AX.X, op=ALU.add),
                                   r=[('wk', 0)], w=[('col', 8)])
                              T.op('act', lambda q: q.activation(out=col[:, 8:10], in_=col[:, 8:10], func=AF.Sqrt, bias=EPS, scale=1.0 / 64), r=[('col', 8)], w=[('col', 8)])
                              T.op('dve', lambda q: q.reciprocal(out=col[:, 8:10], in_=col[:, 8:10]), r=[('col', 8)], w=[('col', 8)])
                              so = stg[sl]
                              for h in range(2):
                                  T.op('dve', lambda q: q.scalar_tensor_tensor(out=so[:, h * 64:(h + 1) * 64], in0=ps[b][:, h * 64:(h + 1) * 64], scalar=col[:, 8 + h:9 + h],
                                                                               in1=gk[:, :], op0=ALU.mult, op1=ALU.mult), r=[PK[b], ('col', 8), 'gk'], w=[('stg', sl)])
                              T.op('act', lambda q: q.copy(out=so[:, 128:288], in_=ps[b][:, 128:288]), r=[PK[b]], w=[('stg', sl)])
                              T.op('dve', lambda q: q.tensor_copy(out=kb[sl][:, :], in_=so[:, 0:288]), r=[('stg', sl)], w=[kkey])
                              if not issamp:
                                  rows = slice(blk * 128, (blk + 1) * 128)
                                  T.dma('sp', o_bk_p[l, rows, :], so[:, 0:128], r=[('stg', sl)])
                                  T.dma('sp', o_bv_p[l, rows, :], so[:, 128:256], r=[('stg', sl)])
                                  T.dma('sp', o_bi_p[l, rows, :], so[:, 256:288], r=[('stg', sl)])
                              else:
                                  s = job['s']
                                  T.dma('sp', o_bk_s[l, s], so[0:16, 0:128], r=[('stg', sl)])
                                  T.dma('sp', o_bv_s[l, s], so[0:16, 128:256], r=[('stg', sl)])
                                  T.dma('sp', o_bi_s[l, s], so[0:16, 256:288], r=[('stg', sl)])
                          else:
                              s = job['s']
                              rows = slice(blk * 128, (blk + 1) * 128)
                              T.dma('pool', kb[sl][:, 0:128], cbk[l, s, rows, :], w=[kkey])
                              T.dma('pool', kb[sl][:, 128:256], cbv[l, s, rows, :], w=[kkey])
                              T.dma('pool', kb[sl][:, 256:288], cbi[l, s, rows, :], w=[kkey])
                          T.op('pe', lambda q: q.transpose(pst[:, 0:128], kb[sl][:, 0:128], ident), r=[kkey] + CB, w=[PT])
                          T.op('pe', lambda q: q.transpose(pst[0:32, 128:256], kb[sl][:, 256:288], ident), r=[kkey] + CB, w=[PT])
                          T.op('act', lambda q: q.copy(out=bkT[:, blk * 128:(blk + 1) * 128], in_=pst[:, 0:128]), r=[PT], w=[('bkT', blk)])
                          T.op('act', lambda q: q.copy(out=ikT[0:32, blk * 128:(blk + 1) * 128], in_=pst[0:32, 128:256]), r=[PT], w=[('ikT', blk)])
                          T.op('pool', lambda q: q.tensor_copy(out=bv2[:, blk, :], in_=kb[sl][:, 128:256]), r=[kkey], w=[('bv2', blk)])

                      CK('bk')
                      scrr = [0]

                      def stageX(qb, slot):
                              gc = gcol(job, qb)
                              hk = ('hT', gc)
                              hcols = slice(gc * 128, (gc + 1) * 128)
                              Lq = (qb + 1) * 128
                              nlb = qb + 1
                              bq_ = mmbank()
                              for kc in range(8):
                                  T.op('pe', lambda q: q.matmul(ps[bq_][:, :], lhsT=hT[:, kc, hcols], rhs=w_q[:, kc, :], start=(kc == 0), stop=(kc == 7)),
                                       r=[hk, 'w_q'], w=[PK[bq_]])
                              T.op('act', lambda q: q.activation(out=wk[0][:, :], in_=ps[bq_][:, :], func=AF.Square), r=[PK[bq_]], w=[('wk', 0)])
                              bi_ = mmbank()
                              for kc in range(8):
                                  T.op('pe', lambda q: q.matmul(ps[bi_][:, 0:264], lhsT=hT[:, kc, hcols], rhs=w_i[:, kc, :], start=(kc == 0), stop=(kc == 7)),
                                       r=[hk, 'w_i'], w=[PK[bi_]])
                              T.op('act', lambda q: q.copy(out=iqb[:, :], in_=ps[bi_][:, 0:256]), r=[PK[bi_]], w=['iqb'])
                              yield
                              T.op('dve', lambda q: q.tensor_reduce(out=col[:, 16:24], in_=wk[0][:, :].rearrange("p (h d) -> p h d", h=8), axis=AX.X, op=ALU.add),
                                   r=[('wk', 0)], w=[('col', 16)])
                              T.op('act', lambda q: q.activation(out=col[:, 16:24], in_=col[:, 16:24], func=AF.Sqrt, bias=EPS, scale=1.0 / 64), r=[('col', 16)], w=[('col', 16)])
                              T.op('dve', lambda q: q.reciprocal(out=col[:, 16:24], in_=col[:, 16:24]), r=[('col', 16)], w=[('col', 16)])
                              for h in range(8):
                                  T.op('dve', lambda q: q.scalar_tensor_tensor(out=bqn[:, (h % 4) * 128 + (h // 4) * 64:(h % 4) * 128 + (h // 4) * 64 + 64], in0=ps[bq_][:, h * 64:(h + 1) * 64], scalar=col[:, 16 + h:17 + h],
                                                                               in1=gq8[:, :], op0=ALU.mult, op1=ALU.mult), r=[PK[bq_], ('col', 16), 'gq8'], w=['bqn'])
                              transpose_to(bqT2[slot][:], ('bqT', slot), bqn, 'bqn', 4)
                              T.op('dve', lambda q: q.tensor_scalar(out=wq[:, :], in0=ps[bi_][:, 256:264], scalar1=(8.0 ** -0.5) * (32.0 ** -0.5), scalar2=None, op0=ALU.mult),
                                   r=[PK[bi_]], w=['wq'])
                              for h in range(8):
                                  T.op('pe', lambda q: q.transpose(pst[0:32, h * 128:(h + 1) * 128], iqb[:, h * 32:(h + 1) * 32], ident), r=['iqb'] + CB, w=[PT])
                              T.op('act', lambda q: q.copy(out=iqT[:], in_=pst[0:32, :].rearrange("p (h t) -> p h t", h=8)), r=[PT], w=['iqT'])
                              yield
                              for c0 in range(0, Lq, 512):
                                  c1 = min(Lq, c0 + 512)
                                  n = c1 - c0
                                  kdeps = [('ikT', bb) for bb in range(c0 // 128, c1 // 128)]
                                  seng = 'dve'
                                  for h in range(8):
                                      b = mmbank()
                                      T.op('pe', lambda q: q.matmul(ps[b][:, 0:n], lhsT=iqT[:, h, :], rhs=ikT[0:32, c0:c1], start=True, stop=True),
                                           r=['iqT'] + kdeps, w=[PK[b]])
                                      ws = scrr[0]; scrr[0] = (scrr[0] + 1) % 4
                                      T.op('act', lambda q: q.activation(out=wk[ws][:, 0:n], in_=ps[b][:, 0:n], func=AF.Relu), r=[PK[b]], w=[('wk', ws)])
                                      if h == 0:
                                          T.op(seng, lambda q: q.tensor_scalar(out=scores[:, c0:c1], in0=wk[ws][:, 0:n], scalar1=wq[:, 0:1], scalar2=None, op0=ALU.mult),
                                               r=[('wk', ws), 'wq'], w=[('sc', c0)])
                                      elif seng == 'dve':
                                          T.op('dve', lambda q: q.scalar_tensor_tensor(out=scores[:, c0:c1], in0=wk[ws][:, 0:n], scalar=wq[:, h:h + 1], in1=scores[:, c0:c1],
                                                                                       op0=ALU.mult, op1=ALU.add), r=[('wk', ws), 'wq', ('sc', c0)], w=[('sc', c0)])
                                      else:
                                          T.op('pool', lambda q: q.tensor_scalar(out=wk[ws][:, 0:n], in0=wk[ws][:, 0:n], scalar1=wq[:, h:h + 1], scalar2=None, op0=ALU.mult),
                                               r=[('wk', ws), 'wq'], w=[('wk', ws)])
                                          T.op('pool', lambda q: q.tensor_tensor(out=scores[:, c0:c1], in0=scores[:, c0:c1], in1=wk[ws][:, 0:n], op=ALU.add),
                                               r=[('wk', ws), ('sc', c0)], w=[('sc', c0)])
                              sck = [('sc', c0) for c0 in range(0, Lq, 512)]
                              T.op('dve', lambda q: q.tensor_reduce(out=bis[:, 0:1], in_=scores[:, 0:Lq], axis=AX.X, op=ALU.max, apply_absolute_value=True), r=sck, w=['bis'])
                              if not issamp:
                                  T.op('pool', lambda q: q.memset(scores[0:64, Lq - 64:Lq], -BIG), r=['bis'], w=sck)
                              else:
                                  T.op('pool', lambda q: q.memset(scores[:, SEQ + 16:Lq], -BIG), r=['bis'], w=sck)
                              if Lq > 256:
                                  T.op('dve', lambda q: q.tensor_scalar(out=bis[:, 0:1], in0=bis[:, 0:1], scalar1=1.0, scalar2=None, op0=ALU.add), r=['bis'], w=['bis'])
                                  T.op('dve', lambda q: q.tensor_scalar(out=bis[:, 1:2], in0=bis[:, 0:1], scalar1=-1.0, scalar2=None, op0=ALU.mult), r=['bis'], w=['bis'])
                                  T.op('dve', lambda q: q.tensor_scalar(out=bis[:, 4:4 + NIT], in0=cf[:, K2_W2:K2_W2 + NIT], scalar1=bis[:, 0:1], scalar2=None, op0=ALU.mult),
                                       r=['bis', 'cf'], w=['bis'])
                                  for it in range(NIT):
                                      T.op('dve', lambda q: q.tensor_tensor(out=bis[:, 2:3], in0=bis[:, 1:2], in1=bis[:, 4 + it:5 + it], op=ALU.add), r=['bis'], w=['bis'])
                                      T.op('dve', lambda q: q.tensor_scalar(out=maskb[:, 0:Lq], in0=scores[:, 0:Lq], scalar1=bis[:, 2:3], scalar2=0.0, op0=ALU.is_ge, op1=ALU.add,
                                                                            accum_out=bis[:, 3:4]), r=['bis'] + sck, w=['bis', 'maskb'])
                                      T.op('dve', lambda q: q.tensor_scalar(out=bis[:, 3:4], in0=bis[:, 3:4], scalar1=255.5, scalar2=bis[:, 4 + it:5 + it], op0=ALU.is_ge, op1=ALU.mult),
                                           r=['bis'], w=['bis'])
                                      T.op('dve', lambda q: q.tensor_tensor(out=bis[:, 1:2], in0=bis[:, 1:2], in1=bis[:, 3:4], op=ALU.add), r=['bis'], w=['bis'])
                                  thr = bis[:, 1:2]
                                  T.op('dve', lambda q: q.tensor_scalar(out=maskb[:, 0:Lq], in0=scores[:, 0:Lq], scalar1=thr, scalar2=None, op0=ALU.is_ge), r=['bis'] + sck, w=['maskb'])
                              else:
                                  T.op('dve', lambda q: q.tensor_scalar(out=maskb[:, 0:Lq], in0=scores[:, 0:Lq], scalar1=-1.0e29, scalar2=None, op0=ALU.is_ge), r=sck, w=['maskb'])
                              CK('topk')
                              yield
                              for lb0 in range(0, nlb, 8):
                                  lb1 = min(nlb, lb0 + 8)
                                  for lb in range(lb0, lb1):
                                      T.op('pe', lambda q: q.transpose(pst[:, (lb - lb0) * 128:(lb - lb0 + 1) * 128], maskb[:, lb * 128:(lb + 1) * 128], ident),
                                           r=['maskb'] + CB, w=[PT])
                                  T.op('dve', lambda q: q.tensor_scalar(out=mneg2[slot][:, lb0:lb1, :], in0=pst[:, 0:(lb1 - lb0) * 128].rearrange("p (c t) -> p c t", c=lb1 - lb0),
                                                                        scalar1=1.0, scalar2=-NEG, op0=ALU.subtract, op1=ALU.mult), r=[PT], w=[('mnegT', slot)])

                      def stageY(qb, slot):
                              gc = gcol(job, qb)
                              nlb = qb + 1
                              for g in range(2):
                                  bo, bd = 4, 5
                                  prow = slice(g * 64, g * 64 + 64)

                                  def logits(lb):
                                      bz = 2 + (lb & 1)
                                      T.op('pe', lambda q: q.matmul(ps[bz][:, :], lhsT=bkT[prow, lb * 128:(lb + 1) * 128],
                                                                    rhs=bqT2[slot][prow, :, :], start=True, stop=False), r=[('bkT', lb), ('bqT', slot)], w=[PK[bz]])
                                      for hh in range(4):
                                          T.op('pe', lambda q: q.matmul(ps[bz][:, hh * 128:(hh + 1) * 128], lhsT=ident, rhs=mneg2[slot][:, lb, :], start=False, stop=(hh == 3)),
                                               r=[('mnegT', slot)] + CB, w=[PK[bz]])
                                      T.op('act', lambda q: q.activation(out=pB[lb & 1][:, :], in_=ps[bz][:, :], func=AF.Exp, bias=nshift, scale=1.0),
                                           r=[PK[bz], 'shiftc2'], w=[('pB', lb & 1)])
                                  logits(0)
                                  for lb in range(nlb):
                                      sl = lb & 1
                                      if lb + 1 < nlb:
                                          logits(lb + 1)
                                      T.op('pe', lambda q: q.matmul(ps[bo][:, :], lhsT=bv2[:, lb, :], rhs=pB[sl][:, :], start=(lb == 0), stop=(lb == nlb - 1)),
                                           r=[('bv2', lb), ('pB', sl)], w=[PK[bo]])
                                      T.op('pe', lambda q: q.matmul(ps[bd][:, :], lhsT=onesb, rhs=pB[sl][:, :], start=(lb == 0), stop=(lb == nlb - 1)),
                                           r=[('pB', sl)] + CB, w=[PK[bd]])
                                  T.op('act', lambda q: q.copy(out=oraw[g][prow, :], in_=ps[bo][prow, :]), r=[PK[bo]], w=['oraw'])
                                  T.op('act', lambda q: q.activation(out=draw[g][prow, :], in_=ps[bd][prow, :], func=AF.Ln, bias=1e-30, scale=1.0), r=[PK[bd]], w=['draw'])
                                  T.op('act', lambda q: q.activation(out=draw[g][prow, :], in_=draw[g][prow, :], func=AF.Exp, scale=-1.0), r=['draw'], w=['draw'])
                                  T.op('pool', lambda q: q.tensor_tensor(out=obT[prow, :, :].rearrange("p c t -> p (c t)"), in0=oraw[g][prow, :], in1=draw[g][prow, :], op=ALU.mult),
                                       r=['oraw', 'draw'], w=['obT'])
                              T.dma('sp', oT_d[1, :, :, gc * 128:(gc + 1) * 128].rearrange("c p t -> p c t"), obT[:], r=['obT'], w=[('oTd', 1, gc)])

                      qbs = job['qblocks']
                      nq_ = len(qbs)
                      gens = [stageX(qbs[i], i & 1) for i in range(nq_)]
                      next(gens[0]); next(gens[0]); next(gens[0])
                      if nq_ > 1:
                          next(gens[1])
                      next(gens[0], None)
                      if nq_ > 1:
                          next(gens[1])
                      for i in range(1, nq_):
                          next(gens[i])
                          if i + 1 < nq_:
                              next(gens[i + 1])
                          stageY(qbs[i - 1], (i - 1) & 1)
                          next(gens[i], None)
                          if i + 1 < nq_:
                              next(gens[i + 1])
                      stageY(qbs[-1], (nq_ - 1) & 1)
                      T.barrier()

                  CK('B')
                  with ExitStack() as sC:
                      def sbc(name, shape, dt=F32):
                          return sC.enter_context(nc.sbuf_tensor(U(name), list(shape), dt))
                      nq = 128 * len(job['qblocks'])
                      ckT = sbc("ckT", [128, L], BF16); cvt = sbc("cvt", [128, nb, 128], BF16); cqT = sbc("cqT", [128, nq], BF16)
                      w_c = sbc("w_c", [128, 8, 384], BF16); kc2 = [sbc(f"kc2{i}", [128, 128], BF16) for i in range(2)]
                      kcs = [sbc(f"kcs{i}", [128, 8, 128], BF16) for i in range(2)] if issamp else None
                      ocT = [sbc(f"ocT{i}", [128, 512], BF16) for i in range(2)]
                      Gt = [sbc(f"Gt{i}", [128, 512]) for i in range(2)]; wvt = [sbc(f"wvt{i}", [128, 512], BF16) for i in range(4)]
                      for hp in range(4):
                          T.dma('pool', w_c[:, :, 0:128], wview(w_in, l, C_CQ + hp * 128, C_CQ + (hp + 1) * 128), w=['w_c'])
                          T.dma('pool', w_c[:, :, 128:256], wview(w_in, l, C_CK + hp * 128, C_CK + (hp + 1) * 128), w=['w_c'])
                          T.dma('pool', w_c[:, :, 256:384], wview(w_in, l, C_CV + hp * 128, C_CV + (hp + 1) * 128), w=['w_c'])
                          if issamp:
                              s = job['s']
                              for gi in range(NBP // 8):
                                  b0 = gi * 8
                                  st = kcs[gi & 1]
                                  rws = slice(b0 * 128, (b0 + 8) * 128)
                                  T.dma('pool', st[:], cck[l, s, rws, hp * 128:(hp + 1) * 128].rearrange("(b p) c -> p b c", p=128), w=[('kcs', gi & 1)])
                                  T.dma('pool', cvt[:, b0:b0 + 8, :], ccv[l, s, rws, hp * 128:(hp + 1) * 128].rearrange("(b p) c -> p b c", p=128),
                                        w=[('cvt', b_) for b_ in range(b0, b0 + 8)])
                                  for j in range(8):
                                      blk = b0 + j
                                      T.op('pe', lambda q: q.transpose(pst[:, (j & 1) * 128:(j & 1) * 128 + 128], st[:, j, :], ident), r=[('kcs', gi & 1)] + CB, w=[PT])
                                      T.op('act', lambda q: q.copy(out=ckT[:, blk * 128:(blk + 1) * 128], in_=pst[:, (j & 1) * 128:(j & 1) * 128 + 128]), r=[PT], w=[('ckT', blk)])
                          for blk in range(nb):
                              if issamp and blk < NBP:
                                  continue
                              sl = blk & 1
                              kkey = ('kc2', sl)
                              if blk in comp_blocks:
                                  gc = gcol(job, blk)
                                  hk = ('hT', gc)
                                  hcols = slice(gc * 128, (gc + 1) * 128)
                                  b = mmbank()
                                  for kc in range(8):
                                      T.op('pe', lambda q: q.matmul(ps[b][:, 0:256], lhsT=hT[:, kc, hcols], rhs=w_c[:, kc, 128:384], start=(kc == 0), stop=(kc == 7)),
                                           r=[hk, 'w_c'], w=[PK[b]])
                                  so = stg[sl]
                                  T.op('act', lambda q: q.copy(out=so[:, 0:256], in_=ps[b][:, 0:256]), r=[PK[b]], w=[('stg', sl)])
                                  T.op('dve', lambda q: q.tensor_copy(out=kc2[sl][:, :], in_=so[:, 0:128]), r=[('stg', sl)], w=[kkey])
                                  T.op('pool', lambda q: q.tensor_copy(out=cvt[:, blk, :], in_=so[:, 128:256]), r=[('stg', sl)], w=[('cvt', blk)])
                                  if not issamp:
                                      rows = slice(blk * 128, (blk + 1) * 128)
                                      T.dma('sp', o_ck_p[l, rows, hp * 128:(hp + 1) * 128], so[:, 0:128], r=[('stg', sl)])
                                      T.dma('sp', o_cv_p[l, rows, hp * 128:(hp + 1) * 128], so[:, 128:256], r=[('stg', sl)])
                                  else:
                                      s = job['s']
                                      T.dma('sp', o_ck_s[l, s, :, hp * 128:(hp + 1) * 128], so[0:16, 0:128], r=[('stg', sl)])
                                      T.dma('sp', o_cv_s[l, s, :, hp * 128:(hp + 1) * 128], so[0:16, 128:256], r=[('stg', sl)])
                              else:
                                  s = job['s']
                                  rows = slice(blk * 128, (blk + 1) * 128)
                                  T.dma('pool', kc2[sl][:, :], cck[l, s, rows, hp * 128:(hp + 1) * 128], w=[kkey])
                                  T.dma('pool', cvt[:, blk, :], ccv[l, s, rows, hp * 128:(hp + 1) * 128], w=[('cvt', blk)])
                              T.op('pe', lambda q: q.transpose(pst[:, 0:128], kc2[sl][:, :], ident), r=[kkey] + CB, w=[PT])
                              T.op('act', lambda q: q.copy(out=ckT[:, blk * 128:(blk + 1) * 128], in_=pst[:, 0:128]), r=[PT], w=[('ckT', blk)])
                          for qi, qb in enumerate(job['qblocks']):
                              gc = gcol(job, qb)
                              b = mmbank()
                              for kc in range(8):
                                  T.op('pe', lambda q: q.matmul(ps[b][:, 0:128], lhsT=w_c[:, kc, 0:128], rhs=hT[:, kc, gc * 128:(gc + 1) * 128], start=(kc == 0), stop=(kc == 7)),
                                       r=[('hT', gc), 'w_c'], w=[PK[b]])
                              T.op('act', lambda q: q.activation(out=cqT[:, qi * 128:(qi + 1) * 128], in_=ps[b][:, 0:128], func=AF.Copy, scale=0.125), r=[PK[b]], w=[('cqT', qi)])
                          qtiles = [job['qblocks'][i:i + 4] for i in range(0, len(job['qblocks']), 4)]
                          for ti, qt in enumerate(qtiles):
                              n = 128 * len(qt)
                              qc0 = ti * 512
                              qkeys = [('cqT', ti * 4 + i) for i in range(len(qt))]
                              nlb = qt[-1] + 1
                              osl = ti & 1
                              order = list(range(nlb - 1, -1, -1))

                              def stageA(lb, buf):
                                  diag = lb >= qt[0]
                                  for e2 in range(2):
                                      prow = slice(e2 * 64, e2 * 64 + 64)
                                      T.op('pe', lambda q: q.matmul(ps[e2][:, 0:n], lhsT=ckT[prow, lb * 128:(lb + 1) * 128], rhs=cqT[prow, qc0:qc0 + n], start=True, stop=(not diag)),
                                           r=[('ckT', lb)] + qkeys, w=[PK[e2]])
                                      if diag:
                                          o = lb - qt[0]
                                          T.op('pe', lambda q: q.matmul(ps[e2][:, 0:n], lhsT=ident, rhs=cb[:, K_DM + 512 * o:K_DM + 512 * o + n], start=False, stop=True),
                                               r=CB, w=[PK[e2]])
                                  for e2 in range(2):
                                      et = wk[2 * e2 + buf]; sp = wkb[2 * e2 + buf]
                                      T.op('act', lambda q: q.activation(out=et[:, 0:n], in_=ps[e2][:, 0:n], func=AF.Exp), r=[PK[e2]], w=[('wk', 2 * e2 + buf)])
                                      T.op('act', lambda q: q.activation(out=sp[:, 0:n], in_=et[:, 0:n], func=AF.Ln, bias=1.0, scale=1.0), r=[('wk', 2 * e2 + buf)], w=[('wkb', 2 * e2 + buf)])

                              def stageB(lb, buf, first):
                                  for e2 in range(2):
                                      sp = wkb[2 * e2 + buf]
                                      T.op('pe', lambda q: q.matmul(ps[2 + e2][:, 0:n], lhsT=negU, rhs=sp[:, 0:n], start=first, stop=True, skip_group_check=True), r=[('wkb', 2 * e2 + buf)] + CB, w=[PK[2 + e2]])
                                  for e2 in range(2):
                                      et = wk[2 * e2 + buf]
                                      T.op('act', lambda q: q.activation(out=Gt[e2][:, 0:n], in_=ps[2 + e2][:, 0:n], func=AF.Exp), r=[PK[2 + e2]], w=[('Gt', e2)])
                                      T.op('dve' if e2 == 0 else 'pool', lambda q: q.tensor_tensor(out=wvt[2 * e2 + buf][:, 0:n], in0=et[:, 0:n], in1=Gt[e2][:, 0:n], op=ALU.mult),
                                           r=[('wk', 2 * e2 + buf), ('Gt', e2)], w=[('wvt', 2 * e2 + buf)])
                                  for e2 in range(2):
                                      sp = wkb[2 * e2 + buf]
                                      T.op('pe', lambda q: q.matmul(ps[2 + e2][:, 0:n], lhsT=negL, rhs=sp[:, 0:n], start=False, stop=True, skip_group_check=True), r=[('wkb', 2 * e2 + buf)] + CB, w=[PK[2 + e2]])

                              def stagePV(lb, buf, first, last):
                                  for e2 in range(2):
                                      T.op('pe', lambda q: q.matmul(ps[4 + e2][:, 0:n], lhsT=cvt[:, lb, :], rhs=wvt[2 * e2 + buf][:, 0:n], start=first, stop=last),
                                           r=[('cvt', lb), ('wvt', 2 * e2 + buf)], w=[PK[4 + e2]])

                              no = len(order)
                              stageA(order[0], 0)
                              for i_, lb in enumerate(order):
                                  if i_ + 1 < no:
                                      stageA(order[i_ + 1], (i_ + 1) & 1)
                                  stageB(lb, i_ & 1, i_ == 0)
                                  if i_ >= 1:
                                      stagePV(order[i_ - 1], (i_ - 1) & 1, i_ == 1, False)
                              stagePV(order[no - 1], (no - 1) & 1, no == 1, True)
                              for e2 in range(2):
                                  prow = slice(e2 * 64, e2 * 64 + 64)
                                  T.op('act', lambda q: q.copy(out=ocT[osl][prow, 0:n], in_=ps[4 + e2][prow, 0:n]), r=[PK[4 + e2]], w=[('ocT', osl)])
                              for i, qb in enumerate(qt):
                                  gc = gcol(job, qb)
                                  T.dma('sp', oT_d[2, hp, :, gc * 128:(gc + 1) * 128], ocT[osl][:, i * 128:(i + 1) * 128], r=[('ocT', osl)], w=[('oTd', 2, gc)])
                      T.barrier()

              s12.__exit__(None, None, None); open_scopes.pop()
              CK('jobs')
              groups = [list(range(0, 8)), list(range(8, 16)), list(range(16, 24)), list(range(24, NBT))]
              with ExitStack() as s3:
                  def sb3(name, shape, dt=F32):
                      return s3.enter_context(nc.sbuf_tensor(U(name), list(shape), dt))
                  NG = 10
                  xg = sb3("xg", [128, NG, D]); hg = sb3("hg", [128, 8, NG * 128], BF16)
                  mT = sb3("mT", [128, 8, NG * 128], BF16)
                  gffn = sb3("gffn", [128, D]); gple = sb3("gple", [128, D])
                  bcast(gffn, 'gffn', norm_ffn[l:l + 1, :], D)
                  bcast(gple, 'gple', norm_ple[l:l + 1, :], D)
                  wg = sb3("wg", [128, 8, 1024], BF16); wb_ = sb3("wb_", [128, 4, 1024], BF16)
                  for grp in groups:
                      ng = len(grp)
                      ntok = ng * 128
                      tiles = [(c0, min(ntok, c0 + 512)) for c0 in range(0, ntok, 512)]
                      for i, gblk in enumerate(grp):
                          if gblk < NBP:
                              T.dma('sp', xg[:, i, :], (x_p if l == 0 else xs)[gblk * 128:(gblk + 1) * 128, :], w=[('xg', i)])
                          else:
                              s = gblk - NBP
                              T.op('pool', lambda q: q.memset(xg[:, i, :], 0.0), w=[('xg', i)])
                              T.dma('sp', xg[0:16, i, :], x_s[s] if l == 0 else xs[gblk * 128:gblk * 128 + 16, :], w=[('xg', i)])
                          sl = i & 1
                          rmsnorm_rows(xg[:, i, :], [('xg', i)], D, 128, gmix, 'gmix', xb[sl][:], ('xb', sl), 0)
                          transpose_to(hg[:, :, i * 128:(i + 1) * 128], ('hg', i), xb[sl], ('xb', sl), 8)
                      hkeys = [('hg', i) for i in range(ng)]
                      sM = ExitStack(); sM.__enter__(); open_scopes.append(sM)
                      og = sM.enter_context(nc.sbuf_tensor(U("og"), [128, 4, NG * 128], BF16))
                      macc = sM.enter_context(nc.sbuf_tensor(U("macc"), [128, 8, NG * 128], F32))
                      for br in range(3):
                          T.dma('pool', wg[:], wview(w_in, l, C_GL + br * 1024, C_GL + (br + 1) * 1024), w=['wg'])
                          if br == 1:
                              for g2 in range(2):
                                  T.dma('pool', wb_[g2 * 64:(g2 + 1) * 64, :, :], w_br[br][l, g2 * 256:(g2 + 1) * 256, :].rearrange("(c d) n -> d c n", d=64), w=['wb_'])
                          else:
                              T.dma('pool', wb_[:], w_br[br][l].rearrange("(kc p) n -> p kc n", p=128), w=['wb_'])
                          T.dma('sp', og[:, :, 0:ntok], oT_d[br, :, :, grp[0] * 128:grp[0] * 128 + ntok].rearrange("c p t -> p c t"),
                                r=[('oTd', br, g_) for g_ in grp], w=['og'])
                          for cc in range(8):
                              for (t0, t1) in tiles:
                                  n = t1 - t0
                                  b = mmbank()
                                  for kc in range(8):
                                      T.op('pe', lambda q: q.matmul(ps[b][:, 0:n], lhsT=wg[:, kc, cc * 128:(cc + 1) * 128], rhs=hg[:, kc, t0:t1], start=(kc == 0), stop=(kc == 7)),
                                           r=['wg'] + hkeys, w=[PK[b]])
                                  T.op('act', lambda q: q.activation(out=wk[b][:, 0:n], in_=ps[b][:, 0:n], func=AF.Sigmoid, bias=gb[:, br * 8 + cc:br * 8 + cc + 1], scale=1.0),
                                       r=[PK[b], 'gb'], w=[('wk', b)])
                                  b2 = 2 + b
                                  for kc in range(4):
                                      T.op('pe', lambda q: q.matmul(ps[b2][:, 0:n], lhsT=wb_[:, kc, cc * 128:(cc + 1) * 128], rhs=og[:, kc, t0:t1], start=(kc == 0), stop=(kc == 3)),
                                           r=['wb_', 'og'], w=[PK[b2]])
                                  mk = ('macc', cc, t0)
                                  if br == 0:
                                      T.op('dve', lambda q: q.tensor_tensor(out=macc[:, cc, t0:t1], in0=ps[b2][:, 0:n], in1=wk[b][:, 0:n], op=ALU.mult), r=[PK[b2], ('wk', b)], w=[mk])
                                  else:
                                      T.op('dve', lambda q: q.tensor_tensor(out=wk[2 + b][:, 0:n], in0=ps[b2][:, 0:n], in1=wk[b][:, 0:n], op=ALU.mult), r=[PK[b2], ('wk', b)], w=[('wk', 2 + b)])
                                      if br == 1:
                                          T.op('pool', lambda q: q.tensor_tensor(out=macc[:, cc, t0:t1], in0=macc[:, cc, t0:t1], in1=wk[2 + b][:, 0:n], op=ALU.add), r=[mk, ('wk', 2 + b)], w=[mk])
                                      else:
                                          T.op('pool', lambda q: q.tensor_tensor(out=mT[:, cc, t0:t1], in0=macc[:, cc, t0:t1], in1=wk[2 + b][:, 0:n], op=ALU.add), r=[mk, ('wk', 2 + b)], w=[('mT', cc, t0)])
                      mkeys = [('mT', cc, t0) for cc in range(8) for (t0, _) in tiles]
                      T.barrier()
                      sM.__exit__(None, None, None); open_scopes.pop()
                      sF = ExitStack(); sF.__enter__(); open_scopes.append(sF)
                      actT = sF.enter_context(nc.sbuf_tensor(U("actT"), [128, 22, NG * 128], BF16))

                      def tok_major_update(wsrc3, wkey_unused, lhs_buf, lhs_keys, nkc, post):
                          for i in range(ng):
                              for half in range(2):
                                  b = mmbank()
                                  for kc in range(nkc):
                                      T.op('pe', lambda q: q.matmul(ps[b][:, :], lhsT=lhs_buf[:, kc, i * 128:(i + 1) * 128], rhs=wsrc3[:, kc, half * 512:(half + 1) * 512],
                                                                    start=(kc == 0), stop=(kc == nkc - 1)), r=lhs_keys + [wkey_unused], w=[PK[b]])
                                  post(i, half, b)

                      T.dma('pool', wg[:], w_out[l].rearrange("(kc p) n -> p kc n", p=128), w=['wg'])

                      def post_add(i, half, b):
                          T.op('dve', lambda q: q.tensor_tensor(out=xg[:, i, half * 512:(half + 1) * 512], in0=xg[:, i, half * 512:(half + 1) * 512], in1=ps[b][:, :], op=ALU.add),
                               r=[PK[b], ('xg', i)], w=[('xg', i)])
                      tok_major_update(wg, 'wg', mT, mkeys, 8, post_add)
                      for i in range(ng):
                          sl = i & 1
                          rmsnorm_rows(xg[:, i, :], [('xg', i)], D, 128, gffn, 'gffn', xb[sl][:], ('xb', sl), 0)
                          transpose_to(hg[:, :, i * 128:(i + 1) * 128], ('hg', i), xb[sl], ('xb', sl), 8)
                      for s0 in range(0, DFF, 512):
                          s1 = min(DFF, s0 + 512)
                          nsl = s1 - s0
                          T.dma('pool', wg[:, :, 0:nsl], wview(w_ffn_in, l, s0, s1), w=['wg'])
                          T.dma('pool', wg[:, :, 512:512 + nsl], wview(w_ffn_in, l, DFF + s0, DFF + s1), w=['wg'])
                          for jj in range(nsl // 128):
                              j = s0 // 128 + jj
                              for (t0, t1) in tiles:
                                  n = t1 - t0
                                  b = mmbank(); b2 = 2 + b
                                  for kc in range(8):
                                      T.op('pe', lambda q: q.matmul(ps[b][:, 0:n], lhsT=wg[:, kc, jj * 128:(jj + 1) * 128], rhs=hg[:, kc, t0:t1], start=(kc == 0), stop=(kc == 7)),
                                           r=['wg'] + hkeys, w=[PK[b]])
                                  for kc in range(8):
                                      T.op('pe', lambda q: q.matmul(ps[b2][:, 0:n], lhsT=wg[:, kc, 512 + jj * 128:512 + (jj + 1) * 128], rhs=hg[:, kc, t0:t1], start=(kc == 0), stop=(kc == 7)),
                                           r=['wg'] + hkeys, w=[PK[b2]])
                                  T.op('act', lambda q: q.activation(out=wk[b][:, 0:n], in_=ps[b][:, 0:n], func=AF.Silu), r=[PK[b]], w=[('wk', b)])
                                  T.op('dve', lambda q: q.tensor_tensor(out=actT[:, j, t0:t1], in0=ps[b2][:, 0:n], in1=wk[b][:, 0:n], op=ALU.mult), r=[PK[b2], ('wk', b)], w=[('actT', j, t0)])
                      akeys = [('actT', j, t0) for j in range(22) for (t0, _) in tiles]
                      for i in range(ng):
                          pass
                      for half in range(2):
                          for k0 in range(0, 22, 8):
                              k1 = min(22, k0 + 8)
                              T.dma('pool', wg[:, 0:k1 - k0, 0:512], w_ffn_out[l, k0 * 128:k1 * 128, half * 512:(half + 1) * 512].rearrange("(kc p) n -> p kc n", p=128), w=['wg'])
                              for i in range(ng):
                                  b = mmbank()
                                  for kc in range(k0, k1):
                                      T.op('pe', lambda q: q.matmul(ps[b][:, :], lhsT=actT[:, kc, i * 128:(i + 1) * 128], rhs=wg[:, kc - k0, 0:512], start=(kc == k0), stop=(kc == k1 - 1)),
                                           r=akeys + ['wg'], w=[PK[b]])
                                  post_add(i, half, b)
                      T.barrier()
                      sF.__exit__(None, None, None); open_scopes.pop()
                      sP = ExitStack(); sP.__enter__(); open_scopes.append(sP)
                      pT = sP.enter_context(nc.sbuf_tensor(U("pT"), [128, 2, NG * 128], BF16))
                      pt32 = sP.enter_context(nc.sbuf_tensor(U("pt32"), [128, 256], F32))
                      ptb = sP.enter_context(nc.sbuf_tensor(U("ptb"), [128, 256], BF16))
                      for i, gblk in enumerate(grp):
                          sl = i & 1
                          rmsnorm_rows(xg[:, i, :], [('xg', i)], D, 128, gple, 'gple', xb[sl][:], ('xb', sl), 0)
                          transpose_to(hg[:, :, i * 128:(i + 1) * 128], ('hg', i), xb[sl], ('xb', sl), 8)
                          if gblk < NBP:
                              T.dma('sp', pt32[:, :], p_p[l, gblk * 128:(gblk + 1) * 128, :], w=['pt32'])
                          else:
                              T.op('pool', lambda q: q.memset(pt32[:, :], 0.0), w=['pt32'])
                              T.dma('sp', pt32[0:16, :], p_s[l, gblk - NBP], w=['pt32'])
                          T.op('dve', lambda q: q.tensor_copy(out=ptb[:, :], in_=pt32[:, :]), r=['pt32'], w=['ptb'])
                          transpose_to(pT[:, :, i * 128:(i + 1) * 128], ('pT', i), ptb, 'ptb', 2)
                      T.dma('pool', wg[:], w_ple_gate[l].rearrange("(kc p) n -> p kc n", p=128), w=['wg'])
                      T.dma('pool', wb_[:, 0:2, :], w_ple_proj[l].rearrange("(kc p) n -> p kc n", p=128), w=['wb_'])
                      for i in range(ng):
                          for half in range(2):
                              b = mmbank(); b2 = 2 + b
                              for kc in range(8):
                                  T.op('pe', lambda q: q.matmul(ps[b][:, :], lhsT=hg[:, kc, i * 128:(i + 1) * 128], rhs=wg[:, kc, half * 512:(half + 1) * 512], start=(kc == 0), stop=(kc == 7)),
                                       r=[('hg', i), 'wg'], w=[PK[b]])
                              for kc in range(2):
                                  T.op('pe', lambda q: q.matmul(ps[b2][:, :], lhsT=pT[:, kc, i * 128:(i + 1) * 128], rhs=wb_[:, kc, half * 512:(half + 1) * 512], start=(kc == 0), stop=(kc == 1)),
                                       r=[('pT', i), 'wb_'], w=[PK[b2]])
                              T.op('act', lambda q: q.activation(out=wk[b][:, :], in_=ps[b][:, :], func=AF.Sigmoid), r=[PK[b]], w=[('wk', b)])
                              T.op('dve', lambda q: q.tensor_tensor(out=wk[b][:, :], in0=ps[b2][:, :], in1=wk[b][:, :], op=ALU.mult), r=[PK[b2], ('wk', b)], w=[('wk', b)])
                              T.op('pool', lambda q: q.tensor_tensor(out=xg[:, i, half * 512:(half + 1) * 512], in0=xg[:, i, half * 512:(half + 1) * 512], in1=wk[b][:, :], op=ALU.add),
                                   r=[('wk', b), ('xg', i)], w=[('xg', i)])
                      T.barrier()
                      sP.__exit__(None, None, None); open_scopes.pop()
                      for i, gblk in enumerate(grp):
                          if l == 0:
                              T.dma('sp', xs[gblk * 128:(gblk + 1) * 128, :], xg[:, i, :], r=[('xg', i)], w=[('xs', gblk)])
                          elif gblk < NBP:
                              T.dma('sp', y_p[gblk * 128:(gblk + 1) * 128, :], xg[:, i, :], r=[('xg', i)])
                          else:
                              T.dma('sp', y_s[gblk - NBP], xg[0:16, i, :], r=[('xg', i)])
                  T.barrier()
              CK('L0')

        except _Stop:
            for sc in reversed(open_scopes):
                sc.__exit__(None, None, None)
        T.dead = False
        T.barrier()
        print("instructions:", T.n_inst, "sems:", len(T.sems))
    return nc


_NC_CACHE = {}


def _make_maps(inp):
    f = lambda a: np.ascontiguousarray(np.asarray(a, dtype=np.float32))
    cst, cst2 = _consts()
    shared = {
        'norm_mix': f(inp['norm_mix']), 'w_in': f(inp['w_in']),
        'gbT': f(np.asarray(inp['gate_bias']).reshape(2, 24, 128).transpose(0, 2, 1)),
        'a_vnorm': f(inp['a_vnorm']), 'a_wsT': f(np.asarray(inp['a_ws']).transpose(0, 3, 1, 2)),
        'a_bias': f(np.asarray(inp['a_bias']).reshape(2, 512)),
        'b_qnorm': f(inp['b_qnorm']), 'b_knorm': f(inp['b_knorm']),
        'w_br_a': f(inp['w_br_a']), 'w_br_b': f(inp['w_br_b']), 'w_br_c': f(inp['w_br_c']),
        'w_out': f(inp['w_out']), 'norm_ffn': f(inp['norm_ffn']), 'w_ffn_in': f(inp['w_ffn_in']), 'w_ffn_out': f(inp['w_ffn_out']),
        'norm_ple': f(inp['norm_ple']), 'w_ple_gate': f(inp['w_ple_gate']), 'w_ple_proj': f(inp['w_ple_proj']), 'cst': cst, 'cst2': cst2,
    }
    xp = np.asarray(inp['x_prompt']); xsm = np.asarray(inp['x_sample'])
    in_maps = []
    for c in range(8):
        b = c % 4
        ss = slice(2 * c, 2 * c + 2)
        m = dict(shared)
        m['x_p'] = f(xp[b]); m['x_s'] = f(xsm[ss])
        m['cbk'] = f(np.asarray(inp['cache_b_k'])[:, ss].reshape(2, 2, SEQ, 128))
        m['cbv'] = f(np.asarray(inp['cache_b_v'])[:, ss].reshape(2, 2, SEQ, 128))
        m['cbi'] = f(np.asarray(inp['cache_b_kidx'])[:, ss])
        m['cck'] = f(np.asarray(inp['cache_c_k'])[:, ss].reshape(2, 2, SEQ, 512))
        m['ccv'] = f(np.asarray(inp['cache_c_v'])[:, ss].reshape(2, 2, SEQ, 512))
        m['p_p'] = f(np.asarray(inp['p_prompt'])[:, b]); m['p_s'] = f(np.asarray(inp['p_sample'])[:, ss])
        in_maps.append(m)
    return in_maps


def kernel(**inp):
    if 'nc' not in _NC_CACHE:
        _NC_CACHE['nc'] = build_program()
    nc = _NC_CACHE['nc']
    in_maps = _make_maps(inp)
    res = run_bass_kernel_spmd(nc, in_maps, core_ids=list(range(8))).results
    st = lambda name, cores: np.stack([np.asarray(res[c][name]) for c in cores], axis=1)
    P = range(4); A = range(8)
    y_prompt = np.stack([res[c]['y_p'] for c in P], 0).astype(np.float32)
    y_sample = np.concatenate([res[c]['y_s'] for c in A], 0).astype(np.float32)
    cat_s = lambda name: np.concatenate([np.asarray(res[c][name]) for c in A], axis=1)
    outs = (
        y_prompt, y_sample,
        st('o_bk_p', P).reshape(2, 4, SEQ, 2, 64), st('o_bv_p', P).reshape(2, 4, SEQ, 2, 64), st('o_bi_p', P).reshape(2, 4, SEQ, 32),
        st('o_ck_p', P).reshape(2, 4, SEQ, 8, 64), st('o_cv_p', P).reshape(2, 4, SEQ, 8, 64),
        cat_s('o_bk_s').reshape(2, 16, 16, 2, 64), cat_s('o_bv_s').reshape(2, 16, 16, 2, 64), cat_s('o_bi_s').reshape(2, 16, 16, 32),
        cat_s('o_ck_s').reshape(2, 16, 16, 8, 64), cat_s('o_cv_s').reshape(2, 16, 16, 8, 64), cat_s('o_av_s').reshape(2, 16, 16, 512),
    )
    return tuple(np.ascontiguousarray(o, dtype=np.float32) for o in outs)
```

```python
import numpy as np
from contextlib import ExitStack
import concourse.bass as bass
import concourse.mybir as mybir
from concourse.bass_utils import run_bass_kernel_spmd

F32 = mybir.dt.float32
BF16 = mybir.dt.bfloat16
AF = mybir.ActivationFunctionType
ALU = mybir.AluOpType
AX = mybir.AxisListType

D = 1024
SEQ = 4096
NBP = 32
NBT = 34
NTOK = NBT * 128
DFF = 2816
EPS = 1e-6
NEG = -30000.0
BIG = 1.0e30
NIT = 24
C_AU, C_AV, C_BQ, C_BK, C_BV, C_IQ, C_IK, C_IW, C_CQ, C_CK, C_CV, C_GL = (
    0, 512, 1024, 1536, 1664, 1792, 2048, 2080, 2088, 2600, 3112, 3624)
K_ID, K_NU, K_NL, K_ONE, K_DM, K_GM, K_W2 = 0, 128, 256, 384, 512, 512 + 2048, 512 + 2048 + 128
K_TOT = K_W2 + 32
K2_ONE, K2_GM, K2_W2, K2_TOT = 0, 128, 256, 288


def _consts():
    c = np.zeros((128, K_TOT), np.float32)
    j = np.arange(128)[:, None]
    l = np.arange(128)[None, :]
    c[:, K_ID:K_ID + 128] = np.eye(128)
    c[:, K_NU:K_NU + 128] = -1.0 * (j >= l)
    c[:, K_NL:K_NL + 128] = -1.0 * (j < l)
    c[:, K_ONE:K_ONE + 128] = 1.0
    q = np.arange(512)[None, :]
    for o in range(4):
        c[:, K_DM + 512 * o:K_DM + 512 * (o + 1)] = np.where((128 * o + j) >= q, NEG, 0.0)
    c[:, K_GM:K_GM + 128] = ((j // 64) <= (l // 64))
    c[:, K_W2:K_W2 + NIT] = 2.0 ** (-(np.arange(NIT)[None, :] + 0.0))
    c2 = np.concatenate([c[:, K_ONE:K_ONE + 128], c[:, K_GM:K_GM + 128], c[:, K_W2:K_W2 + 32]], axis=1)
    return c, np.ascontiguousarray(c2)


class Trk:
    EP = 16000

    def __init__(self, nc, es):
        self.nc = nc
        self.q = {'pe': nc.tensor, 'act': nc.scalar, 'dve': nc.vector, 'pool': nc.gpsimd, 'sp': nc.sync}
        self.sems = []
        self.es = es
        self.cnt = {e: 0 for e in ('pe', 'act', 'dve', 'pool')}
        self.esem = {e: [] for e in self.cnt}
        self.seen = {e: {} for e in self.q}
        self.lastw = {}
        self.readers = {}
        self.ndma = 40
        self.dsem = [self._new(f"d{i}") for i in range(self.ndma)]
        self.dval = [0] * self.ndma
        self.drr = 0
        self.drr_sw = 0
        self.n_inst = 0
        self.dead = False

    def _new(self, name):
        s = self.es.enter_context(self.nc.semaphore(name))
        self.sems.append(s)
        return len(self.sems) - 1

    def _wait(self, e, ev):
        src, si, val = ev
        if src == 'pe' and e == 'pe':
            return
        if self.seen[e].get(si, 0) >= val:
            return
        self.q[e].wait_ge(self.sems[si], val)
        self.seen[e][si] = val

    def _deps(self, e, r, w):
        evs = []
        for k in r:
            if k in self.lastw:
                evs.append(self.lastw[k])
            if isinstance(k, tuple) and k[0] in ('ps', 'pst'):
                rd = self.readers.get(k)
                if rd:
                    evs.extend(ev for ev in rd.values() if ev[0] != e)
        for k in w:
            if k in self.lastw:
                evs.append(self.lastw[k])
            rd = self.readers.get(k)
            if rd:
                evs.extend(rd.values())
        for ev in evs:
            self._wait(e, ev)

    def _record(self, ev, r, w):
        for k in w:
            self.lastw[k] = ev
            self.readers[k] = {}
        for k in r:
            d = self.readers.setdefault(k, {})
            o = d.get(ev[1])
            if o is None or o[2] < ev[2]:
                d[ev[1]] = ev

    def op(self, e, fn, r=(), w=()):
        if self.dead:
            return None
        self._deps(e, r, w)
        inst = fn(self.q[e])
        n = self.cnt[e]
        ep, off = divmod(n, self.EP)
        if ep >= len(self.esem[e]):
            self.esem[e].append(self._new(f"{e}{ep}"))
        si = self.esem[e][ep]
        inst.then_inc(self.sems[si], 1)
        self.cnt[e] = n + 1
        self.n_inst += 1
        ev = (e, si, off + 1)
        if e != 'pe':
            self.seen[e][si] = max(self.seen[e].get(si, 0), 0)
        self._record(ev, r, w)
        return ev

    def dma(self, e, out, in_, r=(), w=()):
        if self.dead:
            return None
        half = self.ndma // 2
        if e == 'pool':
            i = self.drr_sw
            self.drr_sw = (self.drr_sw + 1) % half
        else:
            i = half + self.drr
            self.drr = (self.drr + 1) % half
        si = self.dsem[i]
        if self.dval[i] > 0:
            self._wait(e, ('dma', si, self.dval[i]))
        self._deps(e, r, w)
        inst = self.q[e].dma_start(out=out, in_=in_)
        self.dval[i] += 16
        inst.then_inc(self.sems[si], 16)
        self.n_inst += 1
        ev = ('dma', si, self.dval[i])
        self._record(ev, r, w)
        return ev

    def barrier(self):
        for e in self.q:
            for o in self.cnt:
                n = self.cnt[o]
                if n == 0 or o == e:
                    continue
                ep, off = divmod(n - 1, self.EP)
                self._wait(e, (o, self.esem[o][ep], off + 1))
            for i in range(self.ndma):
                if self.dval[i]:
                    self._wait(e, ('dma', self.dsem[i], self.dval[i]))
        for e in ('act', 'dve', 'pool'):
            n = self.cnt[e]
            if n:
                ep, off = divmod(n - 1, self.EP)
                self._wait(e, (e, self.esem[e][ep], off + 1))


class _Stop(Exception):
    pass


def build_program(stop=None):
    nc = bass.Bass("TRN2", target_bir_lowering=False)

    def din(name, shape):
        return nc.dram_tensor(name, list(shape), F32, kind="ExternalInput").ap()

    def dout(name, shape):
        return nc.dram_tensor(name, list(shape), F32, kind="ExternalOutput").ap()

    x_p = din("x_p", [SEQ, D]); x_s = din("x_s", [2, 16, D])
    cbk = din("cbk", [2, 2, SEQ, 128]); cbv = din("cbv", [2, 2, SEQ, 128]); cbi = din("cbi", [2, 2, SEQ, 32])
    cck = din("cck", [2, 2, SEQ, 512]); ccv = din("ccv", [2, 2, SEQ, 512])
    p_p = din("p_p", [2, SEQ, 256]); p_s = din("p_s", [2, 2, 16, 256])
    norm_mix = din("norm_mix", [2, D]); w_in = din("w_in", [2, D, 6696]); gbT = din("gbT", [2, 128, 24])
    a_vnorm = din("a_vnorm", [2, 512]); a_wsT = din("a_wsT", [2, 128, 4, 128]); a_bias = din("a_bias", [2, 512])
    b_qnorm = din("b_qnorm", [2, 64]); b_knorm = din("b_knorm", [2, 64])
    w_br = [din("w_br_a", [2, 512, D]), din("w_br_b", [2, 512, D]), din("w_br_c", [2, 512, D])]
    w_out = din("w_out", [2, D, D]); norm_ffn = din("norm_ffn", [2, D]); w_ffn_in = din("w_ffn_in", [2, D, 2 * DFF])
    w_ffn_out = din("w_ffn_out", [2, DFF, D]); norm_ple = din("norm_ple", [2, D]); w_ple_gate = din("w_ple_gate", [2, D, D])
    w_ple_proj = din("w_ple_proj", [2, 256, D]); cst = din("cst", [128, K_TOT]); cst2 = din("cst2", [128, K2_TOT])

    y_p = dout("y_p", [SEQ, D]); y_s = dout("y_s", [2, 16, D])
    o_bk_p = dout("o_bk_p", [2, SEQ, 128]); o_bv_p = dout("o_bv_p", [2, SEQ, 128]); o_bi_p = dout("o_bi_p", [2, SEQ, 32])
    o_ck_p = dout("o_ck_p", [2, SEQ, 512]); o_cv_p = dout("o_cv_p", [2, SEQ, 512])
    o_bk_s = dout("o_bk_s", [2, 2, 16, 128]); o_bv_s = dout("o_bv_s", [2, 2, 16, 128]); o_bi_s = dout("o_bi_s", [2, 2, 16, 32])
    o_ck_s = dout("o_ck_s", [2, 2, 16, 512]); o_cv_s = dout("o_cv_s", [2, 2, 16, 512]); o_av_s = dout("o_av_s", [2, 2, 16, 512])

    dbgk = "ExternalOutput" if stop is not None else "Internal"
    xs = nc.dram_tensor("xs", [NTOK, D], F32, kind=dbgk).ap()
    oT_d = nc.dram_tensor("oT_d", [3, 4, 128, NTOK], BF16, kind=dbgk).ap()

    es = ExitStack()
    with es:
        T = Trk(nc, es)
        uid = [0]

        def U(name):
            uid[0] += 1
            return f"{name}_{uid[0]}"

        def sb(name, shape, dt=F32):
            return es.enter_context(nc.sbuf_tensor(name, list(shape), dt))

        ps = [es.enter_context(nc.psum_tensor(f"ps{i}", [128, 512], F32)) for i in range(7)]
        pst = es.enter_context(nc.psum_tensor("pst", [128, 1024], BF16))
        PK = [('ps', i) for i in range(7)]
        PT = ('pst',)

        cf = sb("cf", [128, K2_TOT]); cb = sb("cb", [128, K_GM + 128], BF16)
        T.dma('sp', cf[:], cst2[:, :], w=['cf'])
        T.dma('pool', cb[:], cst[:, 0:K_GM + 128], w=['cb'])
        ident = cb[:, K_ID:K_ID + 128]; negU = cb[:, K_NU:K_NU + 128]; negL = cb[:, K_NL:K_NL + 128]
        onesb = cb[:, K_ONE:K_ONE + 128]
        ones1 = cf[0:1, K2_ONE:K2_ONE + 128]
        CB = ['cb']

        gmix = sb("gmix", [128, D]); gq8 = sb("gq8", [128, 64]); gk = sb("gk", [128, 64])
        gb = sb("gb", [128, 24])
        rowt = sb("rowt", [1, 512]); shiftc = sb("shiftc", [128, 4])
        xt = [None, None]
        xb = [sb(f"xbh{i}", [128, D], BF16) for i in range(2)]
        col = sb("col", [128, 64])
        stg = [sb(f"stg{i}", [128, 512]) for i in range(2)]
        wk = [sb(f"wk{i}", [128, 512]) for i in range(4)]
        wkb = [sb(f"wkb{i}", [128, 512], BF16) for i in range(4)]
        mmrr = [0]

        def mmbank():
            mmrr[0] ^= 1
            return mmrr[0]

        def bcast(dst, key, row_ap, n, scale=None):
            for c0 in range(0, n, 512):
                c1 = min(n, c0 + 512)
                T.dma('sp', rowt[0:1, 0:c1 - c0], row_ap[:, c0:c1], w=['rowt'])
                b = mmbank()
                T.op('pe', lambda q: q.matmul(ps[b][:, 0:c1 - c0], lhsT=ones1, rhs=rowt[0:1, 0:c1 - c0], start=True, stop=True),
                     r=['rowt', 'cf'], w=[PK[b]])
                if scale is None:
                    T.op('dve', lambda q: q.tensor_copy(out=dst[:, c0:c1], in_=ps[b][:, 0:c1 - c0]), r=[PK[b]], w=[key])
                else:
                    T.op('dve', lambda q: q.tensor_scalar(out=dst[:, c0:c1], in0=ps[b][:, 0:c1 - c0], scalar1=scale, scalar2=None, op0=ALU.mult),
                         r=[PK[b]], w=[key])

        wslot = [0]

        def load_w(src3, ncols, key=None):
            i = wslot[0]; wslot[0] ^= 1
            kc = src3.shape[1]
            T.dma('pool', wbuf[i][:, 0:kc, 0:ncols], src3, w=[('wbuf', i)])
            return wbuf[i], ('wbuf', i)

        def wview(wap, l, c0, c1):
            return wap[l].rearrange("(kc p) n -> p kc n", p=128)[:, :, c0:c1]

        def rmsnorm_rows(src_ap, src_keys, n, npart, gtile, gkey, dst_bf, dst_key, cidx):
            T.op('act', lambda q: q.activation(out=dst_bf, in_=src_ap, func=AF.Square,
                                               accum_out=col[0:npart, cidx:cidx + 1]), r=src_keys, w=[dst_key, ('col', cidx)])
            T.op('act', lambda q: q.activation(out=col[0:npart, cidx:cidx + 1], in_=col[0:npart, cidx:cidx + 1], func=AF.Sqrt, bias=EPS, scale=1.0 / n), r=[('col', cidx)], w=[('col', cidx)])
            T.op('dve', lambda q: q.reciprocal(out=col[0:npart, cidx:cidx + 1], in_=col[0:npart, cidx:cidx + 1]), r=[('col', cidx)], w=[('col', cidx)])
            T.op('dve', lambda q: q.scalar_tensor_tensor(out=dst_bf, in0=src_ap, scalar=col[0:npart, cidx:cidx + 1], in1=gtile[0:npart, 0:n],
                                                         op0=ALU.mult, op1=ALU.mult), r=list(src_keys) + [('col', cidx), gkey], w=[dst_key])

        def transpose_to(dst3, dst_key, src_bf, src_key, nchunks, npart=128):
            for c in range(nchunks):
                T.op('pe', lambda q: q.transpose(pst[:, c * 128:c * 128 + npart], src_bf[0:npart, c * 128:(c + 1) * 128], ident[0:npart, 0:npart]),
                     r=[src_key] + CB, w=[PT])
            T.op('act', lambda q: q.copy(out=dst3, in_=pst[:, 0:nchunks * 128].rearrange("p (c t) -> p c t", c=nchunks)[:, :, 0:npart]),
                 r=[PT], w=[dst_key])

        def gelu_to(dst, dst_key, src_ap, src_keys, npart, n, tmp_i):
            a = wk[tmp_i][0:npart, 0:n]; b = wk[tmp_i + 1][0:npart, 0:n]
            ka, kb = ('wk', tmp_i), ('wk', tmp_i + 1)
            T.op('act', lambda q: q.activation(out=a, in_=src_ap, func=AF.Square), r=src_keys, w=[ka])
            T.op('dve', lambda q: q.tensor_scalar(out=a, in0=a, scalar1=0.044715, scalar2=1.0, op0=ALU.mult, op1=ALU.add), r=[ka], w=[ka])
            T.op('dve', lambda q: q.tensor_tensor(out=a, in0=a, in1=src_ap, op=ALU.mult), r=[ka] + list(src_keys), w=[ka])
            T.op('act', lambda q: q.activation(out=b, in_=a, func=AF.Sigmoid, scale=1.5957691216057308), r=[ka], w=[kb])
            T.op('dve', lambda q: q.tensor_tensor(out=dst, in0=b, in1=src_ap, op=ALU.mult), r=[kb] + list(src_keys), w=[dst_key])

        def x_block_load(l, gblk, slot):
            key = ('xt', slot)
            if gblk < NBP:
                src = (x_p if l == 0 else xs)[gblk * 128:(gblk + 1) * 128, :]
                T.dma('sp', xt[slot][:], src, w=[key])
            else:
                s = gblk - NBP
                T.op('pool', lambda q: q.memset(xt[slot][:], 0.0), w=[key])
                src = x_s[s] if l == 0 else xs[gblk * 128:gblk * 128 + 16, :]
                T.dma('sp', xt[slot][0:16, :], src, w=[key])
            return key

        def norm_block_to_hT(l, gblk, slot, gt, gkey, hdst, hkey):
            key = ('xt', slot)
            rmsnorm_rows(xt[slot][:], [key], D, 128, gt, gkey, xb[slot][:], ('xb', slot), 0)
            transpose_to(hdst, hkey, xb[slot], ('xb', slot), 8)

        open_scopes = []

        def CK(name):
            if stop == name:
                T.dead = True

        try:
          for l in range(2):
              bcast(gmix, 'gmix', norm_mix[l:l + 1, :], D)
              bcast(gq8, 'gq8', b_qnorm[l:l + 1, :], 64, scale=0.125)
              bcast(gk, 'gk', b_knorm[l:l + 1, :], 64)
              T.dma('sp', gb[:], gbT[l], w=['gb'])
              T.op('dve', lambda q: q.tensor_reduce(out=shiftc[:, 0:1], in_=gq8[:], axis=AX.X, op=ALU.max, apply_absolute_value=True), r=['gq8'], w=['shiftc'])
              T.op('dve', lambda q: q.tensor_reduce(out=shiftc[:, 1:2], in_=gk[:], axis=AX.X, op=ALU.max, apply_absolute_value=True), r=['gk', 'shiftc'], w=['shiftc'])
              T.op('dve', lambda q: q.tensor_scalar(out=shiftc[:, 2:3], in0=shiftc[:, 0:1], scalar1=shiftc[:, 1:2], scalar2=-64.0, op0=ALU.mult, op1=ALU.mult),
                   r=['shiftc'], w=['shiftc2'])
              nshift = shiftc[:, 2:3]

              s12 = ExitStack()
              s12.__enter__(); open_scopes.append(s12)
              hT = s12.enter_context(nc.sbuf_tensor(U("hT"), [128, 8, NTOK], BF16))
              for i_x in range(2):
                  xt[i_x] = s12.enter_context(nc.sbuf_tensor(U("xt"), [128, D], F32))
              for gblk in range(NBT):
                  slot = gblk & 1
                  x_block_load(l, gblk, slot)
                  norm_block_to_hT(l, gblk, slot, gmix, 'gmix', hT[:, :, gblk * 128:(gblk + 1) * 128], ('hT', gblk))

              CK('p1')
              with ExitStack() as sa:
                  def sba(name, shape, dt=F32):
                      return sa.enter_context(nc.sbuf_tensor(U(name), list(shape), dt))
                  w_au = sba("w_au", [128, 8, 512], BF16); w_av = sba("w_av", [128, 8, 512], BF16)
                  avn = sba("avn", [128, 512]); abias = sba("abias", [128, 512])
                  wsT = sba("wsT", [128, 4, 128], BF16); wsTf = sba("wsTf", [128, 4, 128])
                  bcast(avn, 'avn', a_vnorm[l:l + 1, :], 512)
                  bcast(abias, 'abias', a_bias[l:l + 1, :], 512)
                  T.dma('sp', wsTf[:], a_wsT[l], w=['wsTf'])
                  for g in range(4):
                      T.op('dve', lambda q: q.tensor_tensor(out=wsT[:, g, :], in0=wsTf[:, g, :], in1=cf[:, K2_GM:K2_GM + 128], op=ALU.mult),
                           r=['wsTf', 'cf'], w=['wsT'])
                  oaT = [sba(f"oaT{i}", [128, 4, 128], BF16) for i in range(2)]
                  vtm = [sba(f"vtm{i}", [128, 512], BF16) for i in range(2)]
                  uT = [sba(f"uT{i}", [128, 4, 128]) for i in range(2)]
                  T.dma('pool', w_au[:], wview(w_in, l, C_AU, C_AU + 512), w=['w_au'])
                  T.dma('pool', w_av[:], wview(w_in, l, C_AV, C_AV + 512), w=['w_av'])
                  for gblk in range(NBT):
                      sl = gblk & 1
                      hk = ('hT', gblk)
                      hcols = slice(gblk * 128, (gblk + 1) * 128)
                      b = mmbank()
                      for kc in range(8):
                          T.op('pe', lambda q: q.matmul(ps[b][:, :], lhsT=hT[:, kc, hcols], rhs=w_av[:, kc, :], start=(kc == 0), stop=(kc == 7)),
                               r=[hk, 'w_av'], w=[PK[b]])
                      gelu_to(wk[2][:, :], ('wk', 2), ps[b][:, :], [PK[b]], 128, 512, 0)
                      rmsnorm_rows(wk[2][:, :], [('wk', 2)], 512, 128, avn, 'avn', vtm[sl][:], ('vtm', sl), 1)
                      if gblk >= NBP:
                          s = gblk - NBP
                          T.op('dve', lambda q: q.scalar_tensor_tensor(out=stg[0][0:16, 0:512], in0=wk[2][0:16, :], scalar=col[0:16, 1:2], in1=avn[0:16, :],
                                                                       op0=ALU.mult, op1=ALU.mult), r=[('wk', 2), ('col', 1), 'avn'], w=[('stg', 0)])
                          T.dma('sp', o_av_s[l, s], stg[0][0:16, 0:512], r=[('stg', 0)])
                      b2 = mmbank()
                      for g in range(4):
                          for kc in range(8):
                              T.op('pe', lambda q: q.matmul(ps[b2][:, g * 128:(g + 1) * 128], lhsT=w_au[:, kc, g * 128:(g + 1) * 128], rhs=hT[:, kc, hcols],
                                                            start=(kc == 0), stop=(kc == 7)), r=[hk, 'w_au'], w=[PK[b2]])
                      gelu_to(uT[sl][:].rearrange("p g t -> p (g t)"), ('uT', sl), ps[b2][:, :], [PK[b2]], 128, 512, 0)
                      b3 = 2
                      for g in range(4):
                          T.op('pe', lambda q: q.matmul(ps[b3][:, g * 128:(g + 1) * 128], lhsT=vtm[sl][:, g * 128:(g + 1) * 128], rhs=wsT[:, g, :],
                                                        start=True, stop=True), r=[('vtm', sl), 'wsT'], w=[PK[b3]])
                      T.op('dve', lambda q: q.tensor_tensor(out=wk[2][:, :], in0=ps[b3][:, :], in1=abias[:, :], op=ALU.add), r=[PK[b3], 'abias'], w=[('wk', 2)])
                      T.op('dve', lambda q: q.tensor_tensor(out=oaT[sl][:].rearrange("p g t -> p (g t)"), in0=wk[2][:, :],
                                                            in1=uT[sl][:].rearrange("p g t -> p (g t)"), op=ALU.mult), r=[('wk', 2), ('uT', sl)], w=[('oaT', sl)])
                      T.dma('sp', oT_d[0, :, :, gblk * 128:(gblk + 1) * 128].rearrange("c p t -> p c t"), oaT[sl][:], r=[('oaT', sl)], w=[('oTd', 0, gblk)])
                  T.barrier()

              CK('pa')
              jobs = [dict(kind='p', nb=NBP, qblocks=list(range(NBP)), gbase=0)]
              for s in range(2):
                  jobs.append(dict(kind='s', s=s, nb=NBP + 1, qblocks=[NBP], gbase=None))

              def gcol(job, blk):
                  if job['kind'] == 'p':
                      return blk
                  assert blk == NBP
                  return NBP + job['s']

              for job in jobs:
                  nb = job['nb']
                  L = nb * 128
                  issamp = job['kind'] == 's'
                  comp_blocks = list(range(NBP)) if not issamp else [NBP]
                  with ExitStack() as sB:
                      def sbb(name, shape, dt=F32):
                          return sB.enter_context(nc.sbuf_tensor(U(name), list(shape), dt))
                      bkT = sbb("bkT", [128, L], BF16); bv2 = sbb("bv2", [128, nb, 128], BF16); ikT = sbb("ikT", [32, L], BF16)
                      w_k = sbb("w_k", [128, 8, 288], BF16); w_q = sbb("w_q", [128, 8, 512], BF16); w_i = sbb("w_i", [128, 8, 264], BF16)
                      scores = sbb("scores", [128, L]); maskb = sbb("maskb", [128, L], BF16); mneg2 = [sbb(f"mnegT{i}", [128, nb, 128], BF16) for i in range(2)] if not issamp else [sbb("mnegT0", [128, nb, 128], BF16)] * 2
                      kb = [sbb(f"kb{i}", [128, 288], BF16) for i in range(2)]
                      bqn = sbb("bqn", [128, 512], BF16); bqT2 = [sbb(f"bqT{i}", [128, 4, 128], BF16) for i in range(2)]
                      iqb = sbb("iqb", [128, 256], BF16); iqT = sbb("iqT", [32, 8, 128], BF16); wq = sbb("wq", [128, 8])
                      bis = sbb("bis", [128, 32]); obT = sbb("obT", [128, 4, 128], BF16); oraw = [sbb("oraw0", [128, 512])] * 2; draw = [sbb("draw0", [128, 512])] * 2
                      pB = [sbb(f"pB{i}", [128, 512], BF16) for i in range(2)]
                      T.dma('pool', w_k[:, :, 0:256], wview(w_in, l, C_BK, C_BK + 256), w=['w_k'])
                      T.dma('pool', w_k[:, :, 256:288], wview(w_in, l, C_IK, C_IK + 32), w=['w_k'])
                      T.dma('pool', w_q[:], wview(w_in, l, C_BQ, C_BQ + 512), w=['w_q'])
                      T.dma('pool', w_i[:, :, 0:256], wview(w_in, l, C_IQ, C_IQ + 256), w=['w_i'])
                      T.dma('pool', w_i[:, :, 256:264], wview(w_in, l, C_IW, C_IW + 8), w=['w_i'])

                      if issamp:
                          s = job['s']
                          ks8 = [sbb(f"ks8{i}", [128, 8, 160], BF16) for i in range(2)]
                          for gi in range(NBP // 8):
                              b0 = gi * 8
                              st = ks8[gi & 1]
                              rws = slice(b0 * 128, (b0 + 8) * 128)
                              T.dma('pool', st[:, :, 0:128], cbk[l, s, rws, :].rearrange("(b p) c -> p b c", p=128), w=[('ks8k', gi & 1)])
                              T.dma('pool', st[:, :, 128:160], cbi[l, s, rws, :].rearrange("(b p) c -> p b c", p=128), w=[('ks8i', gi & 1)])
                              T.dma('pool', bv2[:, b0:b0 + 8, :], cbv[l, s, rws, :].rearrange("(b p) c -> p b c", p=128), w=[('bv2', b_) for b_ in range(b0, b0 + 8)])
                              for j in range(8):
                                  blk = b0 + j
                                  T.op('pe', lambda q: q.transpose(pst[:, 0:128], st[:, j, 0:128], ident), r=[('ks8k', gi & 1)] + CB, w=[PT])
                                  T.op('pe', lambda q: q.transpose(pst[0:32, 128:256], st[:, j, 128:160], ident), r=[('ks8i', gi & 1)] + CB, w=[PT])
                                  T.op('act', lambda q: q.copy(out=bkT[:, blk * 128:(blk + 1) * 128], in_=pst[:, 0:128]), r=[PT], w=[('bkT', blk)])
                                  T.op('act', lambda q: q.copy(out=ikT[0:32, blk * 128:(blk + 1) * 128], in_=pst[0:32, 128:256]), r=[PT], w=[('ikT', blk)])
                      for blk in range(nb):
                          if issamp and blk < NBP:
                              continue
                          sl = blk & 1
                          kkey = ('kb', sl)
                          if blk in comp_blocks:
                              gc = gcol(job, blk)
                              hk = ('hT', gc)
                              hcols = slice(gc * 128, (gc + 1) * 128)
                              b = mmbank()
                              for kc in range(8):
                                  T.op('pe', lambda q: q.matmul(ps[b][:, 0:288], lhsT=hT[:, kc, hcols], rhs=w_k[:, kc, :], start=(kc == 0), stop=(kc == 7)),
                                       r=[hk, 'w_k'], w=[PK[b]])
                              T.op('act', lambda q: q.activation(out=wk[0][:, 0:128], in_=ps[b][:, 0:128], func=AF.Square), r=[PK[b]], w=[('wk', 0)])
                              T.op('dve', lambda q: q.tensor_reduce(out=col[:, 8:10], in_=wk[0][:, 0:128].rearrange("p (h d) -> p h d", h=2), axis=AX.X, op=ALU.add),
                                   r=[('wk', 0)], w=[('col', 8)])
                              T.op('act', lambda q: q.activation(out=col[:, 8:10], in_=col[:, 8:10], func=AF.Sqrt, bias=EPS, scale=1.0 / 64), r=[('col', 8)], w=[('col', 8)])
                              T.op('dve', lambda q: q.reciprocal(out=col[:, 8:10], in_=col[:, 8:10]), r=[('col', 8)], w=[('col', 8)])
                              so = stg[sl]
                              for h in range(2):
                                  T.op('dve', lambda q: q.scalar_tensor_tensor(out=so[:, h * 64:(h + 1) * 64], in0=ps[b][:, h * 64:(h + 1) * 64], scalar=col[:, 8 + h:9 + h],
                                                                               in1=gk[:, :], op0=ALU.mult, op1=ALU.mult), r=[PK[b], ('col', 8), 'gk'], w=[('stg', sl)])
                              T.op('act', lambda q: q.copy(out=so[:, 128:288], in_=ps[b][:, 128:288]), r=[PK[b]], w=[('stg', sl)])
                              T.op('dve', lambda q: q.tensor_copy(out=kb[sl][:, :], in_=so[:, 0:288]), r=[('stg', sl)], w=[kkey])
                              if not issamp:
                                  rows = slice(blk * 128, (blk + 1) * 128)
                                  T.dma('sp', o_bk_p[l, rows, :], so[:, 0:128], r=[('stg', sl)])
                                  T.dma('sp', o_bv_p[l, rows, :], so[:, 128:256], r=[('stg', sl)])
                                  T.dma('sp', o_bi_p[l, rows, :], so[:, 256:288], r=[('stg', sl)])
                              else:
                                  s = job['s']
                                  T.dma('sp', o_bk_s[l, s], so[0:16, 0:128], r=[('stg', sl)])
                                  T.dma('sp', o_bv_s[l, s], so[0:16, 128:256], r=[('stg', sl)])
                                  T.dma('sp', o_bi_s[l, s], so[0:16, 256:288], r=[('stg', sl)])
                          else:
                              s = job['s']
                              rows = slice(blk * 128, (blk + 1) * 128)
                              T.dma('pool', kb[sl][:, 0:128], cbk[l, s, rows, :], w=[kkey])
                              T.dma('pool', kb[sl][:, 128:256], cbv[l, s, rows, :], w=[kkey])
                              T.dma('pool', kb[sl][:, 256:288], cbi[l, s, rows, :], w=[kkey])
                          T.op('pe', lambda q: q.transpose(pst[:, 0:128], kb[sl][:, 0:128], ident), r=[kkey] + CB, w=[PT])
                          T.op('pe', lambda q: q.transpose(pst[0:32, 128:256], kb[sl][:, 256:288], ident), r=[kkey] + CB, w=[PT])
                          T.op('act', lambda q: q.copy(out=bkT[:, blk * 128:(blk + 1) * 128], in_=pst[:, 0:128]), r=[PT], w=[('bkT', blk)])
                          T.op('act', lambda q: q.copy(out=ikT[0:32, blk * 128:(blk + 1) * 128], in_=pst[0:32, 128:256]), r=[PT], w=[('ikT', blk)])
                          T.op('pool', lambda q: q.tensor_copy(out=bv2[:, blk, :], in_=kb[sl][:, 128:256]), r=[kkey], w=[('bv2', blk)])

                      CK('bk')
                      scrr = [0]

                      def stageX(qb, slot):
                              gc = gcol(job, qb)
                              hk = ('hT', gc)
                              hcols = slice(gc * 128, (gc + 1) * 128)
                              Lq = (qb + 1) * 128
                              nlb = qb + 1
                              bq_ = mmbank()
                              for kc in range(8):
                                  T.op('pe', lambda q: q.matmul(ps[bq_][:, :], lhsT=hT[:, kc, hcols], rhs=w_q[:, kc, :], start=(kc == 0), stop=(kc == 7)),
                                       r=[hk, 'w_q'], w=[PK[bq_]])
                              T.op('act', lambda q: q.activation(out=wk[0][:, :], in_=ps[bq_][:, :], func=AF.Square), r=[PK[bq_]], w=[('wk', 0)])
                              bi_ = mmbank()
                              for kc in range(8):
                                  T.op('pe', lambda q: q.matmul(ps[bi_][:, 0:264], lhsT=hT[:, kc, hcols], rhs=w_i[:, kc, :], start=(kc == 0), stop=(kc == 7)),
                                       r=[hk, 'w_i'], w=[PK[bi_]])
                              T.op('act', lambda q: q.copy(out=iqb[:, :], in_=ps[bi_][:, 0:256]), r=[PK[bi_]], w=['iqb'])
                              yield
                              T.op('dve', lambda q: q.tensor_reduce(out=col[:, 16:24], in_=wk[0][:, :].rearrange("p (h d) -> p h d", h=8), axis=AX.X, op=ALU.add),
                                   r=[('wk', 0)], w=[('col', 16)])
                              T.op('act', lambda q: q.activation(out=col[:, 16:24], in_=col[:, 16:24], func=AF.Sqrt, bias=EPS, scale=1.0 / 64), r=[('col', 16)], w=[('col', 16)])
                              T.op('dve', lambda q: q.reciprocal(out=col[:, 16:24], in_=col[:, 16:24]), r=[('col', 16)], w=[('col', 16)])
                              for h in range(8):
                                  T.op('dve', lambda q: q.scalar_tensor_tensor(out=bqn[:, (h % 4) * 128 + (h // 4) * 64:(h % 4) * 128 + (h // 4) * 64 + 64], in0=ps[bq_][:, h * 64:(h + 1) * 64], scalar=col[:, 16 + h:17 + h],
                                                                               in1=gq8[:, :], op0=ALU.mult, op1=ALU.mult), r=[PK[bq_], ('col', 16), 'gq8'], w=['bqn'])
                              transpose_to(bqT2[slot][:], ('bqT', slot), bqn, 'bqn', 4)
                              T.op('dve', lambda q: q.tensor_scalar(out=wq[:, :], in0=ps[bi_][:, 256:264], scalar1=(8.0 ** -0.5) * (32.0 ** -0.5), scalar2=None, op0=ALU.mult),
                                   r=[PK[bi_]], w=['wq'])
                              for h in range(8):
                                  T.op('pe', lambda q: q.transpose(pst[0:32, h * 128:(h + 1) * 128], iqb[:, h * 32:(h + 1) * 32], ident), r=['iqb'] + CB, w=[PT])
                              T.op('act', lambda q: q.copy(out=iqT[:], in_=pst[0:32, :].rearrange("p (h t) -> p h t", h=8)), r=[PT], w=['iqT'])
                              yield
                              for c0 in range(0, Lq, 512):
                                  c1 = min(Lq, c0 + 512)
                                  n = c1 - c0
                                  kdeps = [('ikT', bb) for bb in range(c0 // 128, c1 // 128)]
                                  seng = 'dve'
                                  for h in range(8):
                                      b = mmbank()
                                      T.op('pe', lambda q: q.matmul(ps[b][:, 0:n], lhsT=iqT[:, h, :], rhs=ikT[0:32, c0:c1], start=True, stop=True),
                                           r=['iqT'] + kdeps, w=[PK[b]])
                                      ws = scrr[0]; scrr[0] = (scrr[0] + 1) % 4
                                      T.op('act', lambda q: q.activation(out=wk[ws][:, 0:n], in_=ps[b][:, 0:n], func=AF.Relu), r=[PK[b]], w=[('wk', ws)])
                                      if h == 0:
                                          T.op(seng, lambda q: q.tensor_scalar(out=scores[:, c0:c1], in0=wk[ws][:, 0:n], scalar1=wq[:, 0:1], scalar2=None, op0=ALU.mult),
                                               r=[('wk', ws), 'wq'], w=[('sc', c0)])
                                      elif seng == 'dve':
                                          T.op('dve', lambda q: q.scalar_tensor_tensor(out=scores[:, c0:c1], in0=wk[ws][:, 0:n], scalar=wq[:, h:h + 1], in1=scores[:, c0:c1],
                                                                                       op0=ALU.mult, op1=ALU.add), r=[('wk', ws), 'wq', ('sc', c0)], w=[('sc', c0)])
                                      else:
                                          T.op('pool', lambda q: q.tensor_scalar(out=wk[ws][:, 0:n], in0=wk[ws][:, 0:n], scalar1=wq[:, h:h + 1], scalar2=None, op0=ALU.mult),
                                               r=[('wk', ws), 'wq'], w=[('wk', ws)])
                                          T.op('pool', lambda q: q.tensor_tensor(out=scores[:, c0:c1], in0=scores[:, c0:c1], in1=wk[ws][:, 0:n], op=ALU.add),
                                               r=[('wk', ws), ('sc', c0)], w=[('sc', c0)])
                              sck = [('sc', c0) for c0 in range(0, Lq, 512)]
                              T.op('dve', lambda q: q.tensor_reduce(out=bis[:, 0:1], in_=scores[:, 0:Lq], axis=AX.X, op=ALU.max, apply_absolute_value=True), r=sck, w=['bis'])
                              if not issamp:
                                  T.op('pool', lambda q: q.memset(scores[0:64, Lq - 64:Lq], -BIG), r=['bis'], w=sck)
                              else:
                                  T.op('pool', lambda q: q.memset(scores[:, SEQ + 16:Lq], -BIG), r=['bis'], w=sck)
                              if Lq > 256:
                                  T.op('dve', lambda q: q.tensor_scalar(out=bis[:, 0:1], in0=bis[:, 0:1], scalar1=1.0, scalar2=None, op0=ALU.add), r=['bis'], w=['bis'])
                                  T.op('dve', lambda q: q.tensor_scalar(out=bis[:, 2:3], in0=bis[:, 0:1], scalar1=0.0, scalar2=None, op0=ALU.mult), r=['bis'], w=['bis'])
                                  T.op('dve', lambda q: q.tensor_scalar(out=bis[:, 4:4 + NIT], in0=cf[:, K2_W2:K2_W2 + NIT], scalar1=bis[:, 0:1], scalar2=None, op0=ALU.mult),
                                       r=['bis', 'cf'], w=['bis'])
                                  for it in range(NIT):
                                      T.op('dve', lambda q: q.tensor_scalar(out=maskb[:, 0:Lq], in0=scores[:, 0:Lq], scalar1=bis[:, 2:3], scalar2=0.0, op0=ALU.is_ge, op1=ALU.add,
                                                                            accum_out=bis[:, 3:4]), r=['bis'] + sck, w=['bis', 'maskb'])
                                      T.op('dve', lambda q: q.tensor_scalar(out=bis[:, 3:4], in0=bis[:, 3:4], scalar1=255.5, scalar2=bis[:, 4 + it:5 + it], op0=ALU.is_ge, op1=ALU.mult),
                                           r=['bis'], w=['bis'])
                                      nx = min(it + 1, NIT - 1)
                                      oc_ = 1 if it == NIT - 1 else 2
                                      T.op('dve', lambda q: q.scalar_tensor_tensor(out=bis[:, oc_:oc_ + 1], in0=bis[:, 2:3], scalar=bis[:, 4 + nx:5 + nx], in1=bis[:, 3:4],
                                                                                   op0=ALU.subtract, op1=ALU.add), r=['bis'], w=['bis'])
                                  thr = bis[:, 1:2]
                                  T.op('dve', lambda q: q.tensor_scalar(out=maskb[:, 0:Lq], in0=scores[:, 0:Lq], scalar1=thr, scalar2=None, op0=ALU.is_ge), r=['bis'] + sck, w=['maskb'])
                              else:
                                  T.op('dve', lambda q: q.tensor_scalar(out=maskb[:, 0:Lq], in0=scores[:, 0:Lq], scalar1=-1.0e29, scalar2=None, op0=ALU.is_ge), r=sck, w=['maskb'])
                              CK('topk')
                              yield
                              for lb0 in range(0, nlb, 8):
                                  lb1 = min(nlb, lb0 + 8)
                                  for lb in range(lb0, lb1):
                                      T.op('pe', lambda q: q.transpose(pst[:, (lb - lb0) * 128:(lb - lb0 + 1) * 128], maskb[:, lb * 128:(lb + 1) * 128], ident),
                                           r=['maskb'] + CB, w=[PT])
                                  T.op('dve', lambda q: q.tensor_scalar(out=mneg2[slot][:, lb0:lb1, :], in0=pst[:, 0:(lb1 - lb0) * 128].rearrange("p (c t) -> p c t", c=lb1 - lb0),
                                                                        scalar1=1.0, scalar2=-NEG, op0=ALU.subtract, op1=ALU.mult), r=[PT], w=[('mnegT', slot)])

                      def stageY(qb, slot):
                              gc = gcol(job, qb)
                              nlb = qb + 1
                              for g in range(2):
                                  bo, bd = 4, 5
                                  prow = slice(g * 64, g * 64 + 64)

                                  def logits(lb):
                                      bz = 2 + (lb & 1)
                                      T.op('pe', lambda q: q.matmul(ps[bz][:, :], lhsT=bkT[prow, lb * 128:(lb + 1) * 128],
                                                                    rhs=bqT2[slot][prow, :, :], start=True, stop=False), r=[('bkT', lb), ('bqT', slot)], w=[PK[bz]])
                                      for hh in range(4):
                                          T.op('pe', lambda q: q.matmul(ps[bz][:, hh * 128:(hh + 1) * 128], lhsT=ident, rhs=mneg2[slot][:, lb, :], start=False, stop=(hh == 3)),
                                               r=[('mnegT', slot)] + CB, w=[PK[bz]])
                                      T.op('act', lambda q: q.activation(out=pB[lb & 1][:, :], in_=ps[bz][:, :], func=AF.Exp, bias=nshift, scale=1.0),
                                           r=[PK[bz], 'shiftc2'], w=[('pB', lb & 1)])
                                  logits(0)
                                  for lb in range(nlb):
                                      sl = lb & 1
                                      if lb + 1 < nlb:
                                          logits(lb + 1)
                                      T.op('pe', lambda q: q.matmul(ps[bo][:, :], lhsT=bv2[:, lb, :], rhs=pB[sl][:, :], start=(lb == 0), stop=(lb == nlb - 1)),
                                           r=[('bv2', lb), ('pB', sl)], w=[PK[bo]])
                                      T.op('pe', lambda q: q.matmul(ps[bd][:, :], lhsT=onesb, rhs=pB[sl][:, :], start=(lb == 0), stop=(lb == nlb - 1)),
                                           r=[('pB', sl)] + CB, w=[PK[bd]])
                                  T.op('act', lambda q: q.copy(out=oraw[g][prow, :], in_=ps[bo][prow, :]), r=[PK[bo]], w=['oraw'])
                                  T.op('act', lambda q: q.activation(out=draw[g][prow, :], in_=ps[bd][prow, :], func=AF.Ln, bias=1e-30, scale=1.0), r=[PK[bd]], w=['draw'])
                                  T.op('act', lambda q: q.activation(out=draw[g][prow, :], in_=draw[g][prow, :], func=AF.Exp, scale=-1.0), r=['draw'], w=['draw'])
                                  T.op('pool', lambda q: q.tensor_tensor(out=obT[prow, :, :].rearrange("p c t -> p (c t)"), in0=oraw[g][prow, :], in1=draw[g][prow, :], op=ALU.mult),
                                       r=['oraw', 'draw'], w=['obT'])
                              T.dma('sp', oT_d[1, :, :, gc * 128:(gc + 1) * 128].rearrange("c p t -> p c t"), obT[:], r=['obT'], w=[('oTd', 1, gc)])

                      qbs = job['qblocks']
                      nq_ = len(qbs)
                      gens = [stageX(qbs[i], i & 1) for i in range(nq_)]
                      next(gens[0]); next(gens[0]); next(gens[0])
                      if nq_ > 1:
                          next(gens[1])
                      next(gens[0], None)
                      if nq_ > 1:
                          next(gens[1])
                      for i in range(1, nq_):
                          next(gens[i])
                          if i + 1 < nq_:
                              next(gens[i + 1])
                          stageY(qbs[i - 1], (i - 1) & 1)
                          next(gens[i], None)
                          if i + 1 < nq_:
                              next(gens[i + 1])
                      stageY(qbs[-1], (nq_ - 1) & 1)
                      T.barrier()

                  CK('B')
                  with ExitStack() as sC:
                      def sbc(name, shape, dt=F32):
                          return sC.enter_context(nc.sbuf_tensor(U(name), list(shape), dt))
                      nq = 128 * len(job['qblocks'])
                      ckT = sbc("ckT", [128, L], BF16); cvt = sbc("cvt", [128, nb, 128], BF16); cqT = sbc("cqT", [128, nq], BF16)
                      w_c = sbc("w_c", [128, 8, 384], BF16); kc2 = [sbc(f"kc2{i}", [128, 128], BF16) for i in range(2)]
                      kcs = [sbc(f"kcs{i}", [128, 8, 128], BF16) for i in range(2)] if issamp else None
                      ocT = [sbc(f"ocT{i}", [128, 512], BF16) for i in range(2)]
                      Gt = [sbc(f"Gt{i}", [128, 512]) for i in range(2)]; wvt = [sbc(f"wvt{i}", [128, 512], BF16) for i in range(4)]
                      for hp in range(4):
                          T.dma('pool', w_c[:, :, 0:128], wview(w_in, l, C_CQ + hp * 128, C_CQ + (hp + 1) * 128), w=['w_c'])
                          T.dma('pool', w_c[:, :, 128:256], wview(w_in, l, C_CK + hp * 128, C_CK + (hp + 1) * 128), w=['w_c'])
                          T.dma('pool', w_c[:, :, 256:384], wview(w_in, l, C_CV + hp * 128, C_CV + (hp + 1) * 128), w=['w_c'])
                          if issamp:
                              s = job['s']
                              for gi in range(NBP // 8):
                                  b0 = gi * 8
                                  st = kcs[gi & 1]
                                  rws = slice(b0 * 128, (b0 + 8) * 128)
                                  T.dma('pool', st[:], cck[l, s, rws, hp * 128:(hp + 1) * 128].rearrange("(b p) c -> p b c", p=128), w=[('kcs', gi & 1)])
                                  T.dma('pool', cvt[:, b0:b0 + 8, :], ccv[l, s, rws, hp * 128:(hp + 1) * 128].rearrange("(b p) c -> p b c", p=128),
                                        w=[('cvt', b_) for b_ in range(b0, b0 + 8)])
                                  for j in range(8):
                                      blk = b0 + j
                                      T.op('pe', lambda q: q.transpose(pst[:, (j & 1) * 128:(j & 1) * 128 + 128], st[:, j, :], ident), r=[('kcs', gi & 1)] + CB, w=[PT])
                                      T.op('act', lambda q: q.copy(out=ckT[:, blk * 128:(blk + 1) * 128], in_=pst[:, (j & 1) * 128:(j & 1) * 128 + 128]), r=[PT], w=[('ckT', blk)])
                          for blk in range(nb):
                              if issamp and blk < NBP:
                                  continue
                              sl = blk & 1
                              kkey = ('kc2', sl)
                              if blk in comp_blocks:
                                  gc = gcol(job, blk)
                                  hk = ('hT', gc)
                                  hcols = slice(gc * 128, (gc + 1) * 128)
                                  b = mmbank()
                                  for kc in range(8):
                                      T.op('pe', lambda q: q.matmul(ps[b][:, 0:256], lhsT=hT[:, kc, hcols], rhs=w_c[:, kc, 128:384], start=(kc == 0), stop=(kc == 7)),
                                           r=[hk, 'w_c'], w=[PK[b]])
                                  so = stg[sl]
                                  T.op('act', lambda q: q.copy(out=so[:, 0:256], in_=ps[b][:, 0:256]), r=[PK[b]], w=[('stg', sl)])
                                  T.op('dve', lambda q: q.tensor_copy(out=kc2[sl][:, :], in_=so[:, 0:128]), r=[('stg', sl)], w=[kkey])
                                  T.op('pool', lambda q: q.tensor_copy(out=cvt[:, blk, :], in_=so[:, 128:256]), r=[('stg', sl)], w=[('cvt', blk)])
                                  if not issamp:
                                      rows = slice(blk * 128, (blk + 1) * 128)
                                      T.dma('sp', o_ck_p[l, rows, hp * 128:(hp + 1) * 128], so[:, 0:128], r=[('stg', sl)])
                                      T.dma('sp', o_cv_p[l, rows, hp * 128:(hp + 1) * 128], so[:, 128:256], r=[('stg', sl)])
                                  else:
                                      s = job['s']
                                      T.dma('sp', o_ck_s[l, s, :, hp * 128:(hp + 1) * 128], so[0:16, 0:128], r=[('stg', sl)])
                                      T.dma('sp', o_cv_s[l, s, :, hp * 128:(hp + 1) * 128], so[0:16, 128:256], r=[('stg', sl)])
                              else:
                                  s = job['s']
                                  rows = slice(blk * 128, (blk + 1) * 128)
                                  T.dma('pool', kc2[sl][:, :], cck[l, s, rows, hp * 128:(hp + 1) * 128], w=[kkey])
                                  T.dma('pool', cvt[:, blk, :], ccv[l, s, rows, hp * 128:(hp + 1) * 128], w=[('cvt', blk)])
                              T.op('pe', lambda q: q.transpose(pst[:, 0:128], kc2[sl][:, :], ident), r=[kkey] + CB, w=[PT])
                              T.op('act', lambda q: q.copy(out=ckT[:, blk * 128:(blk + 1) * 128], in_=pst[:, 0:128]), r=[PT], w=[('ckT', blk)])
                          for qi, qb in enumerate(job['qblocks']):
                              gc = gcol(job, qb)
                              b = mmbank()
                              for kc in range(8):
                                  T.op('pe', lambda q: q.matmul(ps[b][:, 0:128], lhsT=w_c[:, kc, 0:128], rhs=hT[:, kc, gc * 128:(gc + 1) * 128], start=(kc == 0), stop=(kc == 7)),
                                       r=[('hT', gc), 'w_c'], w=[PK[b]])
                              T.op('act', lambda q: q.activation(out=cqT[:, qi * 128:(qi + 1) * 128], in_=ps[b][:, 0:128], func=AF.Copy, scale=0.125), r=[PK[b]], w=[('cqT', qi)])
                          qtiles = [job['qblocks'][i:i + 4] for i in range(0, len(job['qblocks']), 4)]
                          for ti, qt in enumerate(qtiles):
                              n = 128 * len(qt)
                              qc0 = ti * 512
                              qkeys = [('cqT', ti * 4 + i) for i in range(len(qt))]
                              nlb = qt[-1] + 1
                              osl = ti & 1
                              order = list(range(nlb - 1, -1, -1))

                              def stageA(lb, buf):
                                  diag = lb >= qt[0]
                                  for e2 in range(2):
                                      prow = slice(e2 * 64, e2 * 64 + 64)
                                      T.op('pe', lambda q: q.matmul(ps[e2][:, 0:n], lhsT=ckT[prow, lb * 128:(lb + 1) * 128], rhs=cqT[prow, qc0:qc0 + n], start=True, stop=(not diag)),
                                           r=[('ckT', lb)] + qkeys, w=[PK[e2]])
                                      if diag:
                                          o = lb - qt[0]
                                          T.op('pe', lambda q: q.matmul(ps[e2][:, 0:n], lhsT=ident, rhs=cb[:, K_DM + 512 * o:K_DM + 512 * o + n], start=False, stop=True),
                                               r=CB, w=[PK[e2]])
                                  for e2 in range(2):
                                      et = wk[2 * e2 + buf]; sp = wkb[2 * e2 + buf]
                                      T.op('act', lambda q: q.activation(out=et[:, 0:n], in_=ps[e2][:, 0:n], func=AF.Exp), r=[PK[e2]], w=[('wk', 2 * e2 + buf)])
                                      T.op('act', lambda q: q.activation(out=sp[:, 0:n], in_=et[:, 0:n], func=AF.Ln, bias=1.0, scale=1.0), r=[('wk', 2 * e2 + buf)], w=[('wkb', 2 * e2 + buf)])

                              def stageB(lb, buf, first):
                                  for e2 in range(2):
                                      sp = wkb[2 * e2 + buf]
                                      T.op('pe', lambda q: q.matmul(ps[2 + e2][:, 0:n], lhsT=negU, rhs=sp[:, 0:n], start=first, stop=True, skip_group_check=True), r=[('wkb', 2 * e2 + buf)] + CB, w=[PK[2 + e2]])
                                  for e2 in range(2):
                                      et = wk[2 * e2 + buf]
                                      T.op('act', lambda q: q.activation(out=Gt[e2][:, 0:n], in_=ps[2 + e2][:, 0:n], func=AF.Exp), r=[PK[2 + e2]], w=[('Gt', e2)])
                                      T.op('dve' if e2 == 0 else 'pool', lambda q: q.tensor_tensor(out=wvt[2 * e2 + buf][:, 0:n], in0=et[:, 0:n], in1=Gt[e2][:, 0:n], op=ALU.mult),
                                           r=[('wk', 2 * e2 + buf), ('Gt', e2)], w=[('wvt', 2 * e2 + buf)])
                                  for e2 in range(2):
                                      sp = wkb[2 * e2 + buf]
                                      T.op('pe', lambda q: q.matmul(ps[2 + e2][:, 0:n], lhsT=negL, rhs=sp[:, 0:n], start=False, stop=True, skip_group_check=True), r=[('wkb', 2 * e2 + buf)] + CB, w=[PK[2 + e2]])

                              def stagePV(lb, buf, first, last):
                                  for e2 in range(2):
                                      T.op('pe', lambda q: q.matmul(ps[4 + e2][:, 0:n], lhsT=cvt[:, lb, :], rhs=wvt[2 * e2 + buf][:, 0:n], start=first, stop=last),
                                           r=[('cvt', lb), ('wvt', 2 * e2 + buf)], w=[PK[4 + e2]])

                              no = len(order)
                              stageA(order[0], 0)
                              for i_, lb in enumerate(order):
                                  if i_ + 1 < no:
                                      stageA(order[i_ + 1], (i_ + 1) & 1)
                                  stageB(lb, i_ & 1, i_ == 0)
                                  if i_ >= 1:
                                      stagePV(order[i_ - 1], (i_ - 1) & 1, i_ == 1, False)
                              stagePV(order[no - 1], (no - 1) & 1, no == 1, True)
                              for e2 in range(2):
                                  prow = slice(e2 * 64, e2 * 64 + 64)
                                  T.op('act', lambda q: q.copy(out=ocT[osl][prow, 0:n], in_=ps[4 + e2][prow, 0:n]), r=[PK[4 + e2]], w=[('ocT', osl)])
                              for i, qb in enumerate(qt):
                                  gc = gcol(job, qb)
                                  T.dma('sp', oT_d[2, hp, :, gc * 128:(gc + 1) * 128], ocT[osl][:, i * 128:(i + 1) * 128], r=[('ocT', osl)], w=[('oTd', 2, gc)])
                      T.barrier()

              s12.__exit__(None, None, None); open_scopes.pop()
              CK('jobs')
              groups = [list(range(0, 8)), list(range(8, 16)), list(range(16, 24)), list(range(24, NBT))]
              with ExitStack() as s3:
                  def sb3(name, shape, dt=F32):
                      return s3.enter_context(nc.sbuf_tensor(U(name), list(shape), dt))
                  NG = 10
                  xg = sb3("xg", [128, NG, D]); hg = sb3("hg", [128, 8, NG * 128], BF16)
                  mT = sb3("mT", [128, 8, NG * 128], BF16)
                  gffn = sb3("gffn", [128, D]); gple = sb3("gple", [128, D])
                  bcast(gffn, 'gffn', norm_ffn[l:l + 1, :], D)
                  bcast(gple, 'gple', norm_ple[l:l + 1, :], D)
                  wg = sb3("wg", [128, 8, 1024], BF16); wb_ = sb3("wb_", [128, 4, 1024], BF16)
                  for grp in groups:
                      ng = len(grp)
                      ntok = ng * 128
                      tiles = [(c0, min(ntok, c0 + 512)) for c0 in range(0, ntok, 512)]
                      for i, gblk in enumerate(grp):
                          if gblk < NBP:
                              T.dma('sp', xg[:, i, :], (x_p if l == 0 else xs)[gblk * 128:(gblk + 1) * 128, :], w=[('xg', i)])
                          else:
                              s = gblk - NBP
                              T.op('pool', lambda q: q.memset(xg[:, i, :], 0.0), w=[('xg', i)])
                              T.dma('sp', xg[0:16, i, :], x_s[s] if l == 0 else xs[gblk * 128:gblk * 128 + 16, :], w=[('xg', i)])
                          sl = i & 1
                          rmsnorm_rows(xg[:, i, :], [('xg', i)], D, 128, gmix, 'gmix', xb[sl][:], ('xb', sl), 0)
                          transpose_to(hg[:, :, i * 128:(i + 1) * 128], ('hg', i), xb[sl], ('xb', sl), 8)
                      hkeys = [('hg', i) for i in range(ng)]
                      sM = ExitStack(); sM.__enter__(); open_scopes.append(sM)
                      og = sM.enter_context(nc.sbuf_tensor(U("og"), [128, 4, NG * 128], BF16))
                      macc = sM.enter_context(nc.sbuf_tensor(U("macc"), [128, 8, NG * 128], F32))
                      for br in range(3):
                          T.dma('pool', wg[:], wview(w_in, l, C_GL + br * 1024, C_GL + (br + 1) * 1024), w=['wg'])
                          if br == 1:
                              for g2 in range(2):
                                  T.dma('pool', wb_[g2 * 64:(g2 + 1) * 64, :, :], w_br[br][l, g2 * 256:(g2 + 1) * 256, :].rearrange("(c d) n -> d c n", d=64), w=['wb_'])
                          else:
                              T.dma('pool', wb_[:], w_br[br][l].rearrange("(kc p) n -> p kc n", p=128), w=['wb_'])
                          T.dma('sp', og[:, :, 0:ntok], oT_d[br, :, :, grp[0] * 128:grp[0] * 128 + ntok].rearrange("c p t -> p c t"),
                                r=[('oTd', br, g_) for g_ in grp], w=['og'])
                          for cc in range(8):
                              for (t0, t1) in tiles:
                                  n = t1 - t0
                                  b = mmbank()
                                  for kc in range(8):
                                      T.op('pe', lambda q: q.matmul(ps[b][:, 0:n], lhsT=wg[:, kc, cc * 128:(cc + 1) * 128], rhs=hg[:, kc, t0:t1], start=(kc == 0), stop=(kc == 7)),
                                           r=['wg'] + hkeys, w=[PK[b]])
                                  T.op('act', lambda q: q.activation(out=wk[b][:, 0:n], in_=ps[b][:, 0:n], func=AF.Sigmoid, bias=gb[:, br * 8 + cc:br * 8 + cc + 1], scale=1.0),
                                       r=[PK[b], 'gb'], w=[('wk', b)])
                                  b2 = 2 + b
                                  for kc in range(4):
                                      T.op('pe', lambda q: q.matmul(ps[b2][:, 0:n], lhsT=wb_[:, kc, cc * 128:(cc + 1) * 128], rhs=og[:, kc, t0:t1], start=(kc == 0), stop=(kc == 3)),
                                           r=['wb_', 'og'], w=[PK[b2]])
                                  mk = ('macc', cc, t0)
                                  if br == 0:
                                      T.op('dve', lambda q: q.tensor_tensor(out=macc[:, cc, t0:t1], in0=ps[b2][:, 0:n], in1=wk[b][:, 0:n], op=ALU.mult), r=[PK[b2], ('wk', b)], w=[mk])
                                  else:
                                      T.op('dve', lambda q: q.tensor_tensor(out=wk[2 + b][:, 0:n], in0=ps[b2][:, 0:n], in1=wk[b][:, 0:n], op=ALU.mult), r=[PK[b2], ('wk', b)], w=[('wk', 2 + b)])
                                      if br == 1:
                                          T.op('pool', lambda q: q.tensor_tensor(out=macc[:, cc, t0:t1], in0=macc[:, cc, t0:t1], in1=wk[2 + b][:, 0:n], op=ALU.add), r=[mk, ('wk', 2 + b)], w=[mk])
                                      else:
                                          T.op('pool', lambda q: q.tensor_tensor(out=mT[:, cc, t0:t1], in0=macc[:, cc, t0:t1], in1=wk[2 + b][:, 0:n], op=ALU.add), r=[mk, ('wk', 2 + b)], w=[('mT', cc, t0)])
                      mkeys = [('mT', cc, t0) for cc in range(8) for (t0, _) in tiles]
                      T.barrier()
                      sM.__exit__(None, None, None); open_scopes.pop()
                      sF = ExitStack(); sF.__enter__(); open_scopes.append(sF)
                      actT = sF.enter_context(nc.sbuf_tensor(U("actT"), [128, 22, NG * 128], BF16))

                      def tok_major_update(wsrc3, wkey_unused, lhs_buf, lhs_keys, nkc, post):
                          for i in range(ng):
                              for half in range(2):
                                  b = mmbank()
                                  for kc in range(nkc):
                                      T.op('pe', lambda q: q.matmul(ps[b][:, :], lhsT=lhs_buf[:, kc, i * 128:(i + 1) * 128], rhs=wsrc3[:, kc, half * 512:(half + 1) * 512],
                                                                    start=(kc == 0), stop=(kc == nkc - 1)), r=lhs_keys + [wkey_unused], w=[PK[b]])
                                  post(i, half, b)

                      T.dma('pool', wg[:], w_out[l].rearrange("(kc p) n -> p kc n", p=128), w=['wg'])

                      def post_add(i, half, b):
                          T.op('dve', lambda q: q.tensor_tensor(out=xg[:, i, half * 512:(half + 1) * 512], in0=xg[:, i, half * 512:(half + 1) * 512], in1=ps[b][:, :], op=ALU.add),
                               r=[PK[b], ('xg', i)], w=[('xg', i)])
                      tok_major_update(wg, 'wg', mT, mkeys, 8, post_add)
                      for i in range(ng):
                          sl = i & 1
                          rmsnorm_rows(xg[:, i, :], [('xg', i)], D, 128, gffn, 'gffn', xb[sl][:], ('xb', sl), 0)
                          transpose_to(hg[:, :, i * 128:(i + 1) * 128], ('hg', i), xb[sl], ('xb', sl), 8)
                      for s0 in range(0, DFF, 512):
                          s1 = min(DFF, s0 + 512)
                          nsl = s1 - s0
                          T.dma('pool', wg[:, :, 0:nsl], wview(w_ffn_in, l, s0, s1), w=['wg'])
                          T.dma('pool', wg[:, :, 512:512 + nsl], wview(w_ffn_in, l, DFF + s0, DFF + s1), w=['wg'])
                          for jj in range(nsl // 128):
                              j = s0 // 128 + jj
                              for (t0, t1) in tiles:
                                  n = t1 - t0
                                  b = mmbank(); b2 = 2 + b
                                  for kc in range(8):
                                      T.op('pe', lambda q: q.matmul(ps[b][:, 0:n], lhsT=wg[:, kc, jj * 128:(jj + 1) * 128], rhs=hg[:, kc, t0:t1], start=(kc == 0), stop=(kc == 7)),
                                           r=['wg'] + hkeys, w=[PK[b]])
                                  for kc in range(8):
                                      T.op('pe', lambda q: q.matmul(ps[b2][:, 0:n], lhsT=wg[:, kc, 512 + jj * 128:512 + (jj + 1) * 128], rhs=hg[:, kc, t0:t1], start=(kc == 0), stop=(kc == 7)),
                                           r=['wg'] + hkeys, w=[PK[b2]])
                                  T.op('act', lambda q: q.activation(out=wk[b][:, 0:n], in_=ps[b][:, 0:n], func=AF.Silu), r=[PK[b]], w=[('wk', b)])
                                  T.op('dve', lambda q: q.tensor_tensor(out=actT[:, j, t0:t1], in0=ps[b2][:, 0:n], in1=wk[b][:, 0:n], op=ALU.mult), r=[PK[b2], ('wk', b)], w=[('actT', j, t0)])
                      akeys = [('actT', j, t0) for j in range(22) for (t0, _) in tiles]
                      for i in range(ng):
                          pass
                      for half in range(2):
                          for k0 in range(0, 22, 8):
                              k1 = min(22, k0 + 8)
                              T.dma('pool', wg[:, 0:k1 - k0, 0:512], w_ffn_out[l, k0 * 128:k1 * 128, half * 512:(half + 1) * 512].rearrange("(kc p) n -> p kc n", p=128), w=['wg'])
                              for i in range(ng):
                                  b = mmbank()
                                  for kc in range(k0, k1):
                                      T.op('pe', lambda q: q.matmul(ps[b][:, :], lhsT=actT[:, kc, i * 128:(i + 1) * 128], rhs=wg[:, kc - k0, 0:512], start=(kc == k0), stop=(kc == k1 - 1)),
                                           r=akeys + ['wg'], w=[PK[b]])
                                  post_add(i, half, b)
                      T.barrier()
                      sF.__exit__(None, None, None); open_scopes.pop()
                      sP = ExitStack(); sP.__enter__(); open_scopes.append(sP)
                      pT = sP.enter_context(nc.sbuf_tensor(U("pT"), [128, 2, NG * 128], BF16))
                      pt32 = sP.enter_context(nc.sbuf_tensor(U("pt32"), [128, 256], F32))
                      ptb = sP.enter_context(nc.sbuf_tensor(U("ptb"), [128, 256], BF16))
                      for i, gblk in enumerate(grp):
                          sl = i & 1
                          rmsnorm_rows(xg[:, i, :], [('xg', i)], D, 128, gple, 'gple', xb[sl][:], ('xb', sl), 0)
                          transpose_to(hg[:, :, i * 128:(i + 1) * 128], ('hg', i), xb[sl], ('xb', sl), 8)
                          if gblk < NBP:
                              T.dma('sp', pt32[:, :], p_p[l, gblk * 128:(gblk + 1) * 128, :], w=['pt32'])
                          else:
                              T.op('pool', lambda q: q.memset(pt32[:, :], 0.0), w=['pt32'])
                              T.dma('sp', pt32[0:16, :], p_s[l, gblk - NBP], w=['pt32'])
                          T.op('dve', lambda q: q.tensor_copy(out=ptb[:, :], in_=pt32[:, :]), r=['pt32'], w=['ptb'])
                          transpose_to(pT[:, :, i * 128:(i + 1) * 128], ('pT', i), ptb, 'ptb', 2)
                      T.dma('pool', wg[:], w_ple_gate[l].rearrange("(kc p) n -> p kc n", p=128), w=['wg'])
                      T.dma('pool', wb_[:, 0:2, :], w_ple_proj[l].rearrange("(kc p) n -> p kc n", p=128), w=['wb_'])
                      for i in range(ng):
                          for half in range(2):
                              b = mmbank(); b2 = 2 + b
                              for kc in range(8):
                                  T.op('pe', lambda q: q.matmul(ps[b][:, :], lhsT=hg[:, kc, i * 128:(i + 1) * 128], rhs=wg[:, kc, half * 512:(half + 1) * 512], start=(kc == 0), stop=(kc == 7)),
                                       r=[('hg', i), 'wg'], w=[PK[b]])
                              for kc in range(2):
                                  T.op('pe', lambda q: q.matmul(ps[b2][:, :], lhsT=pT[:, kc, i * 128:(i + 1) * 128], rhs=wb_[:, kc, half * 512:(half + 1) * 512], start=(kc == 0), stop=(kc == 1)),
                                       r=[('pT', i), 'wb_'], w=[PK[b2]])
                              T.op('act', lambda q: q.activation(out=wk[b][:, :], in_=ps[b][:, :], func=AF.Sigmoid), r=[PK[b]], w=[('wk', b)])
                              T.op('dve', lambda q: q.tensor_tensor(out=wk[b][:, :], in0=ps[b2][:, :], in1=wk[b][:, :], op=ALU.mult), r=[PK[b2], ('wk', b)], w=[('wk', b)])
                              T.op('pool', lambda q: q.tensor_tensor(out=xg[:, i, half * 512:(half + 1) * 512], in0=xg[:, i, half * 512:(half + 1) * 512], in1=wk[b][:, :], op=ALU.add),
                                   r=[('wk', b), ('xg', i)], w=[('xg', i)])
                      T.barrier()
                      sP.__exit__(None, None, None); open_scopes.pop()
                      for i, gblk in enumerate(grp):
                          if l == 0:
                              T.dma('sp', xs[gblk * 128:(gblk + 1) * 128, :], xg[:, i, :], r=[('xg', i)], w=[('xs', gblk)])
                          elif gblk < NBP:
                              T.dma('sp', y_p[gblk * 128:(gblk + 1) * 128, :], xg[:, i, :], r=[('xg', i)])
                          else:
                              T.dma('sp', y_s[gblk - NBP], xg[0:16, i, :], r=[('xg', i)])
                  T.barrier()
              CK('L0')

        except _Stop:
            for sc in reversed(open_scopes):
                sc.__exit__(None, None, None)
        T.dead = False
        T.barrier()
        print("instructions:", T.n_inst, "sems:", len(T.sems))
    return nc


_NC_CACHE = {}


def _make_maps(inp):
    f = lambda a: np.ascontiguousarray(np.asarray(a, dtype=np.float32))
    cst, cst2 = _consts()
    shared = {
        'norm_mix': f(inp['norm_mix']), 'w_in': f(inp['w_in']),
        'gbT': f(np.asarray(inp['gate_bias']).reshape(2, 24, 128).transpose(0, 2, 1)),
        'a_vnorm': f(inp['a_vnorm']), 'a_wsT': f(np.asarray(inp['a_ws']).transpose(0, 3, 1, 2)),
        'a_bias': f(np.asarray(inp['a_bias']).reshape(2, 512)),
        'b_qnorm': f(inp['b_qnorm']), 'b_knorm': f(inp['b_knorm']),
        'w_br_a': f(inp['w_br_a']), 'w_br_b': f(inp['w_br_b']), 'w_br_c': f(inp['w_br_c']),
        'w_out': f(inp['w_out']), 'norm_ffn': f(inp['norm_ffn']), 'w_ffn_in': f(inp['w_ffn_in']), 'w_ffn_out': f(inp['w_ffn_out']),
        'norm_ple': f(inp['norm_ple']), 'w_ple_gate': f(inp['w_ple_gate']), 'w_ple_proj': f(inp['w_ple_proj']), 'cst': cst, 'cst2': cst2,
    }
    xp = np.asarray(inp['x_prompt']); xsm = np.asarray(inp['x_sample'])
    in_maps = []
    for c in range(8):
        b = c % 4
        ss = slice(2 * c, 2 * c + 2)
        m = dict(shared)
        m['x_p'] = f(xp[b]); m['x_s'] = f(xsm[ss])
        m['cbk'] = f(np.asarray(inp['cache_b_k'])[:, ss].reshape(2, 2, SEQ, 128))
        m['cbv'] = f(np.asarray(inp['cache_b_v'])[:, ss].reshape(2, 2, SEQ, 128))
        m['cbi'] = f(np.asarray(inp['cache_b_kidx'])[:, ss])
        m['cck'] = f(np.asarray(inp['cache_c_k'])[:, ss].reshape(2, 2, SEQ, 512))
        m['ccv'] = f(np.asarray(inp['cache_c_v'])[:, ss].reshape(2, 2, SEQ, 512))
        m['p_p'] = f(np.asarray(inp['p_prompt'])[:, b]); m['p_s'] = f(np.asarray(inp['p_sample'])[:, ss])
        in_maps.append(m)
    return in_maps


def kernel(**inp):
    if 'nc' not in _NC_CACHE:
        _NC_CACHE['nc'] = build_program()
    nc = _NC_CACHE['nc']
    in_maps = _make_maps(inp)
    res = run_bass_kernel_spmd(nc, in_maps, core_ids=list(range(8))).results
    st = lambda name, cores: np.stack([np.asarray(res[c][name]) for c in cores], axis=1)
    P = range(4); A = range(8)
    y_prompt = np.stack([res[c]['y_p'] for c in P], 0).astype(np.float32)
    y_sample = np.concatenate([res[c]['y_s'] for c in A], 0).astype(np.float32)
    cat_s = lambda name: np.concatenate([np.asarray(res[c][name]) for c in A], axis=1)
    outs = (
        y_prompt, y_sample,
        st('o_bk_p', P).reshape(2, 4, SEQ, 2, 64), st('o_bv_p', P).reshape(2, 4, SEQ, 2, 64), st('o_bi_p', P).reshape(2, 4, SEQ, 32),
        st('o_ck_p', P).reshape(2, 4, SEQ, 8, 64), st('o_cv_p', P).reshape(2, 4, SEQ, 8, 64),
        cat_s('o_bk_s').reshape(2, 16, 16, 2, 64), cat_s('o_bv_s').reshape(2, 16, 16, 2, 64), cat_s('o_bi_s').reshape(2, 16, 16, 32),
        cat_s('o_ck_s').reshape(2, 16, 16, 8, 64), cat_s('o_cv_s').reshape(2, 16, 16, 8, 64), cat_s('o_av_s').reshape(2, 16, 16, 512),
    )
    return tuple(np.ascontiguousarray(o, dtype=np.float32) for o in outs)
```

```python
import numpy as np
from contextlib import ExitStack
import concourse.bass as bass
import concourse.mybir as mybir
from concourse.bass_utils import run_bass_kernel_spmd

F32 = mybir.dt.float32
BF16 = mybir.dt.bfloat16
AF = mybir.ActivationFunctionType
ALU = mybir.AluOpType
AX = mybir.AxisListType

D = 1024
SEQ = 4096
NBP = 32
NBT = 34
NTOK = NBT * 128
DFF = 2816
EPS = 1e-6
NEG = -30000.0
BIG = 1.0e30
NIT = 24
C_AU, C_AV, C_BQ, C_BK, C_BV, C_IQ, C_IK, C_IW, C_CQ, C_CK, C_CV, C_GL = (
    0, 512, 1024, 1536, 1664, 1792, 2048, 2080, 2088, 2600, 3112, 3624)
K_ID, K_NU, K_NL, K_ONE, K_DM, K_GM, K_W2 = 0, 128, 256, 384, 512, 512 + 2048, 512 + 2048 + 128
K_TOT = K_W2 + 32
K2_ONE, K2_GM, K2_W2, K2_TOT = 0, 128, 256, 288


def _consts():
    c = np.zeros((128, K_TOT), np.float32)
    j = np.arange(128)[:, None]
    l = np.arange(128)[None, :]
    c[:, K_ID:K_ID + 128] = np.eye(128)
    c[:, K_NU:K_NU + 128] = -1.0 * (j >= l)
    c[:, K_NL:K_NL + 128] = -1.0 * (j < l)
    c[:, K_ONE:K_ONE + 128] = 1.0
    q = np.arange(512)[None, :]
    for o in range(4):
        c[:, K_DM + 512 * o:K_DM + 512 * (o + 1)] = np.where((128 * o + j) >= q, NEG, 0.0)
    c[:, K_GM:K_GM + 128] = ((j // 64) <= (l // 64))
    c[:, K_W2:K_W2 + NIT] = 2.0 ** (-(np.arange(NIT)[None, :] + 0.0))
    c2 = np.concatenate([c[:, K_ONE:K_ONE + 128], c[:, K_GM:K_GM + 128], c[:, K_W2:K_W2 + 32]], axis=1)
    return c, np.ascontiguousarray(c2)


class Trk:
    EP = 16000

    def __init__(self, nc, es):
        self.nc = nc
        self.q = {'pe': nc.tensor, 'act': nc.scalar, 'dve': nc.vector, 'pool': nc.gpsimd, 'sp': nc.sync}
        self.sems = []
        self.es = es
        self.cnt = {e: 0 for e in ('pe', 'act', 'dve', 'pool')}
        self.esem = {e: [] for e in self.cnt}
        self.seen = {e: {} for e in self.q}
        self.lastw = {}
        self.readers = {}
        self.ndma = 40
        self.dsem = [self._new(f"d{i}") for i in range(self.ndma)]
        self.dval = [0] * self.ndma
        self.drr = 0
        self.drr_sw = 0
        self.n_inst = 0
        self.dead = False

    def _new(self, name):
        s = self.es.enter_context(self.nc.semaphore(name))
        self.sems.append(s)
        return len(self.sems) - 1

    def _wait(self, e, ev):
        src, si, val = ev
        if src == 'pe' and e == 'pe':
            return
        if self.seen[e].get(si, 0) >= val:
            return
        self.q[e].wait_ge(self.sems[si], val)
        self.seen[e][si] = val

    def _deps(self, e, r, w):
        evs = []
        for k in r:
            if k in self.lastw:
                evs.append(self.lastw[k])
            if isinstance(k, tuple) and k[0] in ('ps', 'pst'):
                rd = self.readers.get(k)
                if rd:
                    evs.extend(ev for ev in rd.values() if ev[0] != e)
        for k in w:
            if k in self.lastw:
                evs.append(self.lastw[k])
            rd = self.readers.get(k)
            if rd:
                evs.extend(rd.values())
        for ev in evs:
            self._wait(e, ev)

    def _record(self, ev, r, w):
        for k in w:
            self.lastw[k] = ev
            self.readers[k] = {}
        for k in r:
            d = self.readers.setdefault(k, {})
            o = d.get(ev[1])
            if o is None or o[2] < ev[2]:
                d[ev[1]] = ev

    def op(self, e, fn, r=(), w=()):
        if self.dead:
            return None
        self._deps(e, r, w)
        inst = fn(self.q[e])
        n = self.cnt[e]
        ep, off = divmod(n, self.EP)
        if ep >= len(self.esem[e]):
            self.esem[e].append(self._new(f"{e}{ep}"))
        si = self.esem[e][ep]
        inst.then_inc(self.sems[si], 1)
        self.cnt[e] = n + 1
        self.n_inst += 1
        ev = (e, si, off + 1)
        if e != 'pe':
            self.seen[e][si] = max(self.seen[e].get(si, 0), 0)
        self._record(ev, r, w)
        return ev

    def dma(self, e, out, in_, r=(), w=()):
        if self.dead:
            return None
        half = self.ndma // 2
        if e == 'pool':
            i = self.drr_sw
            self.drr_sw = (self.drr_sw + 1) % half
        else:
            i = half + self.drr
            self.drr = (self.drr + 1) % half
        si = self.dsem[i]
        if self.dval[i] > 0:
            self._wait(e, ('dma', si, self.dval[i]))
        self._deps(e, r, w)
        inst = self.q[e].dma_start(out=out, in_=in_)
        self.dval[i] += 16
        inst.then_inc(self.sems[si], 16)
        self.n_inst += 1
        ev = ('dma', si, self.dval[i])
        self._record(ev, r, w)
        return ev

    def barrier(self):
        for e in self.q:
            for o in self.cnt:
                n = self.cnt[o]
                if n == 0 or o == e:
                    continue
                ep, off = divmod(n - 1, self.EP)
                self._wait(e, (o, self.esem[o][ep], off + 1))
            for i in range(self.ndma):
                if self.dval[i]:
                    self._wait(e, ('dma', self.dsem[i], self.dval[i]))
        for e in ('act', 'dve', 'pool'):
            n = self.cnt[e]
            if n:
                ep, off = divmod(n - 1, self.EP)
                self._wait(e, (e, self.esem[e][ep], off + 1))


class _Stop(Exception):
    pass


def build_program(stop=None):
    nc = bass.Bass("TRN2", target_bir_lowering=False)

    def din(name, shape):
        return nc.dram_tensor(name, list(shape), F32, kind="ExternalInput").ap()

    def dout(name, shape):
        return nc.dram_tensor(name, list(shape), F32, kind="ExternalOutput").ap()

    x_p = din("x_p", [SEQ, D]); x_s = din("x_s", [2, 16, D])
    cbk = din("cbk", [2, 2, SEQ, 128]); cbv = din("cbv", [2, 2, SEQ, 128]); cbi = din("cbi", [2, 2, SEQ, 32])
    cck = din("cck", [2, 2, SEQ, 512]); ccv = din("ccv", [2, 2, SEQ, 512])
    p_p = din("p_p", [2, SEQ, 256]); p_s = din("p_s", [2, 2, 16, 256])
    norm_mix = din("norm_mix", [2, D]); w_in = din("w_in", [2, D, 6696]); gbT = din("gbT", [2, 128, 24])
    a_vnorm = din("a_vnorm", [2, 512]); a_wsT = din("a_wsT", [2, 128, 4, 128]); a_bias = din("a_bias", [2, 512])
    b_qnorm = din("b_qnorm", [2, 64]); b_knorm = din("b_knorm", [2, 64])
    w_br = [din("w_br_a", [2, 512, D]), din("w_br_b", [2, 512, D]), din("w_br_c", [2, 512, D])]
    w_out = din("w_out", [2, D, D]); norm_ffn = din("norm_ffn", [2, D]); w_ffn_in = din("w_ffn_in", [2, D, 2 * DFF])
    w_ffn_out = din("w_ffn_out", [2, DFF, D]); norm_ple = din("norm_ple", [2, D]); w_ple_gate = din("w_ple_gate", [2, D, D])
    w_ple_proj = din("w_ple_proj", [2, 256, D]); cst = din("cst", [128, K_TOT]); cst2 = din("cst2", [128, K2_TOT])

    y_p = dout("y_p", [SEQ, D]); y_s = dout("y_s", [2, 16, D])
    o_bk_p = dout("o_bk_p", [2, SEQ, 128]); o_bv_p = dout("o_bv_p", [2, SEQ, 128]); o_bi_p = dout("o_bi_p", [2, SEQ, 32])
    o_ck_p = dout("o_ck_p", [2, SEQ, 512]); o_cv_p = dout("o_cv_p", [2, SEQ, 512])
    o_bk_s = dout("o_bk_s", [2, 2, 16, 128]); o_bv_s = dout("o_bv_s", [2, 2, 16, 128]); o_bi_s = dout("o_bi_s", [2, 2, 16, 32])
    o_ck_s = dout("o_ck_s", [2, 2, 16, 512]); o_cv_s = dout("o_cv_s", [2, 2, 16, 512]); o_av_s = dout("o_av_s", [2, 2, 16, 512])

    dbgk = "ExternalOutput" if stop is not None else "Internal"
    xs = nc.dram_tensor("xs", [NTOK, D], F32, kind=dbgk).ap()
    oT_d = nc.dram_tensor("oT_d", [3, 4, 128, NTOK], BF16, kind=dbgk).ap()

    es = ExitStack()
    with es:
        T = Trk(nc, es)
        uid = [0]

        def U(name):
            uid[0] += 1
            return f"{name}_{uid[0]}"

        def sb(name, shape, dt=F32):
            return es.enter_context(nc.sbuf_tensor(name, list(shape), dt))

        ps = [es.enter_context(nc.psum_tensor(f"ps{i}", [128, 512], F32)) for i in range(7)]
        pst = es.enter_context(nc.psum_tensor("pst", [128, 1024], BF16))
        PK = [('ps', i) for i in range(7)]
        PT = ('pst',)

        cf = sb("cf", [128, K2_TOT]); cb = sb("cb", [128, K_GM + 128], BF16)
        T.dma('sp', cf[:], cst2[:, :], w=['cf'])
        T.dma('pool', cb[:], cst[:, 0:K_GM + 128], w=['cb'])
        ident = cb[:, K_ID:K_ID + 128]; negU = cb[:, K_NU:K_NU + 128]; negL = cb[:, K_NL:K_NL + 128]
        onesb = cb[:, K_ONE:K_ONE + 128]
        ones1 = cf[0:1, K2_ONE:K2_ONE + 128]
        CB = ['cb']

        gmix = sb("gmix", [128, D]); gq8 = sb("gq8", [128, 64]); gk = sb("gk", [128, 64])
        gb = sb("gb", [128, 24])
        rowt = sb("rowt", [1, 512]); shiftc = sb("shiftc", [128, 4])
        xt = [None, None]
        xb = [sb(f"xbh{i}", [128, D], BF16) for i in range(2)]
        col = sb("col", [128, 64])
        stg = [sb(f"stg{i}", [128, 512]) for i in range(2)]
        wk = [sb(f"wk{i}", [128, 512]) for i in range(4)]
        wkb = [sb(f"wkb{i}", [128, 512], BF16) for i in range(4)]
        mmrr = [0]

        def mmbank():
            mmrr[0] ^= 1
            return mmrr[0]

        def bcast(dst, key, row_ap, n, scale=None):
            for c0 in range(0, n, 512):
                c1 = min(n, c0 + 512)
                T.dma('sp', rowt[0:1, 0:c1 - c0], row_ap[:, c0:c1], w=['rowt'])
                b = mmbank()
                T.op('pe', lambda q: q.matmul(ps[b][:, 0:c1 - c0], lhsT=ones1, rhs=rowt[0:1, 0:c1 - c0], start=True, stop=True),
                     r=['rowt', 'cf'], w=[PK[b]])
                if scale is None:
                    T.op('dve', lambda q: q.tensor_copy(out=dst[:, c0:c1], in_=ps[b][:, 0:c1 - c0]), r=[PK[b]], w=[key])
                else:
                    T.op('dve', lambda q: q.tensor_scalar(out=dst[:, c0:c1], in0=ps[b][:, 0:c1 - c0], scalar1=scale, scalar2=None, op0=ALU.mult),
                         r=[PK[b]], w=[key])

        wslot = [0]

        def load_w(src3, ncols, key=None):
            i = wslot[0]; wslot[0] ^= 1
            kc = src3.shape[1]
            T.dma('pool', wbuf[i][:, 0:kc, 0:ncols], src3, w=[('wbuf', i)])
            return wbuf[i], ('wbuf', i)

        def wview(wap, l, c0, c1):
            return wap[l].rearrange("(kc p) n -> p kc n", p=128)[:, :, c0:c1]

        def rmsnorm_rows(src_ap, src_keys, n, npart, gtile, gkey, dst_bf, dst_key, cidx):
            T.op('act', lambda q: q.activation(out=dst_bf, in_=src_ap, func=AF.Square,
                                               accum_out=col[0:npart, cidx:cidx + 1]), r=src_keys, w=[dst_key, ('col', cidx)])
            T.op('act', lambda q: q.activation(out=col[0:npart, cidx:cidx + 1], in_=col[0:npart, cidx:cidx + 1], func=AF.Sqrt, bias=EPS, scale=1.0 / n), r=[('col', cidx)], w=[('col', cidx)])
            T.op('dve', lambda q: q.reciprocal(out=col[0:npart, cidx:cidx + 1], in_=col[0:npart, cidx:cidx + 1]), r=[('col', cidx)], w=[('col', cidx)])
            T.op('dve', lambda q: q.scalar_tensor_tensor(out=dst_bf, in0=src_ap, scalar=col[0:npart, cidx:cidx + 1], in1=gtile[0:npart, 0:n],
                                                         op0=ALU.mult, op1=ALU.mult), r=list(src_keys) + [('col', cidx), gkey], w=[dst_key])

        def transpose_to(dst3, dst_key, src_bf, src_key, nchunks, npart=128):
            for c in range(nchunks):
                T.op('pe', lambda q: q.transpose(pst[:, c * 128:c * 128 + npart], src_bf[0:npart, c * 128:(c + 1) * 128], ident[0:npart, 0:npart]),
                     r=[src_key] + CB, w=[PT])
            T.op('act', lambda q: q.copy(out=dst3, in_=pst[:, 0:nchunks * 128].rearrange("p (c t) -> p c t", c=nchunks)[:, :, 0:npart]),
                 r=[PT], w=[dst_key])

        def gelu_to(dst, dst_key, src_ap, src_keys, npart, n, tmp_i):
            a = wk[tmp_i][0:npart, 0:n]; b = wk[tmp_i + 1][0:npart, 0:n]
            ka, kb = ('wk', tmp_i), ('wk', tmp_i + 1)
            T.op('act', lambda q: q.activation(out=a, in_=src_ap, func=AF.Square), r=src_keys, w=[ka])
            T.op('dve', lambda q: q.tensor_scalar(out=a, in0=a, scalar1=0.044715, scalar2=1.0, op0=ALU.mult, op1=ALU.add), r=[ka], w=[ka])
            T.op('dve', lambda q: q.tensor_tensor(out=a, in0=a, in1=src_ap, op=ALU.mult), r=[ka] + list(src_keys), w=[ka])
            T.op('act', lambda q: q.activation(out=b, in_=a, func=AF.Sigmoid, scale=1.5957691216057308), r=[ka], w=[kb])
            T.op('dve', lambda q: q.tensor_tensor(out=dst, in0=b, in1=src_ap, op=ALU.mult), r=[kb] + list(src_keys), w=[dst_key])

        def x_block_load(l, gblk, slot):
            key = ('xt', slot)
            if gblk < NBP:
                src = (x_p if l == 0 else xs)[gblk * 128:(gblk + 1) * 128, :]
                T.dma('sp', xt[slot][:], src, w=[key])
            else:
                s = gblk - NBP
                T.op('pool', lambda q: q.memset(xt[slot][:], 0.0), w=[key])
                src = x_s[s] if l == 0 else xs[gblk * 128:gblk * 128 + 16, :]
                T.dma('sp', xt[slot][0:16, :], src, w=[key])
            return key

        def norm_block_to_hT(l, gblk, slot, gt, gkey, hdst, hkey):
            key = ('xt', slot)
            rmsnorm_rows(xt[slot][:], [key], D, 128, gt, gkey, xb[slot][:], ('xb', slot), 0)
            transpose_to(hdst, hkey, xb[slot], ('xb', slot), 8)

        open_scopes = []

        def CK(name):
            if stop == name:
                T.dead = True

        try:
          for l in range(2):
              bcast(gmix, 'gmix', norm_mix[l:l + 1, :], D)
              bcast(gq8, 'gq8', b_qnorm[l:l + 1, :], 64, scale=0.125)
              bcast(gk, 'gk', b_knorm[l:l + 1, :], 64)
              T.dma('sp', gb[:], gbT[l], w=['gb'])
              T.op('dve', lambda q: q.tensor_reduce(out=shiftc[:, 0:1], in_=gq8[:], axis=AX.X, op=ALU.max, apply_absolute_value=True), r=['gq8'], w=['shiftc'])
              T.op('dve', lambda q: q.tensor_reduce(out=shiftc[:, 1:2], in_=gk[:], axis=AX.X, op=ALU.max, apply_absolute_value=True), r=['gk', 'shiftc'], w=['shiftc'])
              T.op('dve', lambda q: q.tensor_scalar(out=shiftc[:, 2:3], in0=shiftc[:, 0:1], scalar1=shiftc[:, 1:2], scalar2=-64.0, op0=ALU.mult, op1=ALU.mult),
                   r=['shiftc'], w=['shiftc2'])
              nshift = shiftc[:, 2:3]

              s12 = ExitStack()
              s12.__enter__(); open_scopes.append(s12)
              hT = s12.enter_context(nc.sbuf_tensor(U("hT"), [128, 8, NTOK], BF16))
              for i_x in range(2):
                  xt[i_x] = s12.enter_context(nc.sbuf_tensor(U("xt"), [128, D], F32))
              for gblk in range(NBT):
                  slot = gblk & 1
                  x_block_load(l, gblk, slot)
                  norm_block_to_hT(l, gblk, slot, gmix, 'gmix', hT[:, :, gblk * 128:(gblk + 1) * 128], ('hT', gblk))

              CK('p1')
              with ExitStack() as sa:
                  def sba(name, shape, dt=F32):
                      return sa.enter_context(nc.sbuf_tensor(U(name), list(shape), dt))
                  w_au = sba("w_au", [128, 8, 512], BF16); w_av = sba("w_av", [128, 8, 512], BF16)
                  avn = sba("avn", [128, 512]); abias = sba("abias", [128, 512])
                  wsT = sba("wsT", [128, 4, 128], BF16); wsTf = sba("wsTf", [128, 4, 128])
                  bcast(avn, 'avn', a_vnorm[l:l + 1, :], 512)
                  bcast(abias, 'abias', a_bias[l:l + 1, :], 512)
                  T.dma('sp', wsTf[:], a_wsT[l], w=['wsTf'])
                  for g in range(4):
                      T.op('dve', lambda q: q.tensor_tensor(out=wsT[:, g, :], in0=wsTf[:, g, :], in1=cf[:, K2_GM:K2_GM + 128], op=ALU.mult),
                           r=['wsTf', 'cf'], w=['wsT'])
                  oaT = [sba(f"oaT{i}", [128, 4, 128], BF16) for i in range(2)]
                  vtm = [sba(f"vtm{i}", [128, 512], BF16) for i in range(2)]
                  uT = [sba(f"uT{i}", [128, 4, 128]) for i in range(2)]
                  T.dma('pool', w_au[:], wview(w_in, l, C_AU, C_AU + 512), w=['w_au'])
                  T.dma('pool', w_av[:], wview(w_in, l, C_AV, C_AV + 512), w=['w_av'])
                  for gblk in range(NBT):
                      sl = gblk & 1
                      hk = ('hT', gblk)
                      hcols = slice(gblk * 128, (gblk + 1) * 128)
                      b = mmbank()
                      for kc in range(8):
                          T.op('pe', lambda q: q.matmul(ps[b][:, :], lhsT=hT[:, kc, hcols], rhs=w_av[:, kc, :], start=(kc == 0), stop=(kc == 7)),
                               r=[hk, 'w_av'], w=[PK[b]])
                      gelu_to(wk[2][:, :], ('wk', 2), ps[b][:, :], [PK[b]], 128, 512, 0)
                      rmsnorm_rows(wk[2][:, :], [('wk', 2)], 512, 128, avn, 'avn', vtm[sl][:], ('vtm', sl), 1)
                      if gblk >= NBP:
                          s = gblk - NBP
                          T.op('dve', lambda q: q.scalar_tensor_tensor(out=stg[0][0:16, 0:512], in0=wk[2][0:16, :], scalar=col[0:16, 1:2], in1=avn[0:16, :],
                                                                       op0=ALU.mult, op1=ALU.mult), r=[('wk', 2), ('col', 1), 'avn'], w=[('stg', 0)])
                          T.dma('sp', o_av_s[l, s], stg[0][0:16, 0:512], r=[('stg', 0)])
                      b2 = mmbank()
                      for g in range(4):
                          for kc in range(8):
                              T.op('pe', lambda q: q.matmul(ps[b2][:, g * 128:(g + 1) * 128], lhsT=w_au[:, kc, g * 128:(g + 1) * 128], rhs=hT[:, kc, hcols],
                                                            start=(kc == 0), stop=(kc == 7)), r=[hk, 'w_au'], w=[PK[b2]])
                      gelu_to(uT[sl][:].rearrange("p g t -> p (g t)"), ('uT', sl), ps[b2][:, :], [PK[b2]], 128, 512, 0)
                      b3 = 2
                      for g in range(4):
                          T.op('pe', lambda q: q.matmul(ps[b3][:, g * 128:(g + 1) * 128], lhsT=vtm[sl][:, g * 128:(g + 1) * 128], rhs=wsT[:, g, :],
                                                        start=True, stop=True), r=[('vtm', sl), 'wsT'], w=[PK[b3]])
                      T.op('dve', lambda q: q.tensor_tensor(out=wk[2][:, :], in0=ps[b3][:, :], in1=abias[:, :], op=ALU.add), r=[PK[b3], 'abias'], w=[('wk', 2)])
                      T.op('dve', lambda q: q.tensor_tensor(out=oaT[sl][:].rearrange("p g t -> p (g t)"), in0=wk[2][:, :],
                                                            in1=uT[sl][:].rearrange("p g t -> p (g t)"), op=ALU.mult), r=[('wk', 2), ('uT', sl)], w=[('oaT', sl)])
                      T.dma('sp', oT_d[0, :, :, gblk * 128:(gblk + 1) * 128].rearrange("c p t -> p c t"), oaT[sl][:], r=[('oaT', sl)], w=[('oTd', 0, gblk)])
                  T.barrier()

              CK('pa')
              jobs = [dict(kind='p', nb=NBP, qblocks=list(range(NBP)), gbase=0)]
              for s in range(2):
                  jobs.append(dict(kind='s', s=s, nb=NBP + 1, qblocks=[NBP], gbase=None))

              def gcol(job, blk):
                  if job['kind'] == 'p':
                      return blk
                  assert blk == NBP
                  return NBP + job['s']

              for job in jobs:
                  nb = job['nb']
                  L = nb * 128
                  issamp = job['kind'] == 's'
                  comp_blocks = list(range(NBP)) if not issamp else [NBP]
                  with ExitStack() as sB:
                      def sbb(name, shape, dt=F32):
                          return sB.enter_context(nc.sbuf_tensor(U(name), list(shape), dt))
                      bkT = sbb("bkT", [128, L], BF16); bv2 = sbb("bv2", [128, nb, 128], BF16); ikT = sbb("ikT", [32, L], BF16)
                      w_k = sbb("w_k", [128, 8, 288], BF16); w_q = sbb("w_q", [128, 8, 512], BF16); w_i = sbb("w_i", [128, 8, 264], BF16)
                      scores = sbb("scores", [128, L]); maskb = sbb("maskb", [128, L], BF16); mneg2 = [sbb(f"mnegT{i}", [128, nb, 128], BF16) for i in range(2)] if not issamp else [sbb("mnegT0", [128, nb, 128], BF16)] * 2
                      kb = [sbb(f"kb{i}", [128, 288], BF16) for i in range(2)]
                      bqn = sbb("bqn", [128, 512], BF16); bqT2 = [sbb(f"bqT{i}", [128, 4, 128], BF16) for i in range(2)]
                      iqb = sbb("iqb", [128, 256], BF16); iqT = sbb("iqT", [32, 8, 128], BF16); wq = sbb("wq", [128, 8])
                      bis = sbb("bis", [128, 32]); obT = sbb("obT", [128, 4, 128], BF16); oraw = [sbb("oraw0", [128, 512])] * 2; draw = [sbb("draw0", [128, 512])] * 2
                      pB = [sbb(f"pB{i}", [128, 512], BF16) for i in range(2)]
                      T.dma('pool', w_k[:, :, 0:256], wview(w_in, l, C_BK, C_BK + 256), w=['w_k'])
                      T.dma('pool', w_k[:, :, 256:288], wview(w_in, l, C_IK, C_IK + 32), w=['w_k'])
                      T.dma('pool', w_q[:], wview(w_in, l, C_BQ, C_BQ + 512), w=['w_q'])
                      T.dma('pool', w_i[:, :, 0:256], wview(w_in, l, C_IQ, C_IQ + 256), w=['w_i'])
                      T.dma('pool', w_i[:, :, 256:264], wview(w_in, l, C_IW, C_IW + 8), w=['w_i'])

                      if issamp:
                          s = job['s']
                          ks8 = [sbb(f"ks8{i}", [128, 8, 160], BF16) for i in range(2)]
                          for gi in range(NBP // 8):
                              b0 = gi * 8
                              st = ks8[gi & 1]
                              rws = slice(b0 * 128, (b0 + 8) * 128)
                              T.dma('pool', st[:, :, 0:128], cbk[l, s, rws, :].rearrange("(b p) c -> p b c", p=128), w=[('ks8k', gi & 1)])
                              T.dma('pool', st[:, :, 128:160], cbi[l, s, rws, :].rearrange("(b p) c -> p b c", p=128), w=[('ks8i', gi & 1)])
                              T.dma('pool', bv2[:, b0:b0 + 8, :], cbv[l, s, rws, :].rearrange("(b p) c -> p b c", p=128), w=[('bv2', b_) for b_ in range(b0, b0 + 8)])
                              for j in range(8):
                                  blk = b0 + j
                                  T.op('pe', lambda q: q.transpose(pst[:, 0:128], st[:, j, 0:128], ident), r=[('ks8k', gi & 1)] + CB, w=[PT])
                                  T.op('pe', lambda q: q.transpose(pst[0:32, 128:256], st[:, j, 128:160], ident), r=[('ks8i', gi & 1)] + CB, w=[PT])
                                  T.op('act', lambda q: q.copy(out=bkT[:, blk * 128:(blk + 1) * 128], in_=pst[:, 0:128]), r=[PT], w=[('bkT', blk)])
                                  T.op('act', lambda q: q.copy(out=ikT[0:32, blk * 128:(blk + 1) * 128], in_=pst[0:32, 128:256]), r=[PT], w=[('ikT', blk)])
                      for blk in range(nb):
                          if issamp and blk < NBP:
                              continue
                          sl = blk & 1
                          kkey = ('kb', sl)
                          if blk in comp_blocks:
                              gc = gcol(job, blk)
                              hk = ('hT', gc)
                              hcols = slice(gc * 128, (gc + 1) * 128)
                              b = mmbank()
                              for kc in range(8):
                                  T.op('pe', lambda q: q.matmul(ps[b][:, 0:288], lhsT=hT[:, kc, hcols], rhs=w_k[:, kc, :], start=(kc == 0), stop=(kc == 7)),
                                       r=[hk, 'w_k'], w=[PK[b]])
                              T.op('act', lambda q: q.activation(out=wk[0][:, 0:128], in_=ps[b][:, 0:128], func=AF.Square), r=[PK[b]], w=[('wk', 0)])
                              T.op('dve', lambda q: q.tensor_reduce(out=col[:, 8:10], in_=wk[0][:, 0:128].rearrange("p (h d) -> p h d", h=2), axis=AX.X, op=ALU.add),
                                   r=[('wk', 0)], w=[('col', 8)])
                              T.op('act', lambda q: q.activation(out=col[:, 8:10], in_=col[:, 8:10], func=AF.Sqrt, bias=EPS, scale=1.0 / 64), r=[('col', 8)], w=[('col', 8)])
                              T.op('dve', lambda q: q.reciprocal(out=col[:, 8:10], in_=col[:, 8:10]), r=[('col', 8)], w=[('col', 8)])
                              so = stg[sl]
                              for h in range(2):
                                  T.op('dve', lambda q: q.scalar_tensor_tensor(out=so[:, h * 64:(h + 1) * 64], in0=ps[b][:, h * 64:(h + 1) * 64], scalar=col[:, 8 + h:9 + h],
                                                                               in1=gk[:, :], op0=ALU.mult, op1=ALU.mult), r=[PK[b], ('col', 8), 'gk'], w=[('stg', sl)])
                              T.op('act', lambda q: q.copy(out=so[:, 128:288], in_=ps[b][:, 128:288]), r=[PK[b]], w=[('stg', sl)])
                              T.op('dve', lambda q: q.tensor_copy(out=kb[sl][:, :], in_=so[:, 0:288]), r=[('stg', sl)], w=[kkey])
                              if not issamp:
                                  rows = slice(blk * 128, (blk + 1) * 128)
                                  T.dma('sp', o_bk_p[l, rows, :], so[:, 0:128], r=[('stg', sl)])
                                  T.dma('sp', o_bv_p[l, rows, :], so[:, 128:256], r=[('stg', sl)])
                                  T.dma('sp', o_bi_p[l, rows, :], so[:, 256:288], r=[('stg', sl)])
                              else:
                                  s = job['s']
                                  T.dma('sp', o_bk_s[l, s], so[0:16, 0:128], r=[('stg', sl)])
                                  T.dma('sp', o_bv_s[l, s], so[0:16, 128:256], r=[('stg', sl)])
                                  T.dma('sp', o_bi_s[l, s], so[0:16, 256:288], r=[('stg', sl)])
                          else:
                              s = job['s']
                              rows = slice(blk * 128, (blk + 1) * 128)
                              T.dma('pool', kb[sl][:, 0:128], cbk[l, s, rows, :], w=[kkey])
                              T.dma('pool', kb[sl][:, 128:256], cbv[l, s, rows, :], w=[kkey])
                              T.dma('pool', kb[sl][:, 256:288], cbi[l, s, rows, :], w=[kkey])
                          T.op('pe', lambda q: q.transpose(pst[:, 0:128], kb[sl][:, 0:128], ident), r=[kkey] + CB, w=[PT])
                          T.op('pe', lambda q: q.transpose(pst[0:32, 128:256], kb[sl][:, 256:288], ident), r=[kkey] + CB, w=[PT])
                          T.op('act', lambda q: q.copy(out=bkT[:, blk * 128:(blk + 1) * 128], in_=pst[:, 0:128]), r=[PT], w=[('bkT', blk)])
                          T.op('act', lambda q: q.copy(out=ikT[0:32, blk * 128:(blk + 1) * 128], in_=pst[0:32, 128:256]), r=[PT], w=[('ikT', blk)])
                          T.op('pool', lambda q: q.tensor_copy(out=bv2[:, blk, :], in_=kb[sl][:, 128:256]), r=[kkey], w=[('bv2', blk)])

                      CK('bk')
                      scrr = [0]

                      def stageX(qb, slot):
                              gc = gcol(job, qb)
                              hk = ('hT', gc)
                              hcols = slice(gc * 128, (gc + 1) * 128)
                              Lq = (qb + 1) * 128
                              nlb = qb + 1
                              bq_ = mmbank()
                              for kc in range(8):
                                  T.op('pe', lambda q: q.matmul(ps[bq_][:, :], lhsT=hT[:, kc, hcols], rhs=w_q[:, kc, :], start=(kc == 0), stop=(kc == 7)),
                                       r=[hk, 'w_q'], w=[PK[bq_]])
                              T.op('act', lambda q: q.activation(out=wk[0][:, :], in_=ps[bq_][:, :], func=AF.Square), r=[PK[bq_]], w=[('wk', 0)])
                              bi_ = mmbank()
                              for kc in range(8):
                                  T.op('pe', lambda q: q.matmul(ps[bi_][:, 0:264], lhsT=hT[:, kc, hcols], rhs=w_i[:, kc, :], start=(kc == 0), stop=(kc == 7)),
                                       r=[hk, 'w_i'], w=[PK[bi_]])
                              T.op('act', lambda q: q.copy(out=iqb[:, :], in_=ps[bi_][:, 0:256]), r=[PK[bi_]], w=['iqb'])
                              yield
                              T.op('dve', lambda q: q.tensor_reduce(out=col[:, 16:24], in_=wk[0][:, :].rearrange("p (h d) -> p h d", h=8), axis=AX.X, op=ALU.add),
                                   r=[('wk', 0)], w=[('col', 16)])
                              T.op('act', lambda q: q.activation(out=col[:, 16:24], in_=col[:, 16:24], func=AF.Sqrt, bias=EPS, scale=1.0 / 64), r=[('col', 16)], w=[('col', 16)])
                              T.op('dve', lambda q: q.reciprocal(out=col[:, 16:24], in_=col[:, 16:24]), r=[('col', 16)], w=[('col', 16)])
                              for h in range(8):
                                  T.op('dve', lambda q: q.scalar_tensor_tensor(out=bqn[:, (h % 4) * 128 + (h // 4) * 64:(h % 4) * 128 + (h // 4) * 64 + 64], in0=ps[bq_][:, h * 64:(h + 1) * 64], scalar=col[:, 16 + h:17 + h],
                                                                               in1=gq8[:, :], op0=ALU.mult, op1=ALU.mult), r=[PK[bq_], ('col', 16), 'gq8'], w=['bqn'])
                              transpose_to(bqT2[slot][:], ('bqT', slot), bqn, 'bqn', 4)
                              T.op('dve', lambda q: q.tensor_scalar(out=wq[:, :], in0=ps[bi_][:, 256:264], scalar1=(8.0 ** -0.5) * (32.0 ** -0.5), scalar2=None, op0=ALU.mult),
                                   r=[PK[bi_]], w=['wq'])
                              for h in range(8):
                                  T.op('pe', lambda q: q.transpose(pst[0:32, h * 128:(h + 1) * 128], iqb[:, h * 32:(h + 1) * 32], ident), r=['iqb'] + CB, w=[PT])
                              T.op('act', lambda q: q.copy(out=iqT[:], in_=pst[0:32, :].rearrange("p (h t) -> p h t", h=8)), r=[PT], w=['iqT'])
                              yield
                              for c0 in range(0, Lq, 512):
                                  c1 = min(Lq, c0 + 512)
                                  n = c1 - c0
                                  kdeps = [('ikT', bb) for bb in range(c0 // 128, c1 // 128)]
                                  seng = 'dve'
                                  for h in range(8):
                                      b = mmbank()
                                      T.op('pe', lambda q: q.matmul(ps[b][:, 0:n], lhsT=iqT[:, h, :], rhs=ikT[0:32, c0:c1], start=True, stop=True),
                                           r=['iqT'] + kdeps, w=[PK[b]])
                                      ws = scrr[0]; scrr[0] = (scrr[0] + 1) % 4
                                      T.op('act', lambda q: q.activation(out=wk[ws][:, 0:n], in_=ps[b][:, 0:n], func=AF.Relu), r=[PK[b]], w=[('wk', ws)])
                                      if h == 0:
                                          T.op(seng, lambda q: q.tensor_scalar(out=scores[:, c0:c1], in0=wk[ws][:, 0:n], scalar1=wq[:, 0:1], scalar2=None, op0=ALU.mult),
                                               r=[('wk', ws), 'wq'], w=[('sc', c0)])
                                      elif seng == 'dve':
                                          T.op('dve', lambda q: q.scalar_tensor_tensor(out=scores[:, c0:c1], in0=wk[ws][:, 0:n], scalar=wq[:, h:h + 1], in1=scores[:, c0:c1],
                                                                                       op0=ALU.mult, op1=ALU.add), r=[('wk', ws), 'wq', ('sc', c0)], w=[('sc', c0)])
                                      else:
                                          T.op('pool', lambda q: q.tensor_scalar(out=wk[ws][:, 0:n], in0=wk[ws][:, 0:n], scalar1=wq[:, h:h + 1], scalar2=None, op0=ALU.mult),
                                               r=[('wk', ws), 'wq'], w=[('wk', ws)])
                                          T.op('pool', lambda q: q.tensor_tensor(out=scores[:, c0:c1], in0=scores[:, c0:c1], in1=wk[ws][:, 0:n], op=ALU.add),
                                               r=[('wk', ws), ('sc', c0)], w=[('sc', c0)])
                              sck = [('sc', c0) for c0 in range(0, Lq, 512)]
                              T.op('dve', lambda q: q.tensor_reduce(out=bis[:, 0:1], in_=scores[:, 0:Lq], axis=AX.X, op=ALU.max, apply_absolute_value=True), r=sck, w=['bis'])
                              if not issamp:
                                  T.op('pool', lambda q: q.memset(scores[0:64, Lq - 64:Lq], -BIG), r=['bis'], w=sck)
                              else:
                                  T.op('pool', lambda q: q.memset(scores[:, SEQ + 16:Lq], -BIG), r=['bis'], w=sck)
                              if Lq > 256:
                                  T.op('dve', lambda q: q.tensor_scalar(out=bis[:, 0:1], in0=bis[:, 0:1], scalar1=1.0, scalar2=None, op0=ALU.add), r=['bis'], w=['bis'])
                                  T.op('dve', lambda q: q.tensor_scalar(out=bis[:, 2:3], in0=bis[:, 0:1], scalar1=0.0, scalar2=None, op0=ALU.mult), r=['bis'], w=['bis'])
                                  T.op('dve', lambda q: q.tensor_scalar(out=bis[:, 4:4 + NIT], in0=cf[:, K2_W2:K2_W2 + NIT], scalar1=bis[:, 0:1], scalar2=None, op0=ALU.mult),
                                       r=['bis', 'cf'], w=['bis'])
                                  for it in range(NIT):
                                      T.op('dve', lambda q: q.tensor_scalar(out=maskb[:, 0:Lq], in0=scores[:, 0:Lq], scalar1=bis[:, 2:3], scalar2=0.0, op0=ALU.is_ge, op1=ALU.add,
                                                                            accum_out=bis[:, 3:4]), r=['bis'] + sck, w=['bis', 'maskb'])
                                      T.op('dve', lambda q: q.tensor_scalar(out=bis[:, 3:4], in0=bis[:, 3:4], scalar1=255.5, scalar2=bis[:, 4 + it:5 + it], op0=ALU.is_ge, op1=ALU.mult),
                                           r=['bis'], w=['bis'])
                                      nx = min(it + 1, NIT - 1)
                                      oc_ = 1 if it == NIT - 1 else 2
                                      T.op('dve', lambda q: q.scalar_tensor_tensor(out=bis[:, oc_:oc_ + 1], in0=bis[:, 2:3], scalar=bis[:, 4 + nx:5 + nx], in1=bis[:, 3:4],
                                                                                   op0=ALU.subtract, op1=ALU.add), r=['bis'], w=['bis'])
                                  thr = bis[:, 1:2]
                                  T.op('dve', lambda q: q.tensor_scalar(out=maskb[:, 0:Lq], in0=scores[:, 0:Lq], scalar1=thr, scalar2=None, op0=ALU.is_ge), r=['bis'] + sck, w=['maskb'])
                              else:
                                  T.op('dve', lambda q: q.tensor_scalar(out=maskb[:, 0:Lq], in0=scores[:, 0:Lq], scalar1=-1.0e29, scalar2=None, op0=ALU.is_ge), r=sck, w=['maskb'])
                              CK('topk')
                              yield
                              for lb0 in range(0, nlb, 8):
                                  lb1 = min(nlb, lb0 + 8)
                                  for lb in range(lb0, lb1):
                                      T.op('pe', lambda q: q.transpose(pst[:, (lb - lb0) * 128:(lb - lb0 + 1) * 128], maskb[:, lb * 128:(lb + 1) * 128], ident),
                                           r=['maskb'] + CB, w=[PT])
                                  T.op('dve', lambda q: q.tensor_scalar(out=mneg2[slot][:, lb0:lb1, :], in0=pst[:, 0:(lb1 - lb0) * 128].rearrange("p (c t) -> p c t", c=lb1 - lb0),
                                                                        scalar1=1.0, scalar2=-NEG, op0=ALU.subtract, op1=ALU.mult), r=[PT], w=[('mnegT', slot)])

                      def stageY(qb, slot):
                              gc = gcol(job, qb)
                              nlb = qb + 1
                              for g in range(2):
                                  bo, bd = 4, 5
                                  prow = slice(g * 64, g * 64 + 64)

                                  def logits(lb):
                                      bz = 2 + (lb & 1)
                                      T.op('pe', lambda q: q.matmul(ps[bz][:, :], lhsT=bkT[prow, lb * 128:(lb + 1) * 128],
                                                                    rhs=bqT2[slot][prow, :, :], start=True, stop=False), r=[('bkT', lb), ('bqT', slot)], w=[PK[bz]])
                                      for hh in range(4):
                                          T.op('pe', lambda q: q.matmul(ps[bz][:, hh * 128:(hh + 1) * 128], lhsT=ident, rhs=mneg2[slot][:, lb, :], start=False, stop=(hh == 3)),
                                               r=[('mnegT', slot)] + CB, w=[PK[bz]])
                                      T.op('act', lambda q: q.activation(out=pB[lb & 1][:, :], in_=ps[bz][:, :], func=AF.Exp, bias=nshift, scale=1.0),
                                           r=[PK[bz], 'shiftc2'], w=[('pB', lb & 1)])
                                  logits(0)
                                  for lb in range(nlb):
                                      sl = lb & 1
                                      if lb + 1 < nlb:
                                          logits(lb + 1)
                                      T.op('pe', lambda q: q.matmul(ps[bo][:, :], lhsT=bv2[:, lb, :], rhs=pB[sl][:, :], start=(lb == 0), stop=(lb == nlb - 1)),
                                           r=[('bv2', lb), ('pB', sl)], w=[PK[bo]])
                                      T.op('pe', lambda q: q.matmul(ps[bd][:, :], lhsT=onesb, rhs=pB[sl][:, :], start=(lb == 0), stop=(lb == nlb - 1)),
                                           r=[('pB', sl)] + CB, w=[PK[bd]])
                                  T.op('act', lambda q: q.copy(out=oraw[g][prow, :], in_=ps[bo][prow, :]), r=[PK[bo]], w=['oraw'])
                                  T.op('act', lambda q: q.activation(out=draw[g][prow, :], in_=ps[bd][prow, :], func=AF.Ln, bias=1e-30, scale=1.0), r=[PK[bd]], w=['draw'])
                                  T.op('act', lambda q: q.activation(out=draw[g][prow, :], in_=draw[g][prow, :], func=AF.Exp, scale=-1.0), r=['draw'], w=['draw'])
                                  T.op('pool', lambda q: q.tensor_tensor(out=obT[prow, :, :].rearrange("p c t -> p (c t)"), in0=oraw[g][prow, :], in1=draw[g][prow, :], op=ALU.mult),
                                       r=['oraw', 'draw'], w=['obT'])
                              T.dma('sp', oT_d[1, :, :, gc * 128:(gc + 1) * 128].rearrange("c p t -> p c t"), obT[:], r=['obT'], w=[('oTd', 1, gc)])

                      qbs = job['qblocks']
                      nq_ = len(qbs)
                      gens = [stageX(qbs[i], i & 1) for i in range(nq_)]
                      next(gens[0]); next(gens[0]); next(gens[0])
                      if nq_ > 1:
                          next(gens[1])
                      next(gens[0], None)
                      if nq_ > 1:
                          next(gens[1])
                      for i in range(1, nq_):
                          next(gens[i])
                          if i + 1 < nq_:
                              next(gens[i + 1])
                          stageY(qbs[i - 1], (i - 1) & 1)
                          next(gens[i], None)
                          if i + 1 < nq_:
                              next(gens[i + 1])
                      stageY(qbs[-1], (nq_ - 1) & 1)
                      T.barrier()

                  CK('B')
                  with ExitStack() as sC:
                      def sbc(name, shape, dt=F32):
                          return sC.enter_context(nc.sbuf_tensor(U(name), list(shape), dt))
                      nq = 128 * len(job['qblocks'])
                      ckT = sbc("ckT", [128, L], BF16); cvt = sbc("cvt", [128, nb, 128], BF16); cqT = sbc("cqT", [128, nq], BF16)
                      w_c = sbc("w_c", [128, 8, 384], BF16); kc2 = [sbc(f"kc2{i}", [128, 128], BF16) for i in range(2)]
                      kcs = [sbc(f"kcs{i}", [128, 8, 128], BF16) for i in range(2)] if issamp else None
                      ocT = [sbc(f"ocT{i}", [128, 512], BF16) for i in range(2)]
                      Gt = [sbc(f"Gt{i}", [128, 512]) for i in range(2)]; wvt = [sbc(f"wvt{i}", [128, 512], BF16) for i in range(4)]
                      for hp in range(4):
                          T.dma('pool', w_c[:, :, 0:128], wview(w_in, l, C_CQ + hp * 128, C_CQ + (hp + 1) * 128), w=['w_c'])
                          T.dma('pool', w_c[:, :, 128:256], wview(w_in, l, C_CK + hp * 128, C_CK + (hp + 1) * 128), w=['w_c'])
                          T.dma('pool', w_c[:, :, 256:384], wview(w_in, l, C_CV + hp * 128, C_CV + (hp + 1) * 128), w=['w_c'])
                          if issamp:
                              s = job['s']
                              for gi in range(NBP // 8):
                                  b0 = gi * 8
                                  st = kcs[gi & 1]
                                  rws = slice(b0 * 128, (b0 + 8) * 128)
                                  T.dma('pool', st[:], cck[l, s, rws, hp * 128:(hp + 1) * 128].rearrange("(b p) c -> p b c", p=128), w=[('kcs', gi & 1)])
                                  T.dma('pool', cvt[:, b0:b0 + 8, :], ccv[l, s, rws, hp * 128:(hp + 1) * 128].rearrange("(b p) c -> p b c", p=128),
                                        w=[('cvt', b_) for b_ in range(b0, b0 + 8)])
                                  for j in range(8):
                                      blk = b0 + j
                                      T.op('pe', lambda q: q.transpose(pst[:, (j & 1) * 128:(j & 1) * 128 + 128], st[:, j, :], ident), r=[('kcs', gi & 1)] + CB, w=[PT])
                                      T.op('act', lambda q: q.copy(out=ckT[:, blk * 128:(blk + 1) * 128], in_=pst[:, (j & 1) * 128:(j & 1) * 128 + 128]), r=[PT], w=[('ckT', blk)])
                          for blk in range(nb):
                              if issamp and blk < NBP:
                                  continue
                              sl = blk & 1
                              kkey = ('kc2', sl)
                              if blk in comp_blocks:
                                  gc = gcol(job, blk)
                                  hk = ('hT', gc)
                                  hcols = slice(gc * 128, (gc + 1) * 128)
                                  b = mmbank()
                                  for kc in range(8):
                                      T.op('pe', lambda q: q.matmul(ps[b][:, 0:256], lhsT=hT[:, kc, hcols], rhs=w_c[:, kc, 128:384], start=(kc == 0), stop=(kc == 7)),
                                           r=[hk, 'w_c'], w=[PK[b]])
                                  so = stg[sl]
                                  T.op('act', lambda q: q.copy(out=so[:, 0:256], in_=ps[b][:, 0:256]), r=[PK[b]], w=[('stg', sl)])
                                  T.op('dve', lambda q: q.tensor_copy(out=kc2[sl][:, :], in_=so[:, 0:128]), r=[('stg', sl)], w=[kkey])
                                  T.op('pool', lambda q: q.tensor_copy(out=cvt[:, blk, :], in_=so[:, 128:256]), r=[('stg', sl)], w=[('cvt', blk)])
                                  if not issamp:
                                      rows = slice(blk * 128, (blk + 1) * 128)
                                      T.dma('sp', o_ck_p[l, rows, hp * 128:(hp + 1) * 128], so[:, 0:128], r=[('stg', sl)])
                                      T.dma('sp', o_cv_p[l, rows, hp * 128:(hp + 1) * 128], so[:, 128:256], r=[('stg', sl)])
                                  else:
                                      s = job['s']
                                      T.dma('sp', o_ck_s[l, s, :, hp * 128:(hp + 1) * 128], so[0:16, 0:128], r=[('stg', sl)])
                                      T.dma('sp', o_cv_s[l, s, :, hp * 128:(hp + 1) * 128], so[0:16, 128:256], r=[('stg', sl)])
                              else:
                                  s = job['s']
                                  rows = slice(blk * 128, (blk + 1) * 128)
                                  T.dma('pool', kc2[sl][:, :], cck[l, s, rows, hp * 128:(hp + 1) * 128], w=[kkey])
                                  T.dma('pool', cvt[:, blk, :], ccv[l, s, rows, hp * 128:(hp + 1) * 128], w=[('cvt', blk)])
                              T.op('pe', lambda q: q.transpose(pst[:, 0:128], kc2[sl][:, :], ident), r=[kkey] + CB, w=[PT])
                              T.op('act', lambda q: q.copy(out=ckT[:, blk * 128:(blk + 1) * 128], in_=pst[:, 0:128]), r=[PT], w=[('ckT', blk)])
                          for qi, qb in enumerate(job['qblocks']):
                              gc = gcol(job, qb)
                              b = mmbank()
                              for kc in range(8):
                                  T.op('pe', lambda q: q.matmul(ps[b][:, 0:128], lhsT=w_c[:, kc, 0:128], rhs=hT[:, kc, gc * 128:(gc + 1) * 128], start=(kc == 0), stop=(kc == 7)),
                                       r=[('hT', gc), 'w_c'], w=[PK[b]])
                              T.op('act', lambda q: q.activation(out=cqT[:, qi * 128:(qi + 1) * 128], in_=ps[b][:, 0:128], func=AF.Copy, scale=0.125), r=[PK[b]], w=[('cqT', qi)])
                          qtiles = [job['qblocks'][i:i + 4] for i in range(0, len(job['qblocks']), 4)]
                          for ti, qt in enumerate(qtiles):
                              n = 128 * len(qt)
                              qc0 = ti * 512
                              qkeys = [('cqT', ti * 4 + i) for i in range(len(qt))]
                              nlb = qt[-1] + 1
                              osl = ti & 1
                              order = list(range(nlb - 1, -1, -1))

                              def stageA(lb, buf):
                                  diag = lb >= qt[0]
                                  for e2 in range(2):
                                      prow = slice(e2 * 64, e2 * 64 + 64)
                                      T.op('pe', lambda q: q.matmul(ps[e2][:, 0:n], lhsT=ckT[prow, lb * 128:(lb + 1) * 128], rhs=cqT[prow, qc0:qc0 + n], start=True, stop=(not diag)),
                                           r=[('ckT', lb)] + qkeys, w=[PK[e2]])
                                      if diag:
                                          o = lb - qt[0]
                                          T.op('pe', lambda q: q.matmul(ps[e2][:, 0:n], lhsT=ident, rhs=cb[:, K_DM + 512 * o:K_DM + 512 * o + n], start=False, stop=True),
                                               r=CB, w=[PK[e2]])
                                  for e2 in range(2):
                                      et = wk[2 * e2 + buf]; sp = wkb[2 * e2 + buf]
                                      T.op('act', lambda q: q.activation(out=et[:, 0:n], in_=ps[e2][:, 0:n], func=AF.Exp), r=[PK[e2]], w=[('wk', 2 * e2 + buf)])
                                      T.op('act', lambda q: q.activation(out=sp[:, 0:n], in_=et[:, 0:n], func=AF.Ln, bias=1.0, scale=1.0), r=[('wk', 2 * e2 + buf)], w=[('wkb', 2 * e2 + buf)])

                              def stageB(lb, buf, first):
                                  for e2 in range(2):
                                      sp = wkb[2 * e2 + buf]
                                      T.op('pe', lambda q: q.matmul(ps[2 + e2][:, 0:n], lhsT=negU, rhs=sp[:, 0:n], start=first, stop=True, skip_group_check=True), r=[('wkb', 2 * e2 + buf)] + CB, w=[PK[2 + e2]])
                                  for e2 in range(2):
                                      et = wk[2 * e2 + buf]
                                      T.op('act', lambda q: q.activation(out=Gt[e2][:, 0:n], in_=ps[2 + e2][:, 0:n], func=AF.Exp), r=[PK[2 + e2]], w=[('Gt', e2)])
                                      T.op('dve' if e2 == 0 else 'pool', lambda q: q.tensor_tensor(out=wvt[2 * e2 + buf][:, 0:n], in0=et[:, 0:n], in1=Gt[e2][:, 0:n], op=ALU.mult),
                                           r=[('wk', 2 * e2 + buf), ('Gt', e2)], w=[('wvt', 2 * e2 + buf)])
                                  for e2 in range(2):
                                      sp = wkb[2 * e2 + buf]
                                      T.op('pe', lambda q: q.matmul(ps[2 + e2][:, 0:n], lhsT=negL, rhs=sp[:, 0:n], start=False, stop=True, skip_group_check=True), r=[('wkb', 2 * e2 + buf)] + CB, w=[PK[2 + e2]])

                              def stagePV(lb, buf, first, last):
                                  for e2 in range(2):
                                      T.op('pe', lambda q: q.matmul(ps[4 + e2][:, 0:n], lhsT=cvt[:, lb, :], rhs=wvt[2 * e2 + buf][:, 0:n], start=first, stop=last),
                                           r=[('cvt', lb), ('wvt', 2 * e2 + buf)], w=[PK[4 + e2]])

                              no = len(order)
                              stageA(order[0], 0)
                              for i_, lb in enumerate(order):
                                  if i_ + 1 < no:
                                      stageA(order[i_ + 1], (i_ + 1) & 1)
                                  stageB(lb, i_ & 1, i_ == 0)
                                  if i_ >= 1:
                                      stagePV(order[i_ - 1], (i_ - 1) & 1, i_ == 1, False)
                              stagePV(order[no - 1], (no - 1) & 1, no == 1, True)
                              for e2 in range(2):
                                  prow = slice(e2 * 64, e2 * 64 + 64)
                                  T.op('act', lambda q: q.copy(out=ocT[osl][prow, 0:n], in_=ps[4 + e2][prow, 0:n]), r=[PK[4 + e2]], w=[('ocT', osl)])
                              for i, qb in enumerate(qt):
                                  gc = gcol(job, qb)
                                  T.dma('sp', oT_d[2, hp, :, gc * 128:(gc + 1) * 128], ocT[osl][:, i * 128:(i + 1) * 128], r=[('ocT', osl)], w=[('oTd', 2, gc)])
                      T.barrier()

              s12.__exit__(None, None, None); open_scopes.pop()
              CK('jobs')
              groups = [list(range(0, 8)), list(range(8, 16)), list(range(16, 24)), list(range(24, NBT))]
              with ExitStack() as s3:
                  def sb3(name, shape, dt=F32):
                      return s3.enter_context(nc.sbuf_tensor(U(name), list(shape), dt))
                  NG = 10
                  xg = sb3("xg", [128, NG, D]); hg = sb3("hg", [128, 8, NG * 128], BF16)
                  mT = sb3("mT", [128, 8, NG * 128], BF16)
                  gffn = sb3("gffn", [128, D]); gple = sb3("gple", [128, D])
                  bcast(gffn, 'gffn', norm_ffn[l:l + 1, :], D)
                  bcast(gple, 'gple', norm_ple[l:l + 1, :], D)
                  wg = sb3("wg", [128, 8, 1024], BF16); wb_ = sb3("wb_", [128, 4, 1024], BF16)
                  for grp in groups:
                      ng = len(grp)
                      ntok = ng * 128
                      tiles = [(c0, min(ntok, c0 + 512)) for c0 in range(0, ntok, 512)]
                      for i, gblk in enumerate(grp):
                          if gblk < NBP:
                              T.dma('sp', xg[:, i, :], (x_p if l == 0 else xs)[gblk * 128:(gblk + 1) * 128, :], w=[('xg', i)])
                          else:
                              s = gblk - NBP
                              T.op('pool', lambda q: q.memset(xg[:, i, :], 0.0), w=[('xg', i)])
                              T.dma('sp', xg[0:16, i, :], x_s[s] if l == 0 else xs[gblk * 128:gblk * 128 + 16, :], w=[('xg', i)])
                          sl = i & 1
                          rmsnorm_rows(xg[:, i, :], [('xg', i)], D, 128, gmix, 'gmix', xb[sl][:], ('xb', sl), 0)
                          transpose_to(hg[:, :, i * 128:(i + 1) * 128], ('hg', i), xb[sl], ('xb', sl), 8)
                      hkeys = [('hg', i) for i in range(ng)]
                      sM = ExitStack(); sM.__enter__(); open_scopes.append(sM)
                      og = sM.enter_context(nc.sbuf_tensor(U("og"), [128, 4, NG * 128], BF16))
                      macc = sM.enter_context(nc.sbuf_tensor(U("macc"), [128, 8, NG * 128], F32))
                      for br in range(3):
                          T.dma('pool', wg[:], wview(w_in, l, C_GL + br * 1024, C_GL + (br + 1) * 1024), w=['wg'])
                          if br == 1:
                              for g2 in range(2):
                                  T.dma('pool', wb_[g2 * 64:(g2 + 1) * 64, :, :], w_br[br][l, g2 * 256:(g2 + 1) * 256, :].rearrange("(c d) n -> d c n", d=64), w=['wb_'])
                          else:
                              T.dma('pool', wb_[:], w_br[br][l].rearrange("(kc p) n -> p kc n", p=128), w=['wb_'])
                          T.dma('sp', og[:, :, 0:ntok], oT_d[br, :, :, grp[0] * 128:grp[0] * 128 + ntok].rearrange("c p t -> p c t"),
                                r=[('oTd', br, g_) for g_ in grp], w=['og'])
                          for cc in range(8):
                              for (t0, t1) in tiles:
                                  n = t1 - t0
                                  b = mmbank()
                                  for kc in range(8):
                                      T.op('pe', lambda q: q.matmul(ps[b][:, 0:n], lhsT=wg[:, kc, cc * 128:(cc + 1) * 128], rhs=hg[:, kc, t0:t1], start=(kc == 0), stop=(kc == 7)),
                                           r=['wg'] + hkeys, w=[PK[b]])
                                  T.op('act', lambda q: q.activation(out=wk[b][:, 0:n], in_=ps[b][:, 0:n], func=AF.Sigmoid, bias=gb[:, br * 8 + cc:br * 8 + cc + 1], scale=1.0),
                                       r=[PK[b], 'gb'], w=[('wk', b)])
                                  b2 = 2 + b
                                  for kc in range(4):
                                      T.op('pe', lambda q: q.matmul(ps[b2][:, 0:n], lhsT=wb_[:, kc, cc * 128:(cc + 1) * 128], rhs=og[:, kc, t0:t1], start=(kc == 0), stop=(kc == 3)),
                                           r=['wb_', 'og'], w=[PK[b2]])
                                  mk = ('macc', cc, t0)
                                  if br == 0:
                                      T.op('dve', lambda q: q.tensor_tensor(out=macc[:, cc, t0:t1], in0=ps[b2][:, 0:n], in1=wk[b][:, 0:n], op=ALU.mult), r=[PK[b2], ('wk', b)], w=[mk])
                                  else:
                                      T.op('dve', lambda q: q.tensor_tensor(out=wk[2 + b][:, 0:n], in0=ps[b2][:, 0:n], in1=wk[b][:, 0:n], op=ALU.mult), r=[PK[b2], ('wk', b)], w=[('wk', 2 + b)])
                                      if br == 1:
                                          T.op('pool', lambda q: q.tensor_tensor(out=macc[:, cc, t0:t1], in0=macc[:, cc, t0:t1], in1=wk[2 + b][:, 0:n], op=ALU.add), r=[mk, ('wk', 2 + b)], w=[mk])
                                      else:
                                          T.op('pool', lambda q: q.tensor_tensor(out=mT[:, cc, t0:t1], in0=macc[:, cc, t0:t1], in1=wk[2 + b][:, 0:n], op=ALU.add), r=[mk, ('wk', 2 + b)], w=[('mT', cc, t0)])
                      mkeys = [('mT', cc, t0) for cc in range(8) for (t0, _) in tiles]
                      T.barrier()
                      sM.__exit__(None, None, None); open_scopes.pop()
                      sF = ExitStack(); sF.__enter__(); open_scopes.append(sF)
                      actT = sF.enter_context(nc.sbuf_tensor(U("actT"), [128, 22, NG * 128], BF16))

                      def tok_major_update(wsrc3, wkey_unused, lhs_buf, lhs_keys, nkc, post):
                          for i in range(ng):
                              for half in range(2):
                                  b = mmbank()
                                  for kc in range(nkc):
                                      T.op('pe', lambda q: q.matmul(ps[b][:, :], lhsT=lhs_buf[:, kc, i * 128:(i + 1) * 128], rhs=wsrc3[:, kc, half * 512:(half + 1) * 512],
                                                                    start=(kc == 0), stop=(kc == nkc - 1)), r=lhs_keys + [wkey_unused], w=[PK[b]])
                                  post(i, half, b)

                      T.dma('pool', wg[:], w_out[l].rearrange("(kc p) n -> p kc n", p=128), w=['wg'])

                      def post_add(i, half, b):
                          T.op('dve', lambda q: q.tensor_tensor(out=xg[:, i, half * 512:(half + 1) * 512], in0=xg[:, i, half * 512:(half + 1) * 512], in1=ps[b][:, :], op=ALU.add),
                               r=[PK[b], ('xg', i)], w=[('xg', i)])
                      tok_major_update(wg, 'wg', mT, mkeys, 8, post_add)
                      for i in range(ng):
                          sl = i & 1
                          rmsnorm_rows(xg[:, i, :], [('xg', i)], D, 128, gffn, 'gffn', xb[sl][:], ('xb', sl), 0)
                          transpose_to(hg[:, :, i * 128:(i + 1) * 128], ('hg', i), xb[sl], ('xb', sl), 8)
                      T.barrier()
                      for s0 in range(0, DFF, 256):
                          pp = (s0 // 256) & 1
                          co = pp * 256
                          wkey = ('wgp', pp)
                          T.dma('pool', wg[:, :, co:co + 256], wview(w_ffn_in, l, s0, s0 + 256), w=[wkey])
                          T.dma('pool', wg[:, :, 512 + co:512 + co + 256], wview(w_ffn_in, l, DFF + s0, DFF + s0 + 256), w=[wkey])
                          for jj in range(2):
                              j = s0 // 128 + jj
                              for (t0, t1) in tiles:
                                  n = t1 - t0
                                  b = mmbank(); b2 = 2 + b
                                  for kc in range(8):
                                      T.op('pe', lambda q: q.matmul(ps[b][:, 0:n], lhsT=wg[:, kc, co + jj * 128:co + (jj + 1) * 128], rhs=hg[:, kc, t0:t1], start=(kc == 0), stop=(kc == 7)),
                                           r=[wkey] + hkeys, w=[PK[b]])
                                  for kc in range(8):
                                      T.op('pe', lambda q: q.matmul(ps[b2][:, 0:n], lhsT=wg[:, kc, 512 + co + jj * 128:512 + co + (jj + 1) * 128], rhs=hg[:, kc, t0:t1], start=(kc == 0), stop=(kc == 7)),
                                           r=[wkey] + hkeys, w=[PK[b2]])
                                  T.op('act', lambda q: q.activation(out=wk[b][:, 0:n], in_=ps[b][:, 0:n], func=AF.Silu), r=[PK[b]], w=[('wk', b)])
                                  T.op('dve', lambda q: q.tensor_tensor(out=actT[:, j, t0:t1], in0=ps[b2][:, 0:n], in1=wk[b][:, 0:n], op=ALU.mult), r=[PK[b2], ('wk', b)], w=[('actT', j, t0)])
                      akeys = [('actT', j, t0) for j in range(22) for (t0, _) in tiles]
                      T.barrier()
                      so_i = 0
                      for half in range(2):
                          for k0 in range(0, 22, 8):
                              k1 = min(22, k0 + 8)
                              qq = so_i & 1; so_i += 1
                              okey = ('wgo', qq)
                              T.dma('pool', wg[:, 0:k1 - k0, qq * 512:(qq + 1) * 512], w_ffn_out[l, k0 * 128:k1 * 128, half * 512:(half + 1) * 512].rearrange("(kc p) n -> p kc n", p=128), w=[okey])
                              for i in range(ng):
                                  b = mmbank()
                                  for kc in range(k0, k1):
                                      T.op('pe', lambda q: q.matmul(ps[b][:, :], lhsT=actT[:, kc, i * 128:(i + 1) * 128], rhs=wg[:, kc - k0, qq * 512:(qq + 1) * 512], start=(kc == k0), stop=(kc == k1 - 1)),
                                           r=akeys + [okey], w=[PK[b]])
                                  post_add(i, half, b)
                      T.barrier()
                      sF.__exit__(None, None, None); open_scopes.pop()
                      sP = ExitStack(); sP.__enter__(); open_scopes.append(sP)
                      pT = sP.enter_context(nc.sbuf_tensor(U("pT"), [128, 2, NG * 128], BF16))
                      pt32 = sP.enter_context(nc.sbuf_tensor(U("pt32"), [128, 256], F32))
                      ptb = sP.enter_context(nc.sbuf_tensor(U("ptb"), [128, 256], BF16))
                      for i, gblk in enumerate(grp):
                          sl = i & 1
                          rmsnorm_rows(xg[:, i, :], [('xg', i)], D, 128, gple, 'gple', xb[sl][:], ('xb', sl), 0)
                          transpose_to(hg[:, :, i * 128:(i + 1) * 128], ('hg', i), xb[sl], ('xb', sl), 8)
                          if gblk < NBP:
                              T.dma('sp', pt32[:, :], p_p[l, gblk * 128:(gblk + 1) * 128, :], w=['pt32'])
                          else:
                              T.op('pool', lambda q: q.memset(pt32[:, :], 0.0), w=['pt32'])
                              T.dma('sp', pt32[0:16, :], p_s[l, gblk - NBP], w=['pt32'])
                          T.op('dve', lambda q: q.tensor_copy(out=ptb[:, :], in_=pt32[:, :]), r=['pt32'], w=['ptb'])
                          transpose_to(pT[:, :, i * 128:(i + 1) * 128], ('pT', i), ptb, 'ptb', 2)
                      T.dma('pool', wg[:], w_ple_gate[l].rearrange("(kc p) n -> p kc n", p=128), w=['wg'])
                      T.dma('pool', wb_[:, 0:2, :], w_ple_proj[l].rearrange("(kc p) n -> p kc n", p=128), w=['wb_'])
                      for i in range(ng):
                          for half in range(2):
                              b = mmbank(); b2 = 2 + b
                              for kc in range(8):
                                  T.op('pe', lambda q: q.matmul(ps[b][:, :], lhsT=hg[:, kc, i * 128:(i + 1) * 128], rhs=wg[:, kc, half * 512:(half + 1) * 512], start=(kc == 0), stop=(kc == 7)),
                                       r=[('hg', i), 'wg'], w=[PK[b]])
                              for kc in range(2):
                                  T.op('pe', lambda q: q.matmul(ps[b2][:, :], lhsT=pT[:, kc, i * 128:(i + 1) * 128], rhs=wb_[:, kc, half * 512:(half + 1) * 512], start=(kc == 0), stop=(kc == 1)),
                                       r=[('pT', i), 'wb_'], w=[PK[b2]])
                              T.op('act', lambda q: q.activation(out=wk[b][:, :], in_=ps[b][:, :], func=AF.Sigmoid), r=[PK[b]], w=[('wk', b)])
                              T.op('dve', lambda q: q.tensor_tensor(out=wk[b][:, :], in0=ps[b2][:, :], in1=wk[b][:, :], op=ALU.mult), r=[PK[b2], ('wk', b)], w=[('wk', b)])
                              T.op('pool', lambda q: q.tensor_tensor(out=xg[:, i, half * 512:(half + 1) * 512], in0=xg[:, i, half * 512:(half + 1) * 512], in1=wk[b][:, :], op=ALU.add),
                                   r=[('wk', b), ('xg', i)], w=[('xg', i)])
                      T.barrier()
                      sP.__exit__(None, None, None); open_scopes.pop()
                      for i, gblk in enumerate(grp):
                          if l == 0:
                              T.dma('sp', xs[gblk * 128:(gblk + 1) * 128, :], xg[:, i, :], r=[('xg', i)], w=[('xs', gblk)])
                          elif gblk < NBP:
                              T.dma('sp', y_p[gblk * 128:(gblk + 1) * 128, :], xg[:, i, :], r=[('xg', i)])
                          else:
                              T.dma('sp', y_s[gblk - NBP], xg[0:16, i, :], r=[('xg', i)])
                  T.barrier()
              CK('L0')

        except _Stop:
            for sc in reversed(open_scopes):
                sc.__exit__(None, None, None)
        T.dead = False
        T.barrier()
        print("instructions:", T.n_inst, "sems:", len(T.sems))
    return nc


_NC_CACHE = {}


def _make_maps(inp):
    f = lambda a: np.ascontiguousarray(np.asarray(a, dtype=np.float32))
    cst, cst2 = _consts()
    shared = {
        'norm_mix': f(inp['norm_mix']), 'w_in': f(inp['w_in']),
        'gbT': f(np.asarray(inp['gate_bias']).reshape(2, 24, 128).transpose(0, 2, 1)),
        'a_vnorm': f(inp['a_vnorm']), 'a_wsT': f(np.asarray(inp['a_ws']).transpose(0, 3, 1, 2)),
        'a_bias': f(np.asarray(inp['a_bias']).reshape(2, 512)),
        'b_qnorm': f(inp['b_qnorm']), 'b_knorm': f(inp['b_knorm']),
        'w_br_a': f(inp['w_br_a']), 'w_br_b': f(inp['w_br_b']), 'w_br_c': f(inp['w_br_c']),
        'w_out': f(inp['w_out']), 'norm_ffn': f(inp['norm_ffn']), 'w_ffn_in': f(inp['w_ffn_in']), 'w_ffn_out': f(inp['w_ffn_out']),
        'norm_ple': f(inp['norm_ple']), 'w_ple_gate': f(inp['w_ple_gate']), 'w_ple_proj': f(inp['w_ple_proj']), 'cst': cst, 'cst2': cst2,
    }
    xp = np.asarray(inp['x_prompt']); xsm = np.asarray(inp['x_sample'])
    in_maps = []
    for c in range(8):
        b = c % 4
        ss = slice(2 * c, 2 * c + 2)
        m = dict(shared)
        m['x_p'] = f(xp[b]); m['x_s'] = f(xsm[ss])
        m['cbk'] = f(np.asarray(inp['cache_b_k'])[:, ss].reshape(2, 2, SEQ, 128))
        m['cbv'] = f(np.asarray(inp['cache_b_v'])[:, ss].reshape(2, 2, SEQ, 128))
        m['cbi'] = f(np.asarray(inp['cache_b_kidx'])[:, ss])
        m['cck'] = f(np.asarray(inp['cache_c_k'])[:, ss].reshape(2, 2, SEQ, 512))
        m['ccv'] = f(np.asarray(inp['cache_c_v'])[:, ss].reshape(2, 2, SEQ, 512))
        m['p_p'] = f(np.asarray(inp['p_prompt'])[:, b]); m['p_s'] = f(np.asarray(inp['p_sample'])[:, ss])
        in_maps.append(m)
    return in_maps


def kernel(**inp):
    if 'nc' not in _NC_CACHE:
        _NC_CACHE['nc'] = build_program()
    nc = _NC_CACHE['nc']
    in_maps = _make_maps(inp)
    res = run_bass_kernel_spmd(nc, in_maps, core_ids=list(range(8))).results
    st = lambda name, cores: np.stack([np.asarray(res[c][name]) for c in cores], axis=1)
    P = range(4); A = range(8)
    y_prompt = np.stack([res[c]['y_p'] for c in P], 0).astype(np.float32)
    y_sample = np.concatenate([res[c]['y_s'] for c in A], 0).astype(np.float32)
    cat_s = lambda name: np.concatenate([np.asarray(res[c][name]) for c in A], axis=1)
    outs = (
        y_prompt, y_sample,
        st('o_bk_p', P).reshape(2, 4, SEQ, 2, 64), st('o_bv_p', P).reshape(2, 4, SEQ, 2, 64), st('o_bi_p', P).reshape(2, 4, SEQ, 32),
        st('o_ck_p', P).reshape(2, 4, SEQ, 8, 64), st('o_cv_p', P).reshape(2, 4, SEQ, 8, 64),
        cat_s('o_bk_s').reshape(2, 16, 16, 2, 64), cat_s('o_bv_s').reshape(2, 16, 16, 2, 64), cat_s('o_bi_s').reshape(2, 16, 16, 32),
        cat_s('o_ck_s').reshape(2, 16, 16, 8, 64), cat_s('o_cv_s').reshape(2, 16, 16, 8, 64), cat_s('o_av_s').reshape(2, 16, 16, 512),
    )
    return tuple(np.ascontiguousarray(o, dtype=np.float32) for o in outs)
```

```python
import numpy as np
from contextlib import ExitStack
import concourse.bass as bass
import concourse.mybir as mybir
from concourse.bass_utils import run_bass_kernel_spmd

F32 = mybir.dt.float32
BF16 = mybir.dt.bfloat16
AF = mybir.ActivationFunctionType
ALU = mybir.AluOpType
AX = mybir.AxisListType

D = 1024
SEQ = 4096
NBP = 32
NBT = 34
NTOK = NBT * 128
DFF = 2816
EPS = 1e-6
NEG = -30000.0
BIG = 1.0e30
NIT = 24
C_AU, C_AV, C_BQ, C_BK, C_BV, C_IQ, C_IK, C_IW, C_CQ, C_CK, C_CV, C_GL = (
    0, 512, 1024, 1536, 1664, 1792, 2048, 2080, 2088, 2600, 3112, 3624)
K_ID, K_NU, K_NL, K_ONE, K_DM, K_GM, K_W2 = 0, 128, 256, 384, 512, 512 + 2048, 512 + 2048 + 128
K_TOT = K_W2 + 32
K2_ONE, K2_GM, K2_W2, K2_TOT = 0, 128, 256, 288


def _consts():
    c = np.zeros((128, K_TOT), np.float32)
    j = np.arange(128)[:, None]
    l = np.arange(128)[None, :]
    c[:, K_ID:K_ID + 128] = np.eye(128)
    c[:, K_NU:K_NU + 128] = -1.0 * (j >= l)
    c[:, K_NL:K_NL + 128] = -1.0 * (j < l)
    c[:, K_ONE:K_ONE + 128] = 1.0
    q = np.arange(512)[None, :]
    for o in range(4):
        c[:, K_DM + 512 * o:K_DM + 512 * (o + 1)] = np.where((128 * o + j) >= q, NEG, 0.0)
    c[:, K_GM:K_GM + 128] = ((j // 64) <= (l // 64))
    c[:, K_W2:K_W2 + NIT] = 2.0 ** (-(np.arange(NIT)[None, :] + 0.0))
    c2 = np.concatenate([c[:, K_ONE:K_ONE + 128], c[:, K_GM:K_GM + 128], c[:, K_W2:K_W2 + 32]], axis=1)
    return c, np.ascontiguousarray(c2)


class Trk:
    EP = 16000

    def __init__(self, nc, es):
        self.nc = nc
        self.q = {'pe': nc.tensor, 'act': nc.scalar, 'dve': nc.vector, 'pool': nc.gpsimd, 'sp': nc.sync}
        self.sems = []
        self.es = es
        self.cnt = {e: 0 for e in ('pe', 'act', 'dve', 'pool')}
        self.esem = {e: [] for e in self.cnt}
        self.seen = {e: {} for e in self.q}
        self.lastw = {}
        self.readers = {}
        self.ndma = 40
        self.dsem = [self._new(f"d{i}") for i in range(self.ndma)]
        self.dval = [0] * self.ndma
        self.drr = 0
        self.drr_sw = 0
        self.n_inst = 0
        self.dead = False

    def _new(self, name):
        s = self.es.enter_context(self.nc.semaphore(name))
        self.sems.append(s)
        return len(self.sems) - 1

    def _wait(self, e, ev):
        src, si, val = ev
        if src == 'pe' and e == 'pe':
            return
        if self.seen[e].get(si, 0) >= val:
            return
        self.q[e].wait_ge(self.sems[si], val)
        self.seen[e][si] = val

    def _deps(self, e, r, w):
        evs = []
        for k in r:
            if k in self.lastw:
                evs.append(self.lastw[k])
            if isinstance(k, tuple) and k[0] in ('ps', 'pst'):
                rd = self.readers.get(k)
                if rd:
                    evs.extend(ev for ev in rd.values() if ev[0] != e)
        for k in w:
            if k in self.lastw:
                evs.append(self.lastw[k])
            rd = self.readers.get(k)
            if rd:
                evs.extend(rd.values())
        for ev in evs:
            self._wait(e, ev)

    def _record(self, ev, r, w):
        for k in w:
            self.lastw[k] = ev
            self.readers[k] = {}
        for k in r:
            d = self.readers.setdefault(k, {})
            o = d.get(ev[1])
            if o is None or o[2] < ev[2]:
                d[ev[1]] = ev

    def op(self, e, fn, r=(), w=()):
        if self.dead:
            return None
        self._deps(e, r, w)
        inst = fn(self.q[e])
        n = self.cnt[e]
        ep, off = divmod(n, self.EP)
        if ep >= len(self.esem[e]):
            self.esem[e].append(self._new(f"{e}{ep}"))
        si = self.esem[e][ep]
        inst.then_inc(self.sems[si], 1)
        self.cnt[e] = n + 1
        self.n_inst += 1
        ev = (e, si, off + 1)
        if e != 'pe':
            self.seen[e][si] = max(self.seen[e].get(si, 0), 0)
        self._record(ev, r, w)
        return ev

    def dma(self, e, out, in_, r=(), w=()):
        if self.dead:
            return None
        half = self.ndma // 2
        if e == 'pool':
            i = self.drr_sw
            self.drr_sw = (self.drr_sw + 1) % half
        else:
            i = half + self.drr
            self.drr = (self.drr + 1) % half
        si = self.dsem[i]
        if self.dval[i] > 0:
            self._wait(e, ('dma', si, self.dval[i]))
        self._deps(e, r, w)
        inst = self.q[e].dma_start(out=out, in_=in_)
        self.dval[i] += 16
        inst.then_inc(self.sems[si], 16)
        self.n_inst += 1
        ev = ('dma', si, self.dval[i])
        self._record(ev, r, w)
        return ev

    def barrier(self):
        for e in self.q:
            for o in self.cnt:
                n = self.cnt[o]
                if n == 0 or o == e:
                    continue
                ep, off = divmod(n - 1, self.EP)
                self._wait(e, (o, self.esem[o][ep], off + 1))
            for i in range(self.ndma):
                if self.dval[i]:
                    self._wait(e, ('dma', self.dsem[i], self.dval[i]))
        for e in ('act', 'dve', 'pool'):
            n = self.cnt[e]
            if n:
                ep, off = divmod(n - 1, self.EP)
                self._wait(e, (e, self.esem[e][ep], off + 1))


class _Stop(Exception):
    pass


def build_program(stop=None):
    nc = bass.Bass("TRN2", target_bir_lowering=False)

    def din(name, shape):
        return nc.dram_tensor(name, list(shape), F32, kind="ExternalInput").ap()

    def dout(name, shape):
        return nc.dram_tensor(name, list(shape), F32, kind="ExternalOutput").ap()

    x_p = din("x_p", [SEQ, D]); x_s = din("x_s", [2, 16, D])
    cbk = din("cbk", [2, 2, SEQ, 128]); cbv = din("cbv", [2, 2, SEQ, 128]); cbi = din("cbi", [2, 2, SEQ, 32])
    cck = din("cck", [2, 2, SEQ, 512]); ccv = din("ccv", [2, 2, SEQ, 512])
    p_p = din("p_p", [2, SEQ, 256]); p_s = din("p_s", [2, 2, 16, 256])
    norm_mix = din("norm_mix", [2, D]); w_in = din("w_in", [2, D, 6696]); gbT = din("gbT", [2, 128, 24])
    a_vnorm = din("a_vnorm", [2, 512]); a_wsT = din("a_wsT", [2, 128, 4, 128]); a_bias = din("a_bias", [2, 512])
    b_qnorm = din("b_qnorm", [2, 64]); b_knorm = din("b_knorm", [2, 64])
    w_br = [din("w_br_a", [2, 512, D]), din("w_br_b", [2, 512, D]), din("w_br_c", [2, 512, D])]
    w_out = din("w_out", [2, D, D]); norm_ffn = din("norm_ffn", [2, D]); w_ffn_in = din("w_ffn_in", [2, D, 2 * DFF])
    w_ffn_out = din("w_ffn_out", [2, DFF, D]); norm_ple = din("norm_ple", [2, D]); w_ple_gate = din("w_ple_gate", [2, D, D])
    w_ple_proj = din("w_ple_proj", [2, 256, D]); cst = din("cst", [128, K_TOT]); cst2 = din("cst2", [128, K2_TOT])

    y_p = dout("y_p", [SEQ, D]); y_s = dout("y_s", [2, 16, D])
    o_bk_p = dout("o_bk_p", [2, SEQ, 128]); o_bv_p = dout("o_bv_p", [2, SEQ, 128]); o_bi_p = dout("o_bi_p", [2, SEQ, 32])
    o_ck_p = dout("o_ck_p", [2, SEQ, 512]); o_cv_p = dout("o_cv_p", [2, SEQ, 512])
    o_bk_s = dout("o_bk_s", [2, 2, 16, 128]); o_bv_s = dout("o_bv_s", [2, 2, 16, 128]); o_bi_s = dout("o_bi_s", [2, 2, 16, 32])
    o_ck_s = dout("o_ck_s", [2, 2, 16, 512]); o_cv_s = dout("o_cv_s", [2, 2, 16, 512]); o_av_s = dout("o_av_s", [2, 2, 16, 512])

    dbgk = "ExternalOutput" if stop is not None else "Internal"
    xs = nc.dram_tensor("xs", [NTOK, D], F32, kind=dbgk).ap()
    oT_d = nc.dram_tensor("oT_d", [3, 4, 128, NTOK], BF16, kind=dbgk).ap()

    es = ExitStack()
    with es:
        T = Trk(nc, es)
        uid = [0]

        def U(name):
            uid[0] += 1
            return f"{name}_{uid[0]}"

        def sb(name, shape, dt=F32):
            return es.enter_context(nc.sbuf_tensor(name, list(shape), dt))

        ps = [es.enter_context(nc.psum_tensor(f"ps{i}", [128, 512], F32)) for i in range(7)]
        pst = es.enter_context(nc.psum_tensor("pst", [128, 1024], BF16))
        PK = [('ps', i) for i in range(7)]
        PT = ('pst',)

        cf = sb("cf", [128, K2_TOT]); cb = sb("cb", [128, K_GM + 128], BF16)
        T.dma('sp', cf[:], cst2[:, :], w=['cf'])
        T.dma('pool', cb[:], cst[:, 0:K_GM + 128], w=['cb'])
        ident = cb[:, K_ID:K_ID + 128]; negU = cb[:, K_NU:K_NU + 128]; negL = cb[:, K_NL:K_NL + 128]
        onesb = cb[:, K_ONE:K_ONE + 128]
        ones1 = cf[0:1, K2_ONE:K2_ONE + 128]
        CB = ['cb']

        gmix = sb("gmix", [128, D]); gq8 = sb("gq8", [128, 64]); gk = sb("gk", [128, 64])
        gb = sb("gb", [128, 24])
        rowt = sb("rowt", [1, 512]); shiftc = sb("shiftc", [128, 4])
        xt = [None, None]
        xb = [sb(f"xbh{i}", [128, D], BF16) for i in range(2)]
        col = sb("col", [128, 64])
        stg = [sb(f"stg{i}", [128, 512]) for i in range(2)]
        wk = [sb(f"wk{i}", [128, 512]) for i in range(4)]
        wkb = [sb(f"wkb{i}", [128, 512], BF16) for i in range(4)]
        mmrr = [0]

        def mmbank():
            mmrr[0] ^= 1
            return mmrr[0]

        def bcast(dst, key, row_ap, n, scale=None):
            for c0 in range(0, n, 512):
                c1 = min(n, c0 + 512)
                T.dma('sp', rowt[0:1, 0:c1 - c0], row_ap[:, c0:c1], w=['rowt'])
                b = mmbank()
                T.op('pe', lambda q: q.matmul(ps[b][:, 0:c1 - c0], lhsT=ones1, rhs=rowt[0:1, 0:c1 - c0], start=True, stop=True),
                     r=['rowt', 'cf'], w=[PK[b]])
                if scale is None:
                    T.op('dve', lambda q: q.tensor_copy(out=dst[:, c0:c1], in_=ps[b][:, 0:c1 - c0]), r=[PK[b]], w=[key])
                else:
                    T.op('dve', lambda q: q.tensor_scalar(out=dst[:, c0:c1], in0=ps[b][:, 0:c1 - c0], scalar1=scale, scalar2=None, op0=ALU.mult),
                         r=[PK[b]], w=[key])

        wslot = [0]

        def load_w(src3, ncols, key=None):
            i = wslot[0]; wslot[0] ^= 1
            kc = src3.shape[1]
            T.dma('pool', wbuf[i][:, 0:kc, 0:ncols], src3, w=[('wbuf', i)])
            return wbuf[i], ('wbuf', i)

        def wview(wap, l, c0, c1):
            return wap[l].rearrange("(kc p) n -> p kc n", p=128)[:, :, c0:c1]

        def rmsnorm_rows(src_ap, src_keys, n, npart, gtile, gkey, dst_bf, dst_key, cidx):
            T.op('act', lambda q: q.activation(out=dst_bf, in_=src_ap, func=AF.Square,
                                               accum_out=col[0:npart, cidx:cidx + 1]), r=src_keys, w=[dst_key, ('col', cidx)])
            T.op('act', lambda q: q.activation(out=col[0:npart, cidx:cidx + 1], in_=col[0:npart, cidx:cidx + 1], func=AF.Sqrt, bias=EPS, scale=1.0 / n), r=[('col', cidx)], w=[('col', cidx)])
            T.op('dve', lambda q: q.reciprocal(out=col[0:npart, cidx:cidx + 1], in_=col[0:npart, cidx:cidx + 1]), r=[('col', cidx)], w=[('col', cidx)])
            T.op('dve', lambda q: q.scalar_tensor_tensor(out=dst_bf, in0=src_ap, scalar=col[0:npart, cidx:cidx + 1], in1=gtile[0:npart, 0:n],
                                                         op0=ALU.mult, op1=ALU.mult), r=list(src_keys) + [('col', cidx), gkey], w=[dst_key])

        def transpose_to(dst3, dst_key, src_bf, src_key, nchunks, npart=128):
            for c in range(nchunks):
                T.op('pe', lambda q: q.transpose(pst[:, c * 128:c * 128 + npart], src_bf[0:npart, c * 128:(c + 1) * 128], ident[0:npart, 0:npart]),
                     r=[src_key] + CB, w=[PT])
            T.op('act', lambda q: q.copy(out=dst3, in_=pst[:, 0:nchunks * 128].rearrange("p (c t) -> p c t", c=nchunks)[:, :, 0:npart]),
                 r=[PT], w=[dst_key])

        def gelu_to(dst, dst_key, src_ap, src_keys, npart, n, tmp_i):
            a = wk[tmp_i][0:npart, 0:n]; b = wk[tmp_i + 1][0:npart, 0:n]
            ka, kb = ('wk', tmp_i), ('wk', tmp_i + 1)
            T.op('act', lambda q: q.activation(out=a, in_=src_ap, func=AF.Square), r=src_keys, w=[ka])
            T.op('dve', lambda q: q.tensor_scalar(out=a, in0=a, scalar1=0.044715, scalar2=1.0, op0=ALU.mult, op1=ALU.add), r=[ka], w=[ka])
            T.op('dve', lambda q: q.tensor_tensor(out=a, in0=a, in1=src_ap, op=ALU.mult), r=[ka] + list(src_keys), w=[ka])
            T.op('act', lambda q: q.activation(out=b, in_=a, func=AF.Sigmoid, scale=1.5957691216057308), r=[ka], w=[kb])
            T.op('dve', lambda q: q.tensor_tensor(out=dst, in0=b, in1=src_ap, op=ALU.mult), r=[kb] + list(src_keys), w=[dst_key])

        def x_block_load(l, gblk, slot):
            key = ('xt', slot)
            if gblk < NBP:
                src = (x_p if l == 0 else xs)[gblk * 128:(gblk + 1) * 128, :]
                T.dma('sp', xt[slot][:], src, w=[key])
            else:
                s = gblk - NBP
                T.op('pool', lambda q: q.memset(xt[slot][:], 0.0), w=[key])
                src = x_s[s] if l == 0 else xs[gblk * 128:gblk * 128 + 16, :]
                T.dma('sp', xt[slot][0:16, :], src, w=[key])
            return key

        def norm_block_to_hT(l, gblk, slot, gt, gkey, hdst, hkey):
            key = ('xt', slot)
            rmsnorm_rows(xt[slot][:], [key], D, 128, gt, gkey, xb[slot][:], ('xb', slot), 0)
            transpose_to(hdst, hkey, xb[slot], ('xb', slot), 8)

        open_scopes = []

        def CK(name):
            if stop == name:
                T.dead = True

        try:
          for l in range(2):
              bcast(gmix, 'gmix', norm_mix[l:l + 1, :], D)
              bcast(gq8, 'gq8', b_qnorm[l:l + 1, :], 64, scale=0.125)
              bcast(gk, 'gk', b_knorm[l:l + 1, :], 64)
              T.dma('sp', gb[:], gbT[l], w=['gb'])
              T.op('dve', lambda q: q.tensor_reduce(out=shiftc[:, 0:1], in_=gq8[:], axis=AX.X, op=ALU.max, apply_absolute_value=True), r=['gq8'], w=['shiftc'])
              T.op('dve', lambda q: q.tensor_reduce(out=shiftc[:, 1:2], in_=gk[:], axis=AX.X, op=ALU.max, apply_absolute_value=True), r=['gk', 'shiftc'], w=['shiftc'])
              T.op('dve', lambda q: q.tensor_scalar(out=shiftc[:, 2:3], in0=shiftc[:, 0:1], scalar1=shiftc[:, 1:2], scalar2=-64.0, op0=ALU.mult, op1=ALU.mult),
                   r=['shiftc'], w=['shiftc2'])
              nshift = shiftc[:, 2:3]

              s12 = ExitStack()
              s12.__enter__(); open_scopes.append(s12)
              hT = s12.enter_context(nc.sbuf_tensor(U("hT"), [128, 8, NTOK], BF16))
              for i_x in range(2):
                  xt[i_x] = s12.enter_context(nc.sbuf_tensor(U("xt"), [128, D], F32))
              for gblk in range(NBT):
                  slot = gblk & 1
                  x_block_load(l, gblk, slot)
                  norm_block_to_hT(l, gblk, slot, gmix, 'gmix', hT[:, :, gblk * 128:(gblk + 1) * 128], ('hT', gblk))

              CK('p1')
              with ExitStack() as sa:
                  def sba(name, shape, dt=F32):
                      return sa.enter_context(nc.sbuf_tensor(U(name), list(shape), dt))
                  w_au = sba("w_au", [128, 8, 512], BF16); w_av = sba("w_av", [128, 8, 512], BF16)
                  avn = sba("avn", [128, 512]); abias = sba("abias", [128, 512])
                  wsT = sba("wsT", [128, 4, 128], BF16); wsTf = sba("wsTf", [128, 4, 128])
                  bcast(avn, 'avn', a_vnorm[l:l + 1, :], 512)
                  bcast(abias, 'abias', a_bias[l:l + 1, :], 512)
                  T.dma('sp', wsTf[:], a_wsT[l], w=['wsTf'])
                  for g in range(4):
                      T.op('dve', lambda q: q.tensor_tensor(out=wsT[:, g, :], in0=wsTf[:, g, :], in1=cf[:, K2_GM:K2_GM + 128], op=ALU.mult),
                           r=['wsTf', 'cf'], w=['wsT'])
                  oaT = [sba(f"oaT{i}", [128, 4, 128], BF16) for i in range(2)]
                  vtm = [sba(f"vtm{i}", [128, 512], BF16) for i in range(2)]
                  uT = [sba(f"uT{i}", [128, 4, 128]) for i in range(2)]
                  T.dma('pool', w_au[:], wview(w_in, l, C_AU, C_AU + 512), w=['w_au'])
                  T.dma('pool', w_av[:], wview(w_in, l, C_AV, C_AV + 512), w=['w_av'])
                  for gblk in range(NBT):
                      sl = gblk & 1
                      hk = ('hT', gblk)
                      hcols = slice(gblk * 128, (gblk + 1) * 128)
                      b = mmbank()
                      for kc in range(8):
                          T.op('pe', lambda q: q.matmul(ps[b][:, :], lhsT=hT[:, kc, hcols], rhs=w_av[:, kc, :], start=(kc == 0), stop=(kc == 7)),
                               r=[hk, 'w_av'], w=[PK[b]])
                      gelu_to(wk[2][:, :], ('wk', 2), ps[b][:, :], [PK[b]], 128, 512, 0)
                      rmsnorm_rows(wk[2][:, :], [('wk', 2)], 512, 128, avn, 'avn', vtm[sl][:], ('vtm', sl), 1)
                      if gblk >= NBP:
                          s = gblk - NBP
                          T.op('dve', lambda q: q.scalar_tensor_tensor(out=stg[0][0:16, 0:512], in0=wk[2][0:16, :], scalar=col[0:16, 1:2], in1=avn[0:16, :],
                                                                       op0=ALU.mult, op1=ALU.mult), r=[('wk', 2), ('col', 1), 'avn'], w=[('stg', 0)])
                          T.dma('sp', o_av_s[l, s], stg[0][0:16, 0:512], r=[('stg', 0)])
                      b2 = mmbank()
                      for g in range(4):
                          for kc in range(8):
                              T.op('pe', lambda q: q.matmul(ps[b2][:, g * 128:(g + 1) * 128], lhsT=w_au[:, kc, g * 128:(g + 1) * 128], rhs=hT[:, kc, hcols],
                                                            start=(kc == 0), stop=(kc == 7)), r=[hk, 'w_au'], w=[PK[b2]])
                      gelu_to(uT[sl][:].rearrange("p g t -> p (g t)"), ('uT', sl), ps[b2][:, :], [PK[b2]], 128, 512, 0)
                      b3 = 2
                      for g in range(4):
                          T.op('pe', lambda q: q.matmul(ps[b3][:, g * 128:(g + 1) * 128], lhsT=vtm[sl][:, g * 128:(g + 1) * 128], rhs=wsT[:, g, :],
                                                        start=True, stop=True), r=[('vtm', sl), 'wsT'], w=[PK[b3]])
                      T.op('dve', lambda q: q.tensor_tensor(out=wk[2][:, :], in0=ps[b3][:, :], in1=abias[:, :], op=ALU.add), r=[PK[b3], 'abias'], w=[('wk', 2)])
                      T.op('dve', lambda q: q.tensor_tensor(out=oaT[sl][:].rearrange("p g t -> p (g t)"), in0=wk[2][:, :],
                                                            in1=uT[sl][:].rearrange("p g t -> p (g t)"), op=ALU.mult), r=[('wk', 2), ('uT', sl)], w=[('oaT', sl)])
                      T.dma('sp', oT_d[0, :, :, gblk * 128:(gblk + 1) * 128].rearrange("c p t -> p c t"), oaT[sl][:], r=[('oaT', sl)], w=[('oTd', 0, gblk)])
                  T.barrier()

              CK('pa')
              jobs = [dict(kind='p', nb=NBP, qblocks=list(range(NBP)), gbase=0)]
              for s in range(2):
                  jobs.append(dict(kind='s', s=s, nb=NBP + 1, qblocks=[NBP], gbase=None))

              def gcol(job, blk):
                  if job['kind'] == 'p':
                      return blk
                  assert blk == NBP
                  return NBP + job['s']

              for job in jobs:
                  nb = job['nb']
                  L = nb * 128
                  issamp = job['kind'] == 's'
                  comp_blocks = list(range(NBP)) if not issamp else [NBP]
                  with ExitStack() as sB:
                      def sbb(name, shape, dt=F32):
                          return sB.enter_context(nc.sbuf_tensor(U(name), list(shape), dt))
                      bkT = sbb("bkT", [128, L], BF16); bv2 = sbb("bv2", [128, nb, 128], BF16); ikT = sbb("ikT", [32, L], BF16)
                      w_k = sbb("w_k", [128, 8, 288], BF16); w_q = sbb("w_q", [128, 8, 512], BF16); w_i = sbb("w_i", [128, 8, 264], BF16)
                      scores = sbb("scores", [128, L]); maskb = sbb("maskb", [128, L], BF16); mneg2 = [sbb(f"mnegT{i}", [128, nb, 128], BF16) for i in range(2)] if not issamp else [sbb("mnegT0", [128, nb, 128], BF16)] * 2
                      kb = [sbb(f"kb{i}", [128, 288], BF16) for i in range(2)]
                      bqn = sbb("bqn", [128, 512], BF16); bqT2 = [sbb(f"bqT{i}", [128, 4, 128], BF16) for i in range(2)]
                      iqb = sbb("iqb", [128, 256], BF16); iqT = sbb("iqT", [32, 8, 128], BF16); wq = sbb("wq", [128, 8])
                      bis = sbb("bis", [128, 32]); obT = sbb("obT", [128, 4, 128], BF16); oraw = [sbb("oraw0", [128, 512])] * 2; draw = [sbb("draw0", [128, 512])] * 2
                      pB = [sbb(f"pB{i}", [128, 512], BF16) for i in range(2)]
                      T.dma('pool', w_k[:, :, 0:256], wview(w_in, l, C_BK, C_BK + 256), w=['w_k'])
                      T.dma('pool', w_k[:, :, 256:288], wview(w_in, l, C_IK, C_IK + 32), w=['w_k'])
                      T.dma('pool', w_q[:], wview(w_in, l, C_BQ, C_BQ + 512), w=['w_q'])
                      T.dma('pool', w_i[:, :, 0:256], wview(w_in, l, C_IQ, C_IQ + 256), w=['w_i'])
                      T.dma('pool', w_i[:, :, 256:264], wview(w_in, l, C_IW, C_IW + 8), w=['w_i'])

                      if issamp:
                          s = job['s']
                          ks8 = [sbb(f"ks8{i}", [128, 8, 160], BF16) for i in range(2)]
                          for gi in range(NBP // 8):
                              b0 = gi * 8
                              st = ks8[gi & 1]
                              rws = slice(b0 * 128, (b0 + 8) * 128)
                              T.dma('pool', st[:, :, 0:128], cbk[l, s, rws, :].rearrange("(b p) c -> p b c", p=128), w=[('ks8k', gi & 1)])
                              T.dma('pool', st[:, :, 128:160], cbi[l, s, rws, :].rearrange("(b p) c -> p b c", p=128), w=[('ks8i', gi & 1)])
                              T.dma('pool', bv2[:, b0:b0 + 8, :], cbv[l, s, rws, :].rearrange("(b p) c -> p b c", p=128), w=[('bv2', b_) for b_ in range(b0, b0 + 8)])
                              for j in range(8):
                                  blk = b0 + j
                                  T.op('pe', lambda q: q.transpose(pst[:, 0:128], st[:, j, 0:128], ident), r=[('ks8k', gi & 1)] + CB, w=[PT])
                                  T.op('pe', lambda q: q.transpose(pst[0:32, 128:256], st[:, j, 128:160], ident), r=[('ks8i', gi & 1)] + CB, w=[PT])
                                  T.op('act', lambda q: q.copy(out=bkT[:, blk * 128:(blk + 1) * 128], in_=pst[:, 0:128]), r=[PT], w=[('bkT', blk)])
                                  T.op('act', lambda q: q.copy(out=ikT[0:32, blk * 128:(blk + 1) * 128], in_=pst[0:32, 128:256]), r=[PT], w=[('ikT', blk)])
                      for blk in range(nb):
                          if issamp and blk < NBP:
                              continue
                          sl = blk & 1
                          kkey = ('kb', sl)
                          if blk in comp_blocks:
                              gc = gcol(job, blk)
                              hk = ('hT', gc)
                              hcols = slice(gc * 128, (gc + 1) * 128)
                              b = mmbank()
                              for kc in range(8):
                                  T.op('pe', lambda q: q.matmul(ps[b][:, 0:288], lhsT=hT[:, kc, hcols], rhs=w_k[:, kc, :], start=(kc == 0), stop=(kc == 7)),
                                       r=[hk, 'w_k'], w=[PK[b]])
                              T.op('act', lambda q: q.activation(out=wk[0][:, 0:128], in_=ps[b][:, 0:128], func=AF.Square), r=[PK[b]], w=[('wk', 0)])
                              T.op('dve', lambda q: q.tensor_reduce(out=col[:, 8:10], in_=wk[0][:, 0:128].rearrange("p (h d) -> p h d", h=2), axis=AX.X, op=ALU.add),
                                   r=[('wk', 0)], w=[('col', 8)])
                              T.op('act', lambda q: q.activation(out=col[:, 8:10], in_=col[:, 8:10], func=AF.Sqrt, bias=EPS, scale=1.0 / 64), r=[('col', 8)], w=[('col', 8)])
                              T.op('dve', lambda q: q.reciprocal(out=col[:, 8:10], in_=col[:, 8:10]), r=[('col', 8)], w=[('col', 8)])
                              so = stg[sl]
                              for h in range(2):
                                  T.op('dve', lambda q: q.scalar_tensor_tensor(out=so[:, h * 64:(h + 1) * 64], in0=ps[b][:, h * 64:(h + 1) * 64], scalar=col[:, 8 + h:9 + h],
                                                                               in1=gk[:, :], op0=ALU.mult, op1=ALU.mult), r=[PK[b], ('col', 8), 'gk'], w=[('stg', sl)])
                              T.op('act', lambda q: q.copy(out=so[:, 128:288], in_=ps[b][:, 128:288]), r=[PK[b]], w=[('stg', sl)])
                              T.op('dve', lambda q: q.tensor_copy(out=kb[sl][:, :], in_=so[:, 0:288]), r=[('stg', sl)], w=[kkey])
                              if not issamp:
                                  rows = slice(blk * 128, (blk + 1) * 128)
                                  T.dma('sp', o_bk_p[l, rows, :], so[:, 0:128], r=[('stg', sl)])
                                  T.dma('sp', o_bv_p[l, rows, :], so[:, 128:256], r=[('stg', sl)])
                                  T.dma('sp', o_bi_p[l, rows, :], so[:, 256:288], r=[('stg', sl)])
                              else:
                                  s = job['s']
                                  T.dma('sp', o_bk_s[l, s], so[0:16, 0:128], r=[('stg', sl)])
                                  T.dma('sp', o_bv_s[l, s], so[0:16, 128:256], r=[('stg', sl)])
                                  T.dma('sp', o_bi_s[l, s], so[0:16, 256:288], r=[('stg', sl)])
                          else:
                              s = job['s']
                              rows = slice(blk * 128, (blk + 1) * 128)
                              T.dma('pool', kb[sl][:, 0:128], cbk[l, s, rows, :], w=[kkey])
                              T.dma('pool', kb[sl][:, 128:256], cbv[l, s, rows, :], w=[kkey])
                              T.dma('pool', kb[sl][:, 256:288], cbi[l, s, rows, :], w=[kkey])
                          T.op('pe', lambda q: q.transpose(pst[:, 0:128], kb[sl][:, 0:128], ident), r=[kkey] + CB, w=[PT])
                          T.op('pe', lambda q: q.transpose(pst[0:32, 128:256], kb[sl][:, 256:288], ident), r=[kkey] + CB, w=[PT])
                          T.op('act', lambda q: q.copy(out=bkT[:, blk * 128:(blk + 1) * 128], in_=pst[:, 0:128]), r=[PT], w=[('bkT', blk)])
                          T.op('act', lambda q: q.copy(out=ikT[0:32, blk * 128:(blk + 1) * 128], in_=pst[0:32, 128:256]), r=[PT], w=[('ikT', blk)])
                          T.op('pool', lambda q: q.tensor_copy(out=bv2[:, blk, :], in_=kb[sl][:, 128:256]), r=[kkey], w=[('bv2', blk)])

                      CK('bk')
                      scrr = [0]

                      def stageX(qb, slot):
                              gc = gcol(job, qb)
                              hk = ('hT', gc)
                              hcols = slice(gc * 128, (gc + 1) * 128)
                              Lq = (qb + 1) * 128
                              nlb = qb + 1
                              bq_ = mmbank()
                              for kc in range(8):
                                  T.op('pe', lambda q: q.matmul(ps[bq_][:, :], lhsT=hT[:, kc, hcols], rhs=w_q[:, kc, :], start=(kc == 0), stop=(kc == 7)),
                                       r=[hk, 'w_q'], w=[PK[bq_]])
                              T.op('act', lambda q: q.activation(out=wk[0][:, :], in_=ps[bq_][:, :], func=AF.Square), r=[PK[bq_]], w=[('wk', 0)])
                              bi_ = mmbank()
                              for kc in range(8):
                                  T.op('pe', lambda q: q.matmul(ps[bi_][:, 0:264], lhsT=hT[:, kc, hcols], rhs=w_i[:, kc, :], start=(kc == 0), stop=(kc == 7)),
                                       r=[hk, 'w_i'], w=[PK[bi_]])
                              T.op('act', lambda q: q.copy(out=iqb[:, :], in_=ps[bi_][:, 0:256]), r=[PK[bi_]], w=['iqb'])
                              yield
                              T.op('dve', lambda q: q.tensor_reduce(out=col[:, 16:24], in_=wk[0][:, :].rearrange("p (h d) -> p h d", h=8), axis=AX.X, op=ALU.add),
                                   r=[('wk', 0)], w=[('col', 16)])
                              T.op('act', lambda q: q.activation(out=col[:, 16:24], in_=col[:, 16:24], func=AF.Sqrt, bias=EPS, scale=1.0 / 64), r=[('col', 16)], w=[('col', 16)])
                              T.op('dve', lambda q: q.reciprocal(out=col[:, 16:24], in_=col[:, 16:24]), r=[('col', 16)], w=[('col', 16)])
                              for h in range(8):
                                  T.op('dve', lambda q: q.scalar_tensor_tensor(out=bqn[:, (h % 4) * 128 + (h // 4) * 64:(h % 4) * 128 + (h // 4) * 64 + 64], in0=ps[bq_][:, h * 64:(h + 1) * 64], scalar=col[:, 16 + h:17 + h],
                                                                               in1=gq8[:, :], op0=ALU.mult, op1=ALU.mult), r=[PK[bq_], ('col', 16), 'gq8'], w=['bqn'])
                              transpose_to(bqT2[slot][:], ('bqT', slot), bqn, 'bqn', 4)
                              T.op('dve', lambda q: q.tensor_scalar(out=wq[:, :], in0=ps[bi_][:, 256:264], scalar1=(8.0 ** -0.5) * (32.0 ** -0.5), scalar2=None, op0=ALU.mult),
                                   r=[PK[bi_]], w=['wq'])
                              for h in range(8):
                                  T.op('pe', lambda q: q.transpose(pst[0:32, h * 128:(h + 1) * 128], iqb[:, h * 32:(h + 1) * 32], ident), r=['iqb'] + CB, w=[PT])
                              T.op('act', lambda q: q.copy(out=iqT[:], in_=pst[0:32, :].rearrange("p (h t) -> p h t", h=8)), r=[PT], w=['iqT'])
                              yield
                              for c0 in range(0, Lq, 512):
                                  c1 = min(Lq, c0 + 512)
                                  n = c1 - c0
                                  kdeps = [('ikT', bb) for bb in range(c0 // 128, c1 // 128)]
                                  seng = 'dve'
                                  for h in range(8):
                                      b = mmbank()
                                      T.op('pe', lambda q: q.matmul(ps[b][:, 0:n], lhsT=iqT[:, h, :], rhs=ikT[0:32, c0:c1], start=True, stop=True),
                                           r=['iqT'] + kdeps, w=[PK[b]])
                                      ws = scrr[0]; scrr[0] = (scrr[0] + 1) % 4
                                      T.op('act', lambda q: q.activation(out=wk[ws][:, 0:n], in_=ps[b][:, 0:n], func=AF.Relu), r=[PK[b]], w=[('wk', ws)])
                                      if h == 0:
                                          T.op(seng, lambda q: q.tensor_scalar(out=scores[:, c0:c1], in0=wk[ws][:, 0:n], scalar1=wq[:, 0:1], scalar2=None, op0=ALU.mult),
                                               r=[('wk', ws), 'wq'], w=[('sc', c0)])
                                      elif seng == 'dve':
                                          T.op('dve', lambda q: q.scalar_tensor_tensor(out=scores[:, c0:c1], in0=wk[ws][:, 0:n], scalar=wq[:, h:h + 1], in1=scores[:, c0:c1],
                                                                                       op0=ALU.mult, op1=ALU.add), r=[('wk', ws), 'wq', ('sc', c0)], w=[('sc', c0)])
                                      else:
                                          T.op('pool', lambda q: q.tensor_scalar(out=wk[ws][:, 0:n], in0=wk[ws][:, 0:n], scalar1=wq[:, h:h + 1], scalar2=None, op0=ALU.mult),
                                               r=[('wk', ws), 'wq'], w=[('wk', ws)])
                                          T.op('pool', lambda q: q.tensor_tensor(out=scores[:, c0:c1], in0=scores[:, c0:c1], in1=wk[ws][:, 0:n], op=ALU.add),
                                               r=[('wk', ws), ('sc', c0)], w=[('sc', c0)])
                              sck = [('sc', c0) for c0 in range(0, Lq, 512)]
                              T.op('dve', lambda q: q.tensor_reduce(out=bis[:, 0:1], in_=scores[:, 0:Lq], axis=AX.X, op=ALU.max, apply_absolute_value=True), r=sck, w=['bis'])
                              if not issamp:
                                  T.op('pool', lambda q: q.memset(scores[0:64, Lq - 64:Lq], -BIG), r=['bis'], w=sck)
                              else:
                                  T.op('pool', lambda q: q.memset(scores[:, SEQ + 16:Lq], -BIG), r=['bis'], w=sck)
                              if Lq > 256:
                                  T.op('dve', lambda q: q.tensor_scalar(out=bis[:, 0:1], in0=bis[:, 0:1], scalar1=1.0, scalar2=None, op0=ALU.add), r=['bis'], w=['bis'])
                                  T.op('dve', lambda q: q.tensor_scalar(out=bis[:, 2:3], in0=bis[:, 0:1], scalar1=0.0, scalar2=None, op0=ALU.mult), r=['bis'], w=['bis'])
                                  T.op('dve', lambda q: q.tensor_scalar(out=bis[:, 4:4 + NIT], in0=cf[:, K2_W2:K2_W2 + NIT], scalar1=bis[:, 0:1], scalar2=None, op0=ALU.mult),
                                       r=['bis', 'cf'], w=['bis'])
                                  for it in range(NIT):
                                      T.op('dve', lambda q: q.tensor_scalar(out=maskb[:, 0:Lq], in0=scores[:, 0:Lq], scalar1=bis[:, 2:3], scalar2=0.0, op0=ALU.is_ge, op1=ALU.add,
                                                                            accum_out=bis[:, 3:4]), r=['bis'] + sck, w=['bis', 'maskb'])
                                      T.op('dve', lambda q: q.tensor_scalar(out=bis[:, 3:4], in0=bis[:, 3:4], scalar1=255.5, scalar2=bis[:, 4 + it:5 + it], op0=ALU.is_ge, op1=ALU.mult),
                                           r=['bis'], w=['bis'])
                                      nx = min(it + 1, NIT - 1)
                                      oc_ = 1 if it == NIT - 1 else 2
                                      T.op('dve', lambda q: q.scalar_tensor_tensor(out=bis[:, oc_:oc_ + 1], in0=bis[:, 2:3], scalar=bis[:, 4 + nx:5 + nx], in1=bis[:, 3:4],
                                                                                   op0=ALU.subtract, op1=ALU.add), r=['bis'], w=['bis'])
                                  thr = bis[:, 1:2]
                                  T.op('dve', lambda q: q.tensor_scalar(out=maskb[:, 0:Lq], in0=scores[:, 0:Lq], scalar1=thr, scalar2=None, op0=ALU.is_ge), r=['bis'] + sck, w=['maskb'])
                              else:
                                  T.op('dve', lambda q: q.tensor_scalar(out=maskb[:, 0:Lq], in0=scores[:, 0:Lq], scalar1=-1.0e29, scalar2=None, op0=ALU.is_ge), r=sck, w=['maskb'])
                              CK('topk')
                              yield
                              for lb0 in range(0, nlb, 8):
                                  lb1 = min(nlb, lb0 + 8)
                                  for lb in range(lb0, lb1):
                                      T.op('pe', lambda q: q.transpose(pst[:, (lb - lb0) * 128:(lb - lb0 + 1) * 128], maskb[:, lb * 128:(lb + 1) * 128], ident),
                                           r=['maskb'] + CB, w=[PT])
                                  T.op('dve', lambda q: q.tensor_scalar(out=mneg2[slot][:, lb0:lb1, :], in0=pst[:, 0:(lb1 - lb0) * 128].rearrange("p (c t) -> p c t", c=lb1 - lb0),
                                                                        scalar1=1.0, scalar2=-NEG, op0=ALU.subtract, op1=ALU.mult), r=[PT], w=[('mnegT', slot)])

                      def stageY(qb, slot):
                              gc = gcol(job, qb)
                              nlb = qb + 1
                              for g in range(2):
                                  bo, bd = 4, 5
                                  prow = slice(g * 64, g * 64 + 64)

                                  def logits(lb):
                                      bz = 2 + (lb & 1)
                                      T.op('pe', lambda q: q.matmul(ps[bz][:, :], lhsT=bkT[prow, lb * 128:(lb + 1) * 128],
                                                                    rhs=bqT2[slot][prow, :, :], start=True, stop=False), r=[('bkT', lb), ('bqT', slot)], w=[PK[bz]])
                                      for hh in range(4):
                                          T.op('pe', lambda q: q.matmul(ps[bz][:, hh * 128:(hh + 1) * 128], lhsT=ident, rhs=mneg2[slot][:, lb, :], start=False, stop=(hh == 3)),
                                               r=[('mnegT', slot)] + CB, w=[PK[bz]])
                                      T.op('act', lambda q: q.activation(out=pB[lb & 1][:, :], in_=ps[bz][:, :], func=AF.Exp, bias=nshift, scale=1.0),
                                           r=[PK[bz], 'shiftc2'], w=[('pB', lb & 1)])
                                  logits(0)
                                  for lb in range(nlb):
                                      sl = lb & 1
                                      if lb + 1 < nlb:
                                          logits(lb + 1)
                                      T.op('pe', lambda q: q.matmul(ps[bo][:, :], lhsT=bv2[:, lb, :], rhs=pB[sl][:, :], start=(lb == 0), stop=(lb == nlb - 1)),
                                           r=[('bv2', lb), ('pB', sl)], w=[PK[bo]])
                                      T.op('pe', lambda q: q.matmul(ps[bd][:, :], lhsT=onesb, rhs=pB[sl][:, :], start=(lb == 0), stop=(lb == nlb - 1)),
                                           r=[('pB', sl)] + CB, w=[PK[bd]])
                                  T.op('act', lambda q: q.copy(out=oraw[g][prow, :], in_=ps[bo][prow, :]), r=[PK[bo]], w=['oraw'])
                                  T.op('act', lambda q: q.activation(out=draw[g][prow, :], in_=ps[bd][prow, :], func=AF.Ln, bias=1e-30, scale=1.0), r=[PK[bd]], w=['draw'])
                                  T.op('act', lambda q: q.activation(out=draw[g][prow, :], in_=draw[g][prow, :], func=AF.Exp, scale=-1.0), r=['draw'], w=['draw'])
                                  T.op('pool', lambda q: q.tensor_tensor(out=obT[prow, :, :].rearrange("p c t -> p (c t)"), in0=oraw[g][prow, :], in1=draw[g][prow, :], op=ALU.mult),
                                       r=['oraw', 'draw'], w=['obT'])
                              T.dma('sp', oT_d[1, :, :, gc * 128:(gc + 1) * 128].rearrange("c p t -> p c t"), obT[:], r=['obT'], w=[('oTd', 1, gc)])

                      qbs = job['qblocks']
                      nq_ = len(qbs)
                      gens = [stageX(qbs[i], i & 1) for i in range(nq_)]
                      next(gens[0]); next(gens[0]); next(gens[0])
                      if nq_ > 1:
                          next(gens[1])
                      next(gens[0], None)
                      if nq_ > 1:
                          next(gens[1])
                      for i in range(1, nq_):
                          next(gens[i])
                          if i + 1 < nq_:
                              next(gens[i + 1])
                          stageY(qbs[i - 1], (i - 1) & 1)
                          next(gens[i], None)
                          if i + 1 < nq_:
                              next(gens[i + 1])
                      stageY(qbs[-1], (nq_ - 1) & 1)
                      T.barrier()

                  CK('B')
                  with ExitStack() as sC:
                      def sbc(name, shape, dt=F32):
                          return sC.enter_context(nc.sbuf_tensor(U(name), list(shape), dt))
                      nq = 128 * len(job['qblocks'])
                      ckT = sbc("ckT", [128, L], BF16); cvt = sbc("cvt", [128, nb, 128], BF16); cqT = sbc("cqT", [128, nq], BF16)
                      w_c = sbc("w_c", [128, 8, 384], BF16); kc2 = [sbc(f"kc2{i}", [128, 128], BF16) for i in range(2)]
                      kcs = [sbc(f"kcs{i}", [128, 8, 128], BF16) for i in range(2)] if issamp else None
                      ocT = [sbc(f"ocT{i}", [128, 512], BF16) for i in range(2)]
                      Gt = [sbc(f"Gt{i}", [128, 512]) for i in range(2)]; wvt = [sbc(f"wvt{i}", [128, 512], BF16) for i in range(4)]
                      for hp in range(4):
                          T.dma('pool', w_c[:, :, 0:128], wview(w_in, l, C_CQ + hp * 128, C_CQ + (hp + 1) * 128), w=['w_c'])
                          T.dma('pool', w_c[:, :, 128:256], wview(w_in, l, C_CK + hp * 128, C_CK + (hp + 1) * 128), w=['w_c'])
                          T.dma('pool', w_c[:, :, 256:384], wview(w_in, l, C_CV + hp * 128, C_CV + (hp + 1) * 128), w=['w_c'])
                          if issamp:
                              s = job['s']
                              for gi in range(NBP // 8):
                                  b0 = gi * 8
                                  st = kcs[gi & 1]
                                  rws = slice(b0 * 128, (b0 + 8) * 128)
                                  T.dma('pool', st[:], cck[l, s, rws, hp * 128:(hp + 1) * 128].rearrange("(b p) c -> p b c", p=128), w=[('kcs', gi & 1)])
                                  T.dma('pool', cvt[:, b0:b0 + 8, :], ccv[l, s, rws, hp * 128:(hp + 1) * 128].rearrange("(b p) c -> p b c", p=128),
                                        w=[('cvt', b_) for b_ in range(b0, b0 + 8)])
                                  for j in range(8):
                                      blk = b0 + j
                                      T.op('pe', lambda q: q.transpose(pst[:, (j & 1) * 128:(j & 1) * 128 + 128], st[:, j, :], ident), r=[('kcs', gi & 1)] + CB, w=[PT])
                                      T.op('act', lambda q: q.copy(out=ckT[:, blk * 128:(blk + 1) * 128], in_=pst[:, (j & 1) * 128:(j & 1) * 128 + 128]), r=[PT], w=[('ckT', blk)])
                          for blk in range(nb):
                              if issamp and blk < NBP:
                                  continue
                              sl = blk & 1
                              kkey = ('kc2', sl)
                              if blk in comp_blocks:
                                  gc = gcol(job, blk)
                                  hk = ('hT', gc)
                                  hcols = slice(gc * 128, (gc + 1) * 128)
                                  b = mmbank()
                                  for kc in range(8):
                                      T.op('pe', lambda q: q.matmul(ps[b][:, 0:256], lhsT=hT[:, kc, hcols], rhs=w_c[:, kc, 128:384], start=(kc == 0), stop=(kc == 7)),
                                           r=[hk, 'w_c'], w=[PK[b]])
                                  so = stg[sl]
                                  T.op('act', lambda q: q.copy(out=so[:, 0:256], in_=ps[b][:, 0:256]), r=[PK[b]], w=[('stg', sl)])
                                  T.op('dve', lambda q: q.tensor_copy(out=kc2[sl][:, :], in_=so[:, 0:128]), r=[('stg', sl)], w=[kkey])
                                  T.op('pool', lambda q: q.tensor_copy(out=cvt[:, blk, :], in_=so[:, 128:256]), r=[('stg', sl)], w=[('cvt', blk)])
                                  if not issamp:
                                      rows = slice(blk * 128, (blk + 1) * 128)
                                      T.dma('sp', o_ck_p[l, rows, hp * 128:(hp + 1) * 128], so[:, 0:128], r=[('stg', sl)])
                                      T.dma('sp', o_cv_p[l, rows, hp * 128:(hp + 1) * 128], so[:, 128:256], r=[('stg', sl)])
                                  else:
                                      s = job['s']
                                      T.dma('sp', o_ck_s[l, s, :, hp * 128:(hp + 1) * 128], so[0:16, 0:128], r=[('stg', sl)])
                                      T.dma('sp', o_cv_s[l, s, :, hp * 128:(hp + 1) * 128], so[0:16, 128:256], r=[('stg', sl)])
                              else:
                                  s = job['s']
                                  rows = slice(blk * 128, (blk + 1) * 128)
                                  T.dma('pool', kc2[sl][:, :], cck[l, s, rows, hp * 128:(hp + 1) * 128], w=[kkey])
                                  T.dma('pool', cvt[:, blk, :], ccv[l, s, rows, hp * 128:(hp + 1) * 128], w=[('cvt', blk)])
                              T.op('pe', lambda q: q.transpose(pst[:, 0:128], kc2[sl][:, :], ident), r=[kkey] + CB, w=[PT])
                              T.op('act', lambda q: q.copy(out=ckT[:, blk * 128:(blk + 1) * 128], in_=pst[:, 0:128]), r=[PT], w=[('ckT', blk)])
                          for qi, qb in enumerate(job['qblocks']):
                              gc = gcol(job, qb)
                              b = mmbank()
                              for kc in range(8):
                                  T.op('pe', lambda q: q.matmul(ps[b][:, 0:128], lhsT=w_c[:, kc, 0:128], rhs=hT[:, kc, gc * 128:(gc + 1) * 128], start=(kc == 0), stop=(kc == 7)),
                                       r=[('hT', gc), 'w_c'], w=[PK[b]])
                              T.op('act', lambda q: q.activation(out=cqT[:, qi * 128:(qi + 1) * 128], in_=ps[b][:, 0:128], func=AF.Copy, scale=0.125), r=[PK[b]], w=[('cqT', qi)])
                          qtiles = [job['qblocks'][i:i + 4] for i in range(0, len(job['qblocks']), 4)]
                          for ti, qt in enumerate(qtiles):
                              n = 128 * len(qt)
                              qc0 = ti * 512
                              qkeys = [('cqT', ti * 4 + i) for i in range(len(qt))]
                              nlb = qt[-1] + 1
                              osl = ti & 1
                              order = list(range(nlb - 1, -1, -1))

                              def stageA(lb, buf):
                                  diag = lb >= qt[0]
                                  for e2 in range(2):
                                      prow = slice(e2 * 64, e2 * 64 + 64)
                                      T.op('pe', lambda q: q.matmul(ps[e2][:, 0:n], lhsT=ckT[prow, lb * 128:(lb + 1) * 128], rhs=cqT[prow, qc0:qc0 + n], start=True, stop=(not diag)),
                                           r=[('ckT', lb)] + qkeys, w=[PK[e2]])
                                      if diag:
                                          o = lb - qt[0]
                                          T.op('pe', lambda q: q.matmul(ps[e2][:, 0:n], lhsT=ident, rhs=cb[:, K_DM + 512 * o:K_DM + 512 * o + n], start=False, stop=True),
                                               r=CB, w=[PK[e2]])
                                  for e2 in range(2):
                                      et = wk[2 * e2 + buf]; sp = wkb[2 * e2 + buf]
                                      T.op('act', lambda q: q.activation(out=et[:, 0:n], in_=ps[e2][:, 0:n], func=AF.Exp), r=[PK[e2]], w=[('wk', 2 * e2 + buf)])
                                      T.op('act', lambda q: q.activation(out=sp[:, 0:n], in_=et[:, 0:n], func=AF.Ln, bias=1.0, scale=1.0), r=[('wk', 2 * e2 + buf)], w=[('wkb', 2 * e2 + buf)])

                              def stageB(lb, buf, first):
                                  for e2 in range(2):
                                      sp = wkb[2 * e2 + buf]
                                      T.op('pe', lambda q: q.matmul(ps[2 + e2][:, 0:n], lhsT=negU, rhs=sp[:, 0:n], start=first, stop=True, skip_group_check=True), r=[('wkb', 2 * e2 + buf)] + CB, w=[PK[2 + e2]])
                                  for e2 in range(2):
                                      et = wk[2 * e2 + buf]
                                      T.op('act', lambda q: q.activation(out=Gt[e2][:, 0:n], in_=ps[2 + e2][:, 0:n], func=AF.Exp), r=[PK[2 + e2]], w=[('Gt', e2)])
                                      T.op('dve' if e2 == 0 else 'pool', lambda q: q.tensor_tensor(out=wvt[2 * e2 + buf][:, 0:n], in0=et[:, 0:n], in1=Gt[e2][:, 0:n], op=ALU.mult),
                                           r=[('wk', 2 * e2 + buf), ('Gt', e2)], w=[('wvt', 2 * e2 + buf)])
                                  for e2 in range(2):
                                      sp = wkb[2 * e2 + buf]
                                      T.op('pe', lambda q: q.matmul(ps[2 + e2][:, 0:n], lhsT=negL, rhs=sp[:, 0:n], start=False, stop=True, skip_group_check=True), r=[('wkb', 2 * e2 + buf)] + CB, w=[PK[2 + e2]])

                              def stagePV(lb, buf, first, last):
                                  for e2 in range(2):
                                      T.op('pe', lambda q: q.matmul(ps[4 + e2][:, 0:n], lhsT=cvt[:, lb, :], rhs=wvt[2 * e2 + buf][:, 0:n], start=first, stop=last),
                                           r=[('cvt', lb), ('wvt', 2 * e2 + buf)], w=[PK[4 + e2]])

                              no = len(order)
                              stageA(order[0], 0)
                              for i_, lb in enumerate(order):
                                  if i_ + 1 < no:
                                      stageA(order[i_ + 1], (i_ + 1) & 1)
                                  stageB(lb, i_ & 1, i_ == 0)
                                  if i_ >= 1:
                                      stagePV(order[i_ - 1], (i_ - 1) & 1, i_ == 1, False)
                              stagePV(order[no - 1], (no - 1) & 1, no == 1, True)
                              for e2 in range(2):
                                  prow = slice(e2 * 64, e2 * 64 + 64)
                                  T.op('act', lambda q: q.copy(out=ocT[osl][prow, 0:n], in_=ps[4 + e2][prow, 0:n]), r=[PK[4 + e2]], w=[('ocT', osl)])
                              for i, qb in enumerate(qt):
                                  gc = gcol(job, qb)
                                  T.dma('sp', oT_d[2, hp, :, gc * 128:(gc + 1) * 128], ocT[osl][:, i * 128:(i + 1) * 128], r=[('ocT', osl)], w=[('oTd', 2, gc)])
                      T.barrier()

              s12.__exit__(None, None, None); open_scopes.pop()
              CK('jobs')
              groups = [list(range(0, 8)), list(range(8, 16)), list(range(16, 24)), list(range(24, NBT))]
              with ExitStack() as s3:
                  def sb3(name, shape, dt=F32):
                      return s3.enter_context(nc.sbuf_tensor(U(name), list(shape), dt))
                  NG = 10
                  xg = sb3("xg", [128, NG, D]); hg = sb3("hg", [128, 8, NG * 128], BF16)
                  mT = sb3("mT", [128, 8, NG * 128], BF16)
                  gffn = sb3("gffn", [128, D]); gple = sb3("gple", [128, D])
                  bcast(gffn, 'gffn', norm_ffn[l:l + 1, :], D)
                  bcast(gple, 'gple', norm_ple[l:l + 1, :], D)
                  wg = sb3("wg", [128, 8, 1024], BF16); wb_ = sb3("wb_", [128, 4, 1024], BF16)
                  for grp in groups:
                      ng = len(grp)
                      ntok = ng * 128
                      tiles = [(c0, min(ntok, c0 + 512)) for c0 in range(0, ntok, 512)]
                      for i, gblk in enumerate(grp):
                          if gblk < NBP:
                              T.dma('sp', xg[:, i, :], (x_p if l == 0 else xs)[gblk * 128:(gblk + 1) * 128, :], w=[('xg', i)])
                          else:
                              s = gblk - NBP
                              T.op('pool', lambda q: q.memset(xg[:, i, :], 0.0), w=[('xg', i)])
                              T.dma('sp', xg[0:16, i, :], x_s[s] if l == 0 else xs[gblk * 128:gblk * 128 + 16, :], w=[('xg', i)])
                          sl = i & 1
                          rmsnorm_rows(xg[:, i, :], [('xg', i)], D, 128, gmix, 'gmix', xb[sl][:], ('xb', sl), 0)
                          transpose_to(hg[:, :, i * 128:(i + 1) * 128], ('hg', i), xb[sl], ('xb', sl), 8)
                      hkeys = [('hg', i) for i in range(ng)]
                      sM = ExitStack(); sM.__enter__(); open_scopes.append(sM)
                      og = sM.enter_context(nc.sbuf_tensor(U("og"), [128, 4, NG * 128], BF16))
                      macc = sM.enter_context(nc.sbuf_tensor(U("macc"), [128, 8, NG * 128], F32))
                      for br in range(3):
                          for hf in range(2):
                              T.dma('pool', wg[:, :, hf * 512:(hf + 1) * 512], wview(w_in, l, C_GL + br * 1024 + hf * 512, C_GL + br * 1024 + (hf + 1) * 512), w=[('wgg', hf)])
                          if br == 1:
                              for g2 in range(2):
                                  T.dma('pool', wb_[g2 * 64:(g2 + 1) * 64, :, :], w_br[br][l, g2 * 256:(g2 + 1) * 256, :].rearrange("(c d) n -> d c n", d=64), w=['wb_'])
                          else:
                              T.dma('pool', wb_[:], w_br[br][l].rearrange("(kc p) n -> p kc n", p=128), w=['wb_'])
                          T.dma('sp', og[:, :, 0:ntok], oT_d[br, :, :, grp[0] * 128:grp[0] * 128 + ntok].rearrange("c p t -> p c t"),
                                r=[('oTd', br, g_) for g_ in grp], w=['og'])
                          for cc in range(8):
                              for (t0, t1) in tiles:
                                  n = t1 - t0
                                  b = mmbank()
                                  for kc in range(8):
                                      T.op('pe', lambda q: q.matmul(ps[b][:, 0:n], lhsT=wg[:, kc, cc * 128:(cc + 1) * 128], rhs=hg[:, kc, t0:t1], start=(kc == 0), stop=(kc == 7)),
                                           r=[('wgg', cc // 4)] + hkeys, w=[PK[b]])
                                  T.op('act', lambda q: q.activation(out=wk[b][:, 0:n], in_=ps[b][:, 0:n], func=AF.Sigmoid, bias=gb[:, br * 8 + cc:br * 8 + cc + 1], scale=1.0),
                                       r=[PK[b], 'gb'], w=[('wk', b)])
                                  b2 = 2 + b
                                  for kc in range(4):
                                      T.op('pe', lambda q: q.matmul(ps[b2][:, 0:n], lhsT=wb_[:, kc, cc * 128:(cc + 1) * 128], rhs=og[:, kc, t0:t1], start=(kc == 0), stop=(kc == 3)),
                                           r=['wb_', 'og'], w=[PK[b2]])
                                  mk = ('macc', cc, t0)
                                  if br == 0:
                                      T.op('dve', lambda q: q.tensor_tensor(out=macc[:, cc, t0:t1], in0=ps[b2][:, 0:n], in1=wk[b][:, 0:n], op=ALU.mult), r=[PK[b2], ('wk', b)], w=[mk])
                                  else:
                                      T.op('dve', lambda q: q.tensor_tensor(out=wk[2 + b][:, 0:n], in0=ps[b2][:, 0:n], in1=wk[b][:, 0:n], op=ALU.mult), r=[PK[b2], ('wk', b)], w=[('wk', 2 + b)])
                                      if br == 1:
                                          T.op('pool', lambda q: q.tensor_tensor(out=macc[:, cc, t0:t1], in0=macc[:, cc, t0:t1], in1=wk[2 + b][:, 0:n], op=ALU.add), r=[mk, ('wk', 2 + b)], w=[mk])
                                      else:
                                          T.op('pool', lambda q: q.tensor_tensor(out=mT[:, cc, t0:t1], in0=macc[:, cc, t0:t1], in1=wk[2 + b][:, 0:n], op=ALU.add), r=[mk, ('wk', 2 + b)], w=[('mT', cc, t0)])
                      mkeys = [('mT', cc, t0) for cc in range(8) for (t0, _) in tiles]
                      T.barrier()
                      sM.__exit__(None, None, None); open_scopes.pop()
                      sF = ExitStack(); sF.__enter__(); open_scopes.append(sF)
                      actT = sF.enter_context(nc.sbuf_tensor(U("actT"), [128, 22, NG * 128], BF16))

                      def tok_major_update(wsrc3, wkey_unused, lhs_buf, lhs_keys, nkc, post):
                          for i in range(ng):
                              for half in range(2):
                                  b = mmbank()
                                  for kc in range(nkc):
                                      T.op('pe', lambda q: q.matmul(ps[b][:, :], lhsT=lhs_buf[:, kc, i * 128:(i + 1) * 128], rhs=wsrc3[:, kc, half * 512:(half + 1) * 512],
                                                                    start=(kc == 0), stop=(kc == nkc - 1)), r=lhs_keys + [wkey_unused], w=[PK[b]])
                                  post(i, half, b)

                      T.dma('pool', wg[:], w_out[l].rearrange("(kc p) n -> p kc n", p=128), w=['wg'])

                      def post_add(i, half, b):
                          T.op('dve', lambda q: q.tensor_tensor(out=xg[:, i, half * 512:(half + 1) * 512], in0=xg[:, i, half * 512:(half + 1) * 512], in1=ps[b][:, :], op=ALU.add),
                               r=[PK[b], ('xg', i)], w=[('xg', i)])
                      tok_major_update(wg, 'wg', mT, mkeys, 8, post_add)
                      for i in range(ng):
                          sl = i & 1
                          rmsnorm_rows(xg[:, i, :], [('xg', i)], D, 128, gffn, 'gffn', xb[sl][:], ('xb', sl), 0)
                          transpose_to(hg[:, :, i * 128:(i + 1) * 128], ('hg', i), xb[sl], ('xb', sl), 8)
                      T.barrier()
                      for s0 in range(0, DFF, 256):
                          pp = (s0 // 256) & 1
                          co = pp * 256
                          wkey = ('wgp', pp)
                          T.dma('pool', wg[:, :, co:co + 256], wview(w_ffn_in, l, s0, s0 + 256), w=[wkey])
                          T.dma('pool', wg[:, :, 512 + co:512 + co + 256], wview(w_ffn_in, l, DFF + s0, DFF + s0 + 256), w=[wkey])
                          for jj in range(2):
                              j = s0 // 128 + jj
                              for (t0, t1) in tiles:
                                  n = t1 - t0
                                  b = mmbank(); b2 = 2 + b
                                  for kc in range(8):
                                      T.op('pe', lambda q: q.matmul(ps[b][:, 0:n], lhsT=wg[:, kc, co + jj * 128:co + (jj + 1) * 128], rhs=hg[:, kc, t0:t1], start=(kc == 0), stop=(kc == 7)),
                                           r=[wkey] + hkeys, w=[PK[b]])
                                  for kc in range(8):
                                      T.op('pe', lambda q: q.matmul(ps[b2][:, 0:n], lhsT=wg[:, kc, 512 + co + jj * 128:512 + co + (jj + 1) * 128], rhs=hg[:, kc, t0:t1], start=(kc == 0), stop=(kc == 7)),
                                           r=[wkey] + hkeys, w=[PK[b2]])
                                  T.op('act', lambda q: q.activation(out=wk[b][:, 0:n], in_=ps[b][:, 0:n], func=AF.Silu), r=[PK[b]], w=[('wk', b)])
                                  T.op('dve', lambda q: q.tensor_tensor(out=actT[:, j, t0:t1], in0=ps[b2][:, 0:n], in1=wk[b][:, 0:n], op=ALU.mult), r=[PK[b2], ('wk', b)], w=[('actT', j, t0)])
                      akeys = [('actT', j, t0) for j in range(22) for (t0, _) in tiles]
                      T.barrier()
                      so_i = 0
                      for half in range(2):
                          for k0 in range(0, 22, 8):
                              k1 = min(22, k0 + 8)
                              qq = so_i & 1; so_i += 1
                              okey = ('wgo', qq)
                              T.dma('pool', wg[:, 0:k1 - k0, qq * 512:(qq + 1) * 512], w_ffn_out[l, k0 * 128:k1 * 128, half * 512:(half + 1) * 512].rearrange("(kc p) n -> p kc n", p=128), w=[okey])
                              for i in range(ng):
                                  b = mmbank()
                                  for kc in range(k0, k1):
                                      T.op('pe', lambda q: q.matmul(ps[b][:, :], lhsT=actT[:, kc, i * 128:(i + 1) * 128], rhs=wg[:, kc - k0, qq * 512:(qq + 1) * 512], start=(kc == k0), stop=(kc == k1 - 1)),
                                           r=akeys + [okey], w=[PK[b]])
                                  post_add(i, half, b)
                      T.barrier()
                      sF.__exit__(None, None, None); open_scopes.pop()
                      sP = ExitStack(); sP.__enter__(); open_scopes.append(sP)
                      pT = sP.enter_context(nc.sbuf_tensor(U("pT"), [128, 2, NG * 128], BF16))
                      pt32 = sP.enter_context(nc.sbuf_tensor(U("pt32"), [128, 256], F32))
                      ptb = sP.enter_context(nc.sbuf_tensor(U("ptb"), [128, 256], BF16))
                      for i, gblk in enumerate(grp):
                          sl = i & 1
                          rmsnorm_rows(xg[:, i, :], [('xg', i)], D, 128, gple, 'gple', xb[sl][:], ('xb', sl), 0)
                          transpose_to(hg[:, :, i * 128:(i + 1) * 128], ('hg', i), xb[sl], ('xb', sl), 8)
                          if gblk < NBP:
                              T.dma('sp', pt32[:, :], p_p[l, gblk * 128:(gblk + 1) * 128, :], w=['pt32'])
                          else:
                              T.op('pool', lambda q: q.memset(pt32[:, :], 0.0), w=['pt32'])
                              T.dma('sp', pt32[0:16, :], p_s[l, gblk - NBP], w=['pt32'])
                          T.op('dve', lambda q: q.tensor_copy(out=ptb[:, :], in_=pt32[:, :]), r=['pt32'], w=['ptb'])
                          transpose_to(pT[:, :, i * 128:(i + 1) * 128], ('pT', i), ptb, 'ptb', 2)
                      T.dma('pool', wg[:], w_ple_gate[l].rearrange("(kc p) n -> p kc n", p=128), w=['wg'])
                      T.dma('pool', wb_[:, 0:2, :], w_ple_proj[l].rearrange("(kc p) n -> p kc n", p=128), w=['wb_'])
                      for i in range(ng):
                          for half in range(2):
                              b = mmbank(); b2 = 2 + b
                              for kc in range(8):
                                  T.op('pe', lambda q: q.matmul(ps[b][:, :], lhsT=hg[:, kc, i * 128:(i + 1) * 128], rhs=wg[:, kc, half * 512:(half + 1) * 512], start=(kc == 0), stop=(kc == 7)),
                                       r=[('hg', i), 'wg'], w=[PK[b]])
                              for kc in range(2):
                                  T.op('pe', lambda q: q.matmul(ps[b2][:, :], lhsT=pT[:, kc, i * 128:(i + 1) * 128], rhs=wb_[:, kc, half * 512:(half + 1) * 512], start=(kc == 0), stop=(kc == 1)),
                                       r=[('pT', i), 'wb_'], w=[PK[b2]])
                              T.op('act', lambda q: q.activation(out=wk[b][:, :], in_=ps[b][:, :], func=AF.Sigmoid), r=[PK[b]], w=[('wk', b)])
                              T.op('dve', lambda q: q.tensor_tensor(out=wk[b][:, :], in0=ps[b2][:, :], in1=wk[b][:, :], op=ALU.mult), r=[PK[b2], ('wk', b)], w=[('wk', b)])
                              T.op('pool', lambda q: q.tensor_tensor(out=xg[:, i, half * 512:(half + 1) * 512], in0=xg[:, i, half * 512:(half + 1) * 512], in1=wk[b][:, :], op=ALU.add),
                                   r=[('wk', b), ('xg', i)], w=[('xg', i)])
                      T.barrier()
                      sP.__exit__(None, None, None); open_scopes.pop()
                      for i, gblk in enumerate(grp):
                          if l == 0:
                              T.dma('sp', xs[gblk * 128:(gblk + 1) * 128, :], xg[:, i, :], r=[('xg', i)], w=[('xs', gblk)])
                          elif gblk < NBP:
                              T.dma('sp', y_p[gblk * 128:(gblk + 1) * 128, :], xg[:, i, :], r=[('xg', i)])
                          else:
                              T.dma('sp', y_s[gblk - NBP], xg[0:16, i, :], r=[('xg', i)])
                  T.barrier()
              CK('L0')

        except _Stop:
            for sc in reversed(open_scopes):
                sc.__exit__(None, None, None)
        T.dead = False
        T.barrier()
        print("instructions:", T.n_inst, "sems:", len(T.sems))
    return nc


_NC_CACHE = {}


def _make_maps(inp):
    f = lambda a: np.ascontiguousarray(np.asarray(a, dtype=np.float32))
    cst, cst2 = _consts()
    shared = {
        'norm_mix': f(inp['norm_mix']), 'w_in': f(inp['w_in']),
        'gbT': f(np.asarray(inp['gate_bias']).reshape(2, 24, 128).transpose(0, 2, 1)),
        'a_vnorm': f(inp['a_vnorm']), 'a_wsT': f(np.asarray(inp['a_ws']).transpose(0, 3, 1, 2)),
        'a_bias': f(np.asarray(inp['a_bias']).reshape(2, 512)),
        'b_qnorm': f(inp['b_qnorm']), 'b_knorm': f(inp['b_knorm']),
        'w_br_a': f(inp['w_br_a']), 'w_br_b': f(inp['w_br_b']), 'w_br_c': f(inp['w_br_c']),
        'w_out': f(inp['w_out']), 'norm_ffn': f(inp['norm_ffn']), 'w_ffn_in': f(inp['w_ffn_in']), 'w_ffn_out': f(inp['w_ffn_out']),
        'norm_ple': f(inp['norm_ple']), 'w_ple_gate': f(inp['w_ple_gate']), 'w_ple_proj': f(inp['w_ple_proj']), 'cst': cst, 'cst2': cst2,
    }
    xp = np.asarray(inp['x_prompt']); xsm = np.asarray(inp['x_sample'])
    in_maps = []
    for c in range(8):
        b = c % 4
        ss = slice(2 * c, 2 * c + 2)
        m = dict(shared)
        m['x_p'] = f(xp[b]); m['x_s'] = f(xsm[ss])
        m['cbk'] = f(np.asarray(inp['cache_b_k'])[:, ss].reshape(2, 2, SEQ, 128))
        m['cbv'] = f(np.asarray(inp['cache_b_v'])[:, ss].reshape(2, 2, SEQ, 128))
        m['cbi'] = f(np.asarray(inp['cache_b_kidx'])[:, ss])
        m['cck'] = f(np.asarray(inp['cache_c_k'])[:, ss].reshape(2, 2, SEQ, 512))
        m['ccv'] = f(np.asarray(inp['cache_c_v'])[:, ss].reshape(2, 2, SEQ, 512))
        m['p_p'] = f(np.asarray(inp['p_prompt'])[:, b]); m['p_s'] = f(np.asarray(inp['p_sample'])[:, ss])
        in_maps.append(m)
    return in_maps


def kernel(**inp):
    if 'nc' not in _NC_CACHE:
        _NC_CACHE['nc'] = build_program()
    nc = _NC_CACHE['nc']
    in_maps = _make_maps(inp)
    res = run_bass_kernel_spmd(nc, in_maps, core_ids=list(range(8))).results
    st = lambda name, cores: np.stack([np.asarray(res[c][name]) for c in cores], axis=1)
    P = range(4); A = range(8)
    y_prompt = np.stack([res[c]['y_p'] for c in P], 0).astype(np.float32)
    y_sample = np.concatenate([res[c]['y_s'] for c in A], 0).astype(np.float32)
    cat_s = lambda name: np.concatenate([np.asarray(res[c][name]) for c in A], axis=1)
    outs = (
        y_prompt, y_sample,
        st('o_bk_p', P).reshape(2, 4, SEQ, 2, 64), st('o_bv_p', P).reshape(2, 4, SEQ, 2, 64), st('o_bi_p', P).reshape(2, 4, SEQ, 32),
        st('o_ck_p', P).reshape(2, 4, SEQ, 8, 64), st('o_cv_p', P).reshape(2, 4, SEQ, 8, 64),
        cat_s('o_bk_s').reshape(2, 16, 16, 2, 64), cat_s('o_bv_s').reshape(2, 16, 16, 2, 64), cat_s('o_bi_s').reshape(2, 16, 16, 32),
        cat_s('o_ck_s').reshape(2, 16, 16, 8, 64), cat_s('o_cv_s').reshape(2, 16, 16, 8, 64), cat_s('o_av_s').reshape(2, 16, 16, 512),
    )
    return tuple(np.ascontiguousarray(o, dtype=np.float32) for o in outs)
```
